# Optimizing a Trainium2 kernel written in Bass

```python
import jax, jax.numpy as jnp
from jax import lax
import numpy as np

D_MODEL = 2048
BATCH = 4
SEQ = 4096
DEPTH = 1
DEC_BATCH = 4
DEC_SEQ = 2048
PAST_LEN = 128

MIX_WIDTH = D_MODEL
FOURIER_WIDTH = MIX_WIDTH // 2
CONV_WIDTH = MIX_WIDTH - FOURIER_WIDTH
FOURIER_GROUPS = 4
FOURIER_GROUP_DIM = FOURIER_WIDTH // FOURIER_GROUPS
CONV_GROUPS = 4
CONV_KERNEL = 31
CONV_PAD = CONV_KERNEL // 2
D_FF = 4 * D_MODEL
PLE_DIM = 256
ALPHA = (2.0 * DEPTH) ** 0.25
BETA = (8.0 * DEPTH) ** -0.25
LN_EPS = 1e-5

kernel_name = "fnet_conformer_hybrid_encoder"


def layer_norm(x, g, b):
    xf = x.astype(jnp.float32)
    mu = jnp.mean(xf, axis=-1, keepdims=True)
    var = jnp.mean(jnp.square(xf - mu), axis=-1, keepdims=True)
    y = (xf - mu) * lax.rsqrt(var + LN_EPS)
    return (y * g.astype(jnp.float32) + b.astype(jnp.float32)).astype(x.dtype)


def fourier_mix(u):
    B, S, _ = u.shape
    uh = u.reshape(B, S, FOURIER_GROUPS, FOURIER_GROUP_DIM).astype(jnp.float32)
    f = jnp.fft.fft2(uh, axes=(1, 3), norm="ortho")
    return jnp.real(f).reshape(B, S, FOURIER_WIDTH).astype(u.dtype)


def conformer_conv(val, gate, w_dw, b_dw, g, b):
    h = val * jax.nn.sigmoid(gate)
    h = lax.conv_general_dilated(
        h, w_dw[:, None, :].astype(h.dtype),
        window_strides=(1,), padding=[(CONV_PAD, CONV_PAD)],
        dimension_numbers=("NWC", "WIO", "NWC"),
        feature_group_count=CONV_WIDTH) + b_dw
    h = layer_norm(h, g, b)
    return jax.nn.silu(h)


def encoder_layer(x, p, w_in, w_dw, b_dw, conv_ln_g, conv_ln_b, w_out, ln1_g, ln1_b,
                  w_ff1, b_ff1, w_ff2, b_ff2, ln2_g, ln2_b, w_gate, b_gate, w_ple,
                  ln3_g, ln3_b):
    u = jnp.einsum("bsd,dc->bsc", x, w_in)
    u_f = u[..., :FOURIER_WIDTH]
    u_v = u[..., FOURIER_WIDTH:FOURIER_WIDTH + CONV_WIDTH]
    u_g = u[..., FOURIER_WIDTH + CONV_WIDTH:]
    heads = jnp.concatenate(
        [fourier_mix(u_f), conformer_conv(u_v, u_g, w_dw, b_dw, conv_ln_g, conv_ln_b)], axis=-1)
    mix = jnp.einsum("bsc,cd->bsd", heads, w_out)
    x = layer_norm(ALPHA * x + mix, ln1_g, ln1_b)
    hid = jnp.square(jax.nn.relu(jnp.einsum("bsd,df->bsf", x, w_ff1) + b_ff1))
    ff = jnp.einsum("bsf,fd->bsd", hid, w_ff2) + b_ff2
    x = layer_norm(ALPHA * x + ff, ln2_g, ln2_b)
    gate = jax.nn.sigmoid(jnp.einsum("bsd,de->bse", x, w_gate) + b_gate)
    e = gate * jnp.einsum("bsk,kd->bsd", p, w_ple)
    return layer_norm(x + e, ln3_g, ln3_b)


def run_trunk(x, p, emb_ln_g, emb_ln_b, w_in, w_dw, b_dw, conv_ln_g, conv_ln_b, w_out,
              ln1_g, ln1_b, w_ff1, b_ff1, w_ff2, b_ff2, ln2_g, ln2_b, w_gate, b_gate,
              w_ple, ln3_g, ln3_b):
    x = layer_norm(x, emb_ln_g, emb_ln_b)
    for i in range(DEPTH):
        x = encoder_layer(x, p[i], w_in[i], w_dw[i], b_dw[i], conv_ln_g[i], conv_ln_b[i],
                          w_out[i], ln1_g[i], ln1_b[i], w_ff1[i], b_ff1[i], w_ff2[i],
                          b_ff2[i], ln2_g[i], ln2_b[i], w_gate[i], b_gate[i], w_ple[i],
                          ln3_g[i], ln3_b[i])
    return x


def setup_inputs(seed: int = 0) -> dict:
    key = jax.random.key(seed)
    ks = jax.random.split(key, 32)
    f32 = jnp.float32
    n = lambda k, shape, s: (jax.random.normal(k, shape, f32) * s)
    gain = lambda k, shape: 1.0 + 0.05 * jax.random.normal(k, shape, f32)
    bias = lambda k, shape: 0.02 * jax.random.normal(k, shape, f32)
    L = DEPTH
    return {
        "x_prompt": jax.random.normal(ks[0], (BATCH, SEQ, D_MODEL), f32),
        "x_sample": jax.random.normal(ks[1], (DEC_BATCH, DEC_SEQ, D_MODEL), f32),
        "p_prompt": jax.random.normal(ks[2], (DEPTH, BATCH, SEQ, PLE_DIM), f32),
        "p_sample": jax.random.normal(ks[3], (DEPTH, DEC_BATCH, DEC_SEQ, PLE_DIM), f32),
        "emb_ln_g": gain(ks[4], (D_MODEL,)),
        "emb_ln_b": bias(ks[5], (D_MODEL,)),
        "w_in": n(ks[6], (L, D_MODEL, FOURIER_WIDTH + 2 * CONV_WIDTH), D_MODEL ** -0.5),
        "w_dw": n(ks[7], (L, CONV_KERNEL, CONV_WIDTH), CONV_KERNEL ** -0.5),
        "b_dw": bias(ks[8], (L, CONV_WIDTH)),
        "conv_ln_g": gain(ks[9], (L, CONV_WIDTH)),
        "conv_ln_b": bias(ks[10], (L, CONV_WIDTH)),
        "w_out": n(ks[11], (L, MIX_WIDTH, D_MODEL), BETA * MIX_WIDTH ** -0.5),
        "ln1_g": gain(ks[12], (L, D_MODEL)),
        "ln1_b": bias(ks[13], (L, D_MODEL)),
        "w_ff1": n(ks[14], (L, D_MODEL, D_FF), D_MODEL ** -0.5),
        "b_ff1": bias(ks[15], (L, D_FF)),
        "w_ff2": n(ks[16], (L, D_FF, D_MODEL), BETA * D_FF ** -0.5),
        "b_ff2": bias(ks[17], (L, D_MODEL)),
        "ln2_g": gain(ks[18], (L, D_MODEL)),
        "ln2_b": bias(ks[19], (L, D_MODEL)),
        "w_gate": n(ks[20], (L, D_MODEL, D_MODEL), D_MODEL ** -0.5),
        "b_gate": bias(ks[21], (L, D_MODEL)),
        "w_ple": n(ks[22], (L, PLE_DIM, D_MODEL), BETA * PLE_DIM ** -0.5),
        "ln3_g": gain(ks[23], (L, D_MODEL)),
        "ln3_b": bias(ks[24], (L, D_MODEL)),
    }


def reference(x_prompt, x_sample, p_prompt, p_sample, emb_ln_g, emb_ln_b, w_in, w_dw, b_dw,
              conv_ln_g, conv_ln_b, w_out, ln1_g, ln1_b, w_ff1, b_ff1, w_ff2, b_ff2,
              ln2_g, ln2_b, w_gate, b_gate, w_ple, ln3_g, ln3_b):
    y_prompt = run_trunk(x_prompt, p_prompt, emb_ln_g, emb_ln_b, w_in, w_dw, b_dw, conv_ln_g,
                         conv_ln_b, w_out, ln1_g, ln1_b, w_ff1, b_ff1, w_ff2, b_ff2, ln2_g,
                         ln2_b, w_gate, b_gate, w_ple, ln3_g, ln3_b)
    y_sample = run_trunk(x_sample, p_sample, emb_ln_g, emb_ln_b, w_in, w_dw, b_dw, conv_ln_g,
                         conv_ln_b, w_out, ln1_g, ln1_b, w_ff1, b_ff1, w_ff2, b_ff2, ln2_g,
                         ln2_b, w_gate, b_gate, w_ple, ln3_g, ln3_b)
    return (y_prompt, y_sample)
```

```python
import numpy as np
from collections import defaultdict
from contextlib import ExitStack
import concourse.bass as bass
import concourse.mybir as mybir
from concourse.bass_utils import run_bass_kernel_spmd

F32 = mybir.dt.float32
BF16 = mybir.dt.bfloat16
U8 = mybir.dt.uint8
AF = mybir.ActivationFunctionType
ALU = mybir.AluOpType

ALPHA = float(2.0 ** 0.25)
LN_EPS = 1e-5
NSRC = 48
NOWN = 24
NG = 6
ENG = ('pe', 'act', 'dve', 'pool', 'sp')
import os
NTT_DBG = int(os.environ.get('NTT_DBG', '5'))
NG_DBG = int(os.environ.get('NG_DBG', '6'))

PCOLS = {}
_off = 0
for _n, _w in (('emb_g', 16), ('emb_b', 16), ('convg', 8), ('convb', 8), ('bdw', 8), ('wdw', 248),
               ('ln1g', 16), ('ln1b', 16), ('bff1', 64), ('bff2', 16), ('ln2g', 16), ('ln2b', 16),
               ('bgate', 16), ('ln3g', 16), ('ln3b', 16),
               ('aeg', 16), ('aeb', 16), ('a1g', 16), ('a1b', 16)):
    PCOLS[_n] = _off
    _off += _w
NP_IN = PCOLS['aeg']
NP_ALL = _off


class Buf:
    __slots__ = ('rd', 'wr', 'name', 'excl')

    def __init__(self, name='', excl=False):
        self.rd = {}
        self.wr = None
        self.name = name
        self.excl = excl


class Prog:
    def __init__(self):
        self.q = {e: [] for e in ENG}
        self.tick = defaultdict(int)
        self.waited = defaultdict(int)
        self.semnames = set('S_' + e for e in ENG)

    def _wait(self, eng, deps):
        for name, v in deps.items():
            if eng == 'pe' and name == 'S_pe':
                continue
            if self.waited[(eng, name)] >= v:
                continue
            self.waited[(eng, name)] = v
            self.q[eng].append(('w', name, v))

    @staticmethod
    def _deps(reads, writes, eng=None):
        d = {}

        def add(n, v):
            if d.get(n, 0) < v:
                d[n] = v
        for b in reads:
            if b.wr is not None:
                add(*b.wr)
            if b.excl:
                for n, v in b.rd.items():
                    if n != 'S_' + str(eng):
                        add(n, v)
        for b in writes:
            if b.wr is not None:
                add(*b.wr)
            for n, v in b.rd.items():
                add(n, v)
        return d

    def _commit(self, tok, reads, writes):
        name, v = tok
        for b in reads:
            if b.rd.get(name, 0) < v:
                b.rd[name] = v
        for b in writes:
            b.wr = tok
            b.rd = {}

    def op(self, eng, fn, reads=(), writes=(), sem=None, inc=1):
        self._wait(eng, self._deps(reads, writes, eng))
        name = sem or ('S_' + eng)
        self.semnames.add(name)
        self.tick[name] += inc
        tok = (name, self.tick[name])
        self.q[eng].append(('o', fn, name, inc))
        self._commit(tok, reads, writes)
        return tok

    def dma(self, eng, fn, sem, reads=(), writes=()):
        return self.op(eng, fn, reads, writes, sem=sem, inc=16)

    def pe_group(self, fns, reads=(), writes=()):
        self._wait('pe', self._deps(reads, writes))
        for fn in fns[:-1]:
            self.q['pe'].append(('o', fn, None, 0))
        self.tick['S_pe'] += 1
        tok = ('S_pe', self.tick['S_pe'])
        self.q['pe'].append(('o', fns[-1], 'S_pe', 1))
        self._commit(tok, reads, writes)
        return tok

    def barrier(self):
        deps = {n: v for n, v in self.tick.items() if v > 0}
        for e in ENG:
            self._wait(e, deps)


class Ring:
    def __init__(self, n, name):
        self.bufs = [Buf('%s%d' % (name, i)) for i in range(n)]
        self.i = 0
        self.n = n

    def next(self):
        i = self.i
        self.i = (self.i + 1) % self.n
        return i, self.bufs[i]


def I(name, *args, **kw):
    return (name, args, kw)


def build_program(debug=None):
    nc = bass.Bass("TRN2", target_bir_lowering=False)
    P = Prog()
    dt_in = lambda name, shape: nc.dram_tensor(name, shape, F32, kind="ExternalInput").ap()
    xsrc = dt_in("xsrc", [NSRC, 128, 2048])
    xhalo = dt_in("xhalo", [NG, 32, 2048])
    hmask_d = dt_in("hmask", [128, NG * 32])
    pown = dt_in("pown", [NG, 128, 4, 256])
    cmat_p = dt_in("cmat_p", [16, 128, 2 * 32 * 128])
    cmat_s = dt_in("cmat_s", [8, 128, 2 * 16 * 128])
    w_in = dt_in("w_in", [2048, 3072])
    w_out = dt_in("w_out", [2048, 2048])
    w_ff1 = dt_in("w_ff1", [2048, 8192])
    w_ff2 = dt_in("w_ff2", [8192, 2048])
    w_gate = dt_in("w_gate", [2048, 2048])
    w_ple = dt_in("w_ple", [256, 2048])
    pp_d = dt_in("pp", [128, NP_IN])
    cdt_d = dt_in("cdt", [128, 2 * 2 * 256])
    ident_d = dt_in("ident", [128, 128])
    yown = nc.dram_tensor("yown", [NOWN, 128, 2048], F32, kind="ExternalOutput").ap()
    dbg = None
    if debug == 'U':
        dbg = nc.dram_tensor("dbg", [128, NSRC * 1024], BF16, kind="ExternalOutput").ap()
    elif debug == 'YT':
        dbg = nc.dram_tensor("dbg", [128, 8 * 3072], BF16, kind="ExternalOutput").ap()

    es = ExitStack()
    with es:
        SLAB = 211968
        slab = es.enter_context(nc.sbuf_tensor("slab", [128, SLAB], U8))
        base = nc.lookup_mloc(slab).addr

        def sb(name, shape, dtype, off):
            nb = int(np.prod(shape[1:])) * (4 if dtype == F32 else 2)
            assert off % 32 == 0, (name, off)
            assert off + nb <= SLAB, (name, off, nb)
            return nc.alloc_sbuf_tensor_at(name, shape, dtype, offset=base + off), off + ((nb + 31) // 32) * 32

        o = 0
        pp, o = sb("pp", [128, NP_ALL], F32, o)
        ident_f, o = sb("ident_f", [128, 128], F32, o)
        ident_b, o = sb("ident_b", [128, 128], BF16, o)
        onesD, o = sb("onesD", [128, 128], BF16, o)
        onesC, o = sb("onesC", [128, 128], BF16, o)
        cdt_b, o = sb("cdt_b", [128, 2, 2, 256], BF16, o)
        hmask, o = sb("hmask", [128, NG, 32], F32, o)
        small, o = sb("small", [128, 64], F32, o)
        assert o <= 8192, o
        O_YT = 8192
        YT, _ = sb("YT", [128, 8, 3072], BF16, O_YT)
        O_U = O_YT + 49152
        O_X = O_U + 98304

        ps = es.enter_context(nc.psum_tensor("ps", [128, 8, 512], F32))
        sems = {}

        def pcol(name, j=0):
            c = PCOLS[name] + j
            return pp[:, c:c + 1]

        def pcols(name, n=16):
            c = PCOLS[name]
            return pp[:, c:c + n]

        B_const = Buf('const')
        B_YT = [Buf('YT%d' % t) for t in range(NOWN)]

        def dbg_dump(stage, ap2d, dtype, bufs):
            if debug != stage:
                return False
            d_ = nc.dram_tensor("dbg", list(ap2d.shape), dtype, kind="ExternalOutput").ap()
            P.dma('sp', I('dma_start', out=d_, in_=ap2d), 'D_out', reads=bufs)
            return True
        B_small = [Buf('small0'), Buf('small1')]

        P.dma('sp', I('dma_start', out=pp[:, 0:NP_IN], in_=pp_d), 'D_init', writes=[B_const])
        P.dma('sp', I('dma_start', out=ident_f[:], in_=ident_d), 'D_init', writes=[B_const])
        P.dma('sp', I('dma_start', out=hmask[:].rearrange("p a b -> p (a b)"), in_=hmask_d), 'D_init', writes=[B_const])
        P.dma('pool', I('dma_start', out=cdt_b[:].rearrange("p a b c -> p (a b c)"), in_=cdt_d), 'D_init2', writes=[B_const])
        P.op('dve', I('tensor_copy', ident_b[:], ident_f[:]), reads=[B_const], writes=[B_const])
        P.op('dve', I('memset', onesD[:], 1.0 / 2048), writes=[B_const])
        P.op('dve', I('memset', onesC[:], 1.0 / 1024), writes=[B_const])
        P.op('dve', I('tensor_scalar', pcols('aeg'), pcols('emb_g'), ALPHA, None, ALU.mult), reads=[B_const], writes=[B_const])
        P.op('dve', I('tensor_scalar', pcols('aeb'), pcols('emb_b'), ALPHA, None, ALU.mult), reads=[B_const], writes=[B_const])
        P.op('dve', I('tensor_scalar', pcols('a1g'), pcols('ln1g'), ALPHA, None, ALU.mult), reads=[B_const], writes=[B_const])
        P.op('dve', I('scalar_tensor_tensor', pcols('a1b'), pcols('ln1b'), ALPHA, pcols('bff2'), ALU.mult, ALU.add), reads=[B_const], writes=[B_const])

        psbank = [Buf('psb%d' % i, excl=True) for i in range(8)]

        def ln_tile_stats(xt, xbuf, rows, si):
            scol = 32 * si
            bs = B_small[si]
            st = small[0:rows, scol:scol + 24]
            mv = small[0:rows, scol + 24:scol + 26]
            rstd = small[0:rows, scol + 26:scol + 27]
            nmr = small[0:rows, scol + 27:scol + 28]
            for c4 in range(4):
                P.op('dve', I('bn_stats', st[:, c4 * 6:(c4 + 1) * 6], xt[:, c4 * 512:(c4 + 1) * 512]), reads=[xbuf], writes=[bs])
            P.op('dve', I('bn_aggr', mv, st.rearrange("p (a b) -> p a b", a=4)), reads=[bs], writes=[bs])
            P.op('act', I('activation', out=rstd, in_=mv[:, 1:2], func=AF.Sqrt, bias=LN_EPS, scale=1.0), reads=[bs], writes=[bs])
            P.op('dve', I('reciprocal', rstd, rstd), reads=[bs], writes=[bs])
            P.op('dve', I('scalar_tensor_tensor', nmr, mv[:, 0:1], -1.0, rstd, ALU.mult, ALU.mult), reads=[bs], writes=[bs])
            P.op('act', I('activation', out=xt, in_=xt, func=AF.Identity, bias=nmr, scale=rstd), reads=[bs, xbuf], writes=[xbuf])

        def transpose_tile(xt, xbuf, rows, banks, evac):
            for q4 in range(4):
                bi = banks[q4]
                fns = []
                for i in range(4):
                    dk = q4 * 4 + i
                    fns.append(I('transpose', ps[:, bi, i * 128:i * 128 + rows], xt[:, dk * 128:(dk + 1) * 128], ident_f[0:rows, 0:rows]))
                P.pe_group(fns, reads=[xbuf, B_const], writes=[psbank[bi]])
                for i in range(4):
                    dk = q4 * 4 + i
                    evac(dk, ps[:, bi, i * 128:i * 128 + rows], psbank[bi])

        U, _ = sb("U", [128, NSRC, 1024], BF16, O_U)
        B_U = [Buf('U%d' % j) for j in range(NSRC)]
        Wf, _ = sb("Wf", [128, 16, 1024], BF16, O_YT)
        stgA = [sb("stgA%d" % i, [128, 2048], F32, O_YT + 32768 + i * 8192)[0] for i in range(2)]
        B_stgA = [Buf('stgA%d' % i) for i in range(2)]
        xnT = [sb("xnT%d" % i, [128, 16, 128], BF16, O_X + i * 4096)[0] for i in range(2)]
        B_xnT = [Buf('xnT%d' % i) for i in range(2)]
        B_Wf = Buf('Wf')
        for q4 in range(4):
            P.dma('pool', I('dma_start', out=Wf[:, q4 * 4:(q4 + 1) * 4, :],
                            in_=w_in[q4 * 512:(q4 + 1) * 512, 0:1024].rearrange("(dk p) c -> p dk c", p=128)),
                  'D_wf', writes=[B_Wf])

        def A_stage1(j):
            s = j % 2
            P.dma('sp', I('dma_start', out=stgA[s][:], in_=xsrc[j]), 'D_stg%d' % s, writes=[B_stgA[s]])
            ln_tile_stats(stgA[s][:], B_stgA[s], 128, s)

            def evac(dk, psap, bb):
                P.op('dve', I('tensor_scalar', xnT[s][:, dk, :], psap, pcol('emb_g', dk), pcol('emb_b', dk), ALU.mult, ALU.add),
                     reads=[bb, B_const], writes=[B_xnT[s]])
            transpose_tile(stgA[s][:], B_stgA[s], 128, [0, 1, 2, 3], evac)

        def A_stage2(j):
            s = j % 2
            bk = [4 + 2 * s, 5 + 2 * s]
            fns = []
            for dk in range(16):
                for cb in range(2):
                    fns.append(I('matmul', ps[:, bk[cb], :], xnT[s][:, dk, :], Wf[:, dk, cb * 512:(cb + 1) * 512],
                                 start=(dk == 0), stop=(dk == 15)))
            P.pe_group(fns, reads=[B_xnT[s], B_Wf], writes=[psbank[bk[0]], psbank[bk[1]]])
            for cb in range(2):
                P.op('act', I('activation', out=U[:, j, cb * 512:(cb + 1) * 512], in_=ps[:, bk[cb], :], func=AF.Copy),
                     reads=[psbank[bk[cb]]], writes=[B_U[j]])

        for j in range(NSRC + 1):
            if j < NSRC:
                A_stage1(j)
            if j > 0:
                A_stage2(j - 1)

        if debug == 'U':
            P.dma('sp', I('dma_start', out=dbg, in_=U[:].rearrange("p a b -> p (a b)")), 'D_out', reads=B_U)
            return finish(nc, P, es, sems)

        P.barrier()
        cm = [sb("cm%d" % i, [128, 2, 32, 128], BF16, O_X + i * 16384)[0] for i in range(2)]
        B_cm = [Buf('cm%d' % i) for i in range(2)]
        o2 = O_X + 32768
        PQb = []
        for i in range(2):
            t_, o2 = sb("PQb%d" % i, [128, 2, 512], BF16, o2)
            PQb.append(t_)
        B_PQb = [Buf() for _ in range(2)]
        PQT = []
        for i in range(2):
            t_, o2 = sb("PQT%d" % i, [128, 8, 128], BF16, o2)
            PQT.append(t_)
        B_PQT = [Buf() for _ in range(2)]
        rnd = 0
        for t in range(NOWN):
            prompt = t < 16
            nsrc = 32 if prompt else 16
            j0 = 0 if prompt else 32
            s = t % 2
            if prompt:
                P.dma('pool', I('dma_start', out=cm[s][:].rearrange("p a b c -> p (a b c)"), in_=cmat_p[t]),
                      'D_cm%d' % s, writes=[B_cm[s]])
            else:
                P.dma('pool', I('dma_start', out=cm[s][:, :, 0:16, :], in_=cmat_s[t - 16].rearrange("p (a b c) -> p a b c", a=2, b=16)),
                      'D_cm%d' % s, writes=[B_cm[s]])
            for cb in range(2):
                r = rnd % 2
                rnd += 1
                bP, bQ = 2 * r, 2 * r + 1
                fns = []
                for jj in range(nsrc):
                    fns.append(I('matmul', ps[:, bP, :], cm[s][:, 0, jj, :], U[:, j0 + jj, cb * 512:(cb + 1) * 512],
                                 start=(jj == 0), stop=(jj == nsrc - 1)))
                    fns.append(I('matmul', ps[:, bQ, :], cm[s][:, 1, jj, :], U[:, j0 + jj, cb * 512:(cb + 1) * 512],
                                 start=(jj == 0), stop=(jj == nsrc - 1)))
                P.pe_group(fns, reads=[B_cm[s]], writes=[psbank[bP], psbank[bQ]])
                P.op('act', I('activation', out=PQb[r][:, 0, :], in_=ps[:, bP, :], func=AF.Copy), reads=[psbank[bP]], writes=[B_PQb[r]])
                P.op('dve', I('tensor_copy', PQb[r][:, 1, :], ps[:, bQ, :]), reads=[psbank[bQ]], writes=[B_PQb[r]])
                bT = 4 + r
                psT = ps[:, bT, :].bitcast(BF16)
                fns = []
                for pq in range(2):
                    for i in range(4):
                        fns.append(I('transpose', psT[:, (pq * 4 + i) * 128:(pq * 4 + i + 1) * 128],
                                     PQb[r][:, pq, i * 128:(i + 1) * 128], ident_b[:]))
                P.pe_group(fns, reads=[B_PQb[r], B_const], writes=[psbank[bT]])
                P.op('act', I('activation', out=PQT[r][:].rearrange("p a b -> p (a b)"), in_=psT, func=AF.Copy),
                     reads=[psbank[bT]], writes=[B_PQT[r]])
                bY = 6 + r
                fns = []
                for fgl in range(2):
                    for dp in range(2):
                        oc = (fgl * 2 + dp) * 128
                        k = 0
                        for kc in range(2):
                            for cs in range(2):
                                fns.append(I('matmul', ps[:, bY, oc:oc + 128], cdt_b[:, kc, cs, dp * 128:(dp + 1) * 128],
                                             PQT[r][:, cs * 4 + fgl * 2 + kc, :], start=(k == 0), stop=(k == 3)))
                                k += 1
                P.pe_group(fns, reads=[B_PQT[r], B_const], writes=[psbank[bY]])
                P.op('dve', I('tensor_copy', YT[:, cb * 4:(cb + 1) * 4, t * 128:(t + 1) * 128],
                              ps[:, bY, :].rearrange("p (a b) -> p a b", a=4)),
                     reads=[psbank[bY]], writes=[B_YT[t]])

        if debug == 'YT':
            P.dma('sp', I('dma_start', out=dbg, in_=YT[:].rearrange("p a b -> p (a b)")), 'D_out', reads=B_YT)
            return finish(nc, P, es, sems)

        P.barrier()
        o = O_U
        wple_b, o = sb("wple_b", [128, 2, 2048], BF16, o)
        xres, o = sb("xres", [128, 16, 512], F32, o)
        xbf, o = sb("xbf", [128, 16, 512], BF16, o)
        xbfh, o = sb("xbfh", [128, 16, 32], BF16, o)
        mstat, o = sb("mstat", [128, 512], F32, o)
        rstat, o = sb("rstat", [128, 512], F32, o)
        cT, o_after_cT = sb("cT", [128, 8, 512], F32, o)
        hid = [sb("hid%d" % i, [128, 8, 512], BF16, o + i * 8192)[0] for i in range(2)]
        o = o_after_cT
        hT, o_after_hT = sb("hT", [128, 8, 544], BF16, o)
        headsc, _ = sb("headsc", [128, 8, 512], BF16, o)
        o = o_after_hT
        NWR = 3
        wr = []
        for i in range(NWR):
            t_, o = sb("wr%d" % i, [128, 4096], BF16, o)
            wr.append(t_)
        stg = []
        for i in range(2):
            t_, o = sb("stg%d" % i, [128, 2048], F32, o)
            stg.append(t_)
        pstg, o = sb("pstg", [128, 4, 256], F32, o)
        pT, o = sb("pT", [128, 2, 512], BF16, o)
        NT = 3
        tmpf = []
        for i in range(NT):
            t_, o = sb("tmpf%d" % i, [128, 544], F32, o)
            tmpf.append(t_)
        tmpb = []
        for i in range(4):
            t_, o = sb("tmpb%d" % i, [128, 512], BF16, o)
            tmpb.append(t_)
        NDG = 8
        dg = []
        for i in range(NDG):
            t_, o = sb("dg%d" % i, [128, 128], BF16, o)
            dg.append(t_)
        print("phase B sbuf end", o, "of", SLAB)

        B_wple = Buf('wple')
        P.dma('pool', I('dma_start', out=wple_b[:], in_=w_ple.rearrange("(kc p) d -> p kc d", p=128)), 'D_wple', writes=[B_wple])
        B_xres = [Buf('xres%d' % i) for i in range(16)]
        B_xbf = [Buf('xbf%d' % i) for i in range(16)]
        B_xbfh = Buf('xbfh')
        B_m = Buf('m')
        B_cT = [Buf('cT%d' % i) for i in range(8)]
        B_hT = [Buf('hT%d' % i) for i in range(8)]
        B_hc = [Buf('hc%d' % i) for i in range(8)]
        B_hid = [[Buf('hid%d_%d' % (i, k)) for k in range(8)] for i in range(2)]
        R_wr = Ring(NWR, 'wr')
        R_stg = Ring(2, 'stg')
        B_pstg = Buf('pstg')
        B_pT = Buf('pT')
        R_tf = Ring(NT, 'tmpf')
        R_tb = Ring(4, 'tmpb')
        R_dg = Ring(NDG, 'dg')
        R_ps = Ring(6, 'psr')
        R_ps.bufs = psbank[0:6]
        BM, BS = 6, 7

        def load_w(w_ap, r0, nk, c0, ncol):
            i, b = R_wr.next()
            view = wr[i][:, 0:nk * ncol].rearrange("p (a b) -> p a b", a=nk)
            P.dma('pool', I('dma_start', out=view, in_=w_ap[r0:r0 + nk * 128, c0:c0 + ncol].rearrange("(k p) c -> p k c", p=128)),
                  'D_wr%d' % i, writes=[b])
            return view, b

        def ln_fm_stats(r_ap, rbuf, c, nch, ones):
            i1, b1 = R_tb.next()
            i2, b2 = R_tb.next()
            P.op('act', I('activation', out=tmpb[i1][:], in_=r_ap, func=AF.Copy), reads=[rbuf], writes=[b1])
            P.op('act', I('activation', out=tmpb[i2][:], in_=r_ap, func=AF.Square), reads=[rbuf], writes=[b2])
            P.pe_group([I('matmul', ps[:, BM, :], ones[:], tmpb[i1][:], start=(c == 0), stop=(c == nch - 1)),
                        I('matmul', ps[:, BS, :], ones[:], tmpb[i2][:], start=(c == 0), stop=(c == nch - 1))],
                       reads=[b1, b2, B_const], writes=[psbank[BM], psbank[BS]])

        def ln_fm_finish():
            i, b = R_tf.next()
            tv = tmpf[i][:, 0:512]
            P.op('act', I('activation', out=mstat[:], in_=ps[:, BM, :], func=AF.Copy), reads=[psbank[BM]], writes=[B_m])
            P.op('dve', I('tensor_tensor', tv, mstat[:], mstat[:], ALU.mult), reads=[B_m], writes=[b])
            P.op('dve', I('tensor_tensor', tv, ps[:, BS, :], tv, ALU.subtract), reads=[psbank[BS], b], writes=[b])
            P.op('act', I('activation', out=rstat[:], in_=tv, func=AF.Sqrt, bias=LN_EPS, scale=1.0), reads=[b], writes=[B_m])
            P.op('dve', I('reciprocal', rstat[:], rstat[:]), reads=[B_m], writes=[B_m])

        def ln_fm_center(r_ap, rbuf):
            i, b = R_tf.next()
            tv = tmpf[i][:, 0:512]
            P.op('dve', I('tensor_tensor', tv, r_ap, mstat[:], ALU.subtract), reads=[rbuf, B_m], writes=[b])
            P.op('dve', I('tensor_tensor', tv, tv, rstat[:], ALU.mult), reads=[B_m, b], writes=[b])
            return tv, b

        if dbg_dump('B0', YT[:].rearrange("p a b -> p (a b)"), BF16, B_YT + [B_wple]):
            return finish(nc, P, es, sems)
        for g in range(min(NG, NG_DBG)):
            t0 = 4 * g if g < 4 else 16 + 4 * (g - 4)
            j0 = t0 if g < 4 else 32 + (t0 - 16)
            for tt in range(NTT_DBG):
                si, sbuf_ = R_stg.next()
                rows = 128 if tt < 4 else 32
                if tt < 4:
                    P.dma('sp', I('dma_start', out=stg[si][:], in_=xsrc[j0 + tt]), 'D_stg%d' % si, writes=[sbuf_])
                else:
                    P.dma('sp', I('dma_start', out=stg[si][0:32, :], in_=xhalo[g]), 'D_stg%d' % si, writes=[sbuf_])
                xt = stg[si][0:rows, :]
                ln_tile_stats(xt, sbuf_, rows, tt % 2)
                banks = [R_ps.next()[0] for _ in range(4)]
                if tt < 4:
                    def evac(dk, psap, bb, tt=tt):
                        P.op('dve', I('tensor_scalar', xres[:, dk, tt * 128:(tt + 1) * 128], psap, pcol('aeg', dk), pcol('aeb', dk), ALU.mult, ALU.add),
                             reads=[bb, B_const], writes=[B_xres[dk]])
                        P.op('act', I('activation', out=xbf[:, dk, tt * 128:(tt + 1) * 128], in_=xres[:, dk, tt * 128:(tt + 1) * 128], func=AF.Copy, scale=1.0 / ALPHA),
                             reads=[B_xres[dk]], writes=[B_xbf[dk]])
                else:
                    def evac(dk, psap, bb):
                        P.op('act', I('activation', out=xbfh[:, dk, :], in_=psap, func=AF.Identity, bias=pcol('emb_b', dk), scale=pcol('emb_g', dk)),
                             reads=[bb, B_const], writes=[B_xbfh])
                transpose_tile(xt, sbuf_, rows, banks, evac)

            if g == 0 and dbg_dump('B1', xres[:].rearrange("p a b -> p (a b)"), F32, B_xres + B_xbf + [B_xbfh]):
                return finish(nc, P, es, sems)

            def conv_chunk(j):
                bi, bb = R_ps.next()
                for k in range(31):
                    di, db = R_dg.next()
                    P.op('dve', I('tensor_scalar', dg[di][:], ident_f[:], pcol('wdw', j * 31 + k), None, ALU.mult),
                         reads=[B_const], writes=[db])
                    P.pe_group([I('matmul', ps[:, bi, :], dg[di][:], hT[:, j, k:k + 512], start=(k == 0), stop=(k == 30))],
                               reads=[db, B_hT[j]], writes=[bb])
                P.op('act', I('activation', out=cT[:, j, :], in_=ps[:, bi, :], func=AF.Identity, bias=pcol('bdw', j), scale=1.0),
                     reads=[bb, B_const], writes=[B_cT[j]])
                ln_fm_stats(cT[:, j, :], B_cT[j], j, 8, onesC)

            for j in range(9):
                if j < 8:
                    wv, wb = load_w(w_in, 0, 16, 1024 + j * 128, 128)
                    wg, wgb = load_w(w_in, 0, 16, 2048 + j * 128, 128)
                    bv, bvb = R_ps.next()
                    bg, bgb = R_ps.next()
                    bh, bhb = R_ps.next()
                    fns = []
                    for dk in range(16):
                        fns.append(I('matmul', ps[:, bv, :], wv[:, dk, :], xbf[:, dk, :], start=(dk == 0), stop=(dk == 15)))
                    for dk in range(16):
                        fns.append(I('matmul', ps[:, bg, :], wg[:, dk, :], xbf[:, dk, :], start=(dk == 0), stop=(dk == 15)))
                    for dk in range(16):
                        fns.append(I('matmul', ps[:, bh, 0:32], wv[:, dk, :], xbfh[:, dk, :], start=(dk == 0), stop=(dk == 15)))
                    for dk in range(16):
                        fns.append(I('matmul', ps[:, bh, 32:64], wg[:, dk, :], xbfh[:, dk, :], start=(dk == 0), stop=(dk == 15)))
                    P.pe_group(fns, reads=[wb, wgb, B_xbfh] + B_xbf, writes=[bvb, bgb, bhb])
                    ti, tb = R_tf.next()
                    tf = tmpf[ti]
                    P.op('act', I('activation', out=tf[:, 0:512], in_=ps[:, bg, :], func=AF.Sigmoid), reads=[bgb], writes=[tb])
                    P.op('act', I('activation', out=tf[:, 512:544], in_=ps[:, bh, 32:64], func=AF.Sigmoid), reads=[bhb], writes=[tb])
                    P.op('dve', I('tensor_tensor', hT[:, j, 15:527], ps[:, bv, :], tf[:, 0:512], ALU.mult), reads=[bvb, tb], writes=[B_hT[j]])
                    P.op('dve', I('tensor_tensor', tf[:, 512:544], tf[:, 512:544], hmask[:, g, :], ALU.mult), reads=[tb, B_const], writes=[tb])
                    P.op('dve', I('tensor_tensor', hT[:, j, 0:15], ps[:, bh, 0:15], tf[:, 512:527], ALU.mult), reads=[bhb, tb], writes=[B_hT[j]])
                    P.op('dve', I('tensor_tensor', hT[:, j, 527:542], ps[:, bh, 15:30], tf[:, 527:542], ALU.mult), reads=[bhb, tb], writes=[B_hT[j]])
                if j > 0:
                    conv_chunk(j - 1)

            if g == 0 and dbg_dump('B3', cT[:].rearrange("p a b -> p (a b)"), F32, B_cT + B_hT):
                return finish(nc, P, es, sems)

            ln_fm_finish()
            for j in range(8):
                tap, tbuf = ln_fm_center(cT[:, j, :], B_cT[j])
                P.op('act', I('activation', out=headsc[:, j, :], in_=tap, func=AF.Silu, bias=pcol('convb', j), scale=pcol('convg', j)),
                     reads=[tbuf, B_const] + B_hT, writes=[B_hc[j]])

            if g == 0 and dbg_dump('B4', headsc[:].rearrange("p a b -> p (a b)"), BF16, B_hc):
                return finish(nc, P, es, sems)
            for u in range(8):
                wv, wb = load_w(w_out, 0, 16, u * 256, 256)
                for m in range(2):
                    d = 2 * u + m
                    bi, bb = R_ps.next()
                    fns = []
                    for ck in range(16):
                        if ck < 8:
                            rhs = YT[:, ck, t0 * 128:t0 * 128 + 512]
                        else:
                            rhs = headsc[:, ck - 8, :]
                        fns.append(I('matmul', ps[:, bi, :], wv[:, ck, m * 128:(m + 1) * 128], rhs, start=(ck == 0), stop=(ck == 15)))
                    P.pe_group(fns, reads=[wb] + B_hc + B_YT[t0:t0 + 4], writes=[bb])
                    P.op('dve', I('tensor_tensor', xres[:, d, :], xres[:, d, :], ps[:, bi, :], ALU.add), reads=[bb, B_xres[d]], writes=[B_xres[d]])
                    ln_fm_stats(xres[:, d, :], B_xres[d], d, 16, onesD)

            if g == 0 and dbg_dump('B5', xres[:].rearrange("p a b -> p (a b)"), F32, B_xres):
                return finish(nc, P, es, sems)
            ln_fm_finish()
            for d in range(16):
                tap, tbuf = ln_fm_center(xres[:, d, :], B_xres[d])
                P.op('act', I('activation', out=xres[:, d, :], in_=tap, func=AF.Identity, bias=pcol('a1b', d), scale=pcol('a1g', d)),
                     reads=[tbuf, B_const], writes=[B_xres[d]])
                P.op('dve', I('tensor_scalar', xbf[:, d, :], tap, pcol('ln1g', d), pcol('ln1b', d), ALU.mult, ALU.add),
                     reads=[tbuf, B_const], writes=[B_xbf[d]])

            if g == 0 and dbg_dump('LN1', xres[:].rearrange("p a b -> p (a b)"), F32, B_xres + B_xbf):
                return finish(nc, P, es, sems)
            def ff1(q):
                hb = q % 2
                for u in range(4):
                    wv, wb = load_w(w_ff1, 0, 16, (q * 8 + u * 2) * 128, 256)
                    for m in range(2):
                        fl = u * 2 + m
                        f = q * 8 + fl
                        bi, bb = R_ps.next()
                        fns = []
                        for dk in range(16):
                            fns.append(I('matmul', ps[:, bi, :], wv[:, dk, m * 128:(m + 1) * 128], xbf[:, dk, :], start=(dk == 0), stop=(dk == 15)))
                        P.pe_group(fns, reads=[wb] + B_xbf, writes=[bb])
                        ti, tb = R_tf.next()
                        tv = tmpf[ti][:, 0:512]
                        P.op('act', I('activation', out=tv, in_=ps[:, bi, :], func=AF.Relu, bias=pcol('bff1', f), scale=1.0),
                             reads=[bb, B_const], writes=[tb])
                        P.op('dve', I('tensor_tensor', hid[hb][:, fl, :], tv, tv, ALU.mult), reads=[tb], writes=[B_hid[hb][fl]])

            def ff2(q):
                hb = q % 2
                for u in range(4):
                    wv, wb = load_w(w_ff2, q * 1024, 8, u * 512, 512)
                    for m in range(4):
                        d = u * 4 + m
                        bi, bb = R_ps.next()
                        fns = []
                        for fk in range(8):
                            fns.append(I('matmul', ps[:, bi, :], wv[:, fk, m * 128:(m + 1) * 128], hid[hb][:, fk, :], start=(fk == 0), stop=(fk == 7)))
                        P.pe_group(fns, reads=[wb] + B_hid[hb], writes=[bb])
                        P.op('dve', I('tensor_tensor', xres[:, d, :], xres[:, d, :], ps[:, bi, :], ALU.add), reads=[bb, B_xres[d]], writes=[B_xres[d]])
                        if q == 7:
                            ln_fm_stats(xres[:, d, :], B_xres[d], d, 16, onesD)

            for q in range(9):
                if q < 8:
                    ff1(q)
                if q > 0:
                    ff2(q - 1)
            if g == 0 and dbg_dump('FF', xres[:].rearrange("p a b -> p (a b)"), F32, B_xres):
                return finish(nc, P, es, sems)
            ln_fm_finish()
            for d in range(16):
                tap, tbuf = ln_fm_center(xres[:, d, :], B_xres[d])
                P.op('act', I('activation', out=xres[:, d, :], in_=tap, func=AF.Identity, bias=pcol('ln2b', d), scale=pcol('ln2g', d)),
                     reads=[tbuf, B_const], writes=[B_xres[d]])
                P.op('dve', I('tensor_scalar', xbf[:, d, :], tap, pcol('ln2g', d), pcol('ln2b', d), ALU.mult, ALU.add),
                     reads=[tbuf, B_const], writes=[B_xbf[d]])
            if g == 0 and dbg_dump('LN2', xres[:].rearrange("p a b -> p (a b)"), F32, B_xres + B_xbf):
                return finish(nc, P, es, sems)
            P.dma('sp', I('dma_start', out=pstg[:], in_=pown[g]), 'D_pstg', writes=[B_pstg])
            for tt in range(4):
                bi, bb = R_ps.next()
                P.pe_group([I('transpose', ps[:, bi, kc * 128:(kc + 1) * 128], pstg[:, tt, kc * 128:(kc + 1) * 128], ident_f[:]) for kc in range(2)],
                           reads=[B_pstg, B_const], writes=[bb])
                P.op('act', I('activation', out=pT[:, :, tt * 128:(tt + 1) * 128], in_=ps[:, bi, 0:256].rearrange("p (a b) -> p a b", a=2), func=AF.Copy),
                     reads=[bb], writes=[B_pT])
            for u in range(8):
                wv, wb = load_w(w_gate, 0, 16, u * 256, 256)
                for m in range(2):
                    d = 2 * u + m
                    bi, bb = R_ps.next()
                    be, beb = R_ps.next()
                    fns = []
                    for dk in range(16):
                        fns.append(I('matmul', ps[:, bi, :], wv[:, dk, m * 128:(m + 1) * 128], xbf[:, dk, :], start=(dk == 0), stop=(dk == 15)))
                    for kc in range(2):
                        fns.append(I('matmul', ps[:, be, :], wple_b[:, kc, d * 128:(d + 1) * 128], pT[:, kc, :], start=(kc == 0), stop=(kc == 1)))
                    P.pe_group(fns, reads=[wb, B_wple, B_pT] + B_xbf, writes=[bb, beb])
                    ti, tb = R_tf.next()
                    tv = tmpf[ti][:, 0:512]
                    P.op('act', I('activation', out=tv, in_=ps[:, bi, :], func=AF.Sigmoid, bias=pcol('bgate', d), scale=1.0),
                         reads=[bb, B_const], writes=[tb])
                    P.op('dve', I('tensor_tensor', tv, tv, ps[:, be, :], ALU.mult), reads=[beb, tb], writes=[tb])
                    P.op('dve', I('tensor_tensor', xres[:, d, :], xres[:, d, :], tv, ALU.add), reads=[tb, B_xres[d]], writes=[B_xres[d]])
                    ln_fm_stats(xres[:, d, :], B_xres[d], d, 16, onesD)
            if g == 0 and dbg_dump('G', xres[:].rearrange("p a b -> p (a b)"), F32, B_xres):
                return finish(nc, P, es, sems)
            ln_fm_finish()
            for d in range(16):
                tap, tbuf = ln_fm_center(xres[:, d, :], B_xres[d])
                P.op('act', I('activation', out=xres[:, d, :], in_=tap, func=AF.Identity, bias=pcol('ln3b', d), scale=pcol('ln3g', d)),
                     reads=[tbuf, B_const], writes=[B_xres[d]])
            if g == 0 and dbg_dump('LN3', xres[:].rearrange("p a b -> p (a b)"), F32, B_xres):
                return finish(nc, P, es, sems)
            for tt in range(4):
                si, sbuf_ = R_stg.next()
                for q4 in range(4):
                    bi, bb = R_ps.next()
                    P.pe_group([I('transpose', ps[:, bi, i * 128:(i + 1) * 128], xres[:, q4 * 4 + i, tt * 128:(tt + 1) * 128], ident_f[:]) for i in range(4)],
                               reads=[B_const] + B_xres[q4 * 4:q4 * 4 + 4], writes=[bb])
                    if q4 % 2 == 0:
                        P.op('act', I('activation', out=stg[si][:, q4 * 512:(q4 + 1) * 512], in_=ps[:, bi, :], func=AF.Copy), reads=[bb], writes=[sbuf_])
                    else:
                        P.op('dve', I('tensor_copy', stg[si][:, q4 * 512:(q4 + 1) * 512], ps[:, bi, :]), reads=[bb], writes=[sbuf_])
                P.dma('sp', I('dma_start', out=yown[t0 + tt], in_=stg[si][:]), 'D_stg%d' % si, reads=[sbuf_])

        return finish(nc, P, es, sems)


def finish(nc, P, es, sems):
    deps = {n: v for n, v in P.tick.items() if v > 0}
    P._wait('sp', deps)
    for name in sorted(P.semnames):
        sems[name] = es.enter_context(nc.semaphore(name))
    block = es.enter_context(nc.Block())

    def runner(items):
        def run(e):
            for it in items:
                if it[0] == 'w':
                    e.wait_ge(sems[it[1]], it[2])
                else:
                    nm, a, kw = it[1]
                    ins = getattr(e, nm)(*a, **kw)
                    if it[2] is not None:
                        ins.then_inc(sems[it[2]], it[3])
        return run
    block.sync(runner(P.q['sp']))
    block.scalar(runner(P.q['act']))
    block.vector(runner(P.q['dve']))
    block.gpsimd(runner(P.q['pool']))
    block.tensor(runner(P.q['pe']))
    return nc


def _src_tiles(h):
    o = [('p', 16 * h + j) for j in range(16)] + [('p', 16 * (1 - h) + j) for j in range(16)]
    o += [('s', 8 * h + j) for j in range(8)] + [('s', 8 * (1 - h) + j) for j in range(8)]
    return o


_CONST_CACHE = {}


def _dft_tables(h):
    if h in _CONST_CACHE:
        return _CONST_CACHE[h]
    order = _src_tiles(h)
    out = []
    for (S, nown_t, src_list, own_base) in ((4096, 16, [t for (q, t) in order if q == 'p'], 16 * h),
                                            (2048, 8, [t for (q, t) in order if q == 's'], 8 * h)):
        k = np.arange(S, dtype=np.float64)
        ctab = (np.cos(2 * np.pi * k / S) / np.sqrt(S)).astype(np.float32)
        stab = (np.sin(2 * np.pi * k / S) / np.sqrt(S)).astype(np.float32)
        nsrc = len(src_list)
        pos_src = (np.array(src_list, dtype=np.int64)[None, :] * 128 + np.arange(128, dtype=np.int64)[:, None])
        cm = np.empty((nown_t, 128, 2, nsrc, 128), dtype=np.float32)
        for t in range(nown_t):
            pos_own = (own_base + t) * 128 + np.arange(128, dtype=np.int64)
            idx = (pos_src[:, :, None] * pos_own[None, None, :]) % S
            cm[t, :, 0] = ctab[idx]
            cm[t, :, 1] = stab[idx]
        out.append(cm.reshape(nown_t, 128, 2 * nsrc * 128))
    _CONST_CACHE[h] = out
    return out


def _chan_table():
    d = np.arange(256, dtype=np.int64)
    idx = (d[:, None] * d[None, :]) % 256
    k = np.arange(256, dtype=np.float64)
    c = (np.cos(2 * np.pi * k / 256) / 16.0).astype(np.float32)[idx]
    s = (-np.sin(2 * np.pi * k / 256) / 16.0).astype(np.float32)[idx]
    t = np.stack([c, s], axis=1)
    t = t.reshape(2, 128, 2, 256).transpose(1, 0, 2, 3)
    return np.ascontiguousarray(t.reshape(128, 2 * 2 * 256))


def _cols(v, n):
    return np.ascontiguousarray(np.asarray(v, dtype=np.float32).reshape(n, 128).T)


def _prep(inputs):
    xp = np.asarray(inputs['x_prompt'], dtype=np.float32)
    xs = np.asarray(inputs['x_sample'], dtype=np.float32)
    pp_ = np.asarray(inputs['p_prompt'], dtype=np.float32)[0]
    ps_ = np.asarray(inputs['p_sample'], dtype=np.float32)[0]
    g = lambda n: np.asarray(inputs[n], dtype=np.float32)
    wdw = g('w_dw')[0]
    wdw_cols = np.ascontiguousarray(wdw.reshape(31, 8, 128).transpose(2, 1, 0).reshape(128, 248))
    pp = np.concatenate([
        _cols(g('emb_ln_g'), 16), _cols(g('emb_ln_b'), 16), _cols(g('conv_ln_g')[0], 8), _cols(g('conv_ln_b')[0], 8),
        _cols(g('b_dw')[0], 8), wdw_cols, _cols(g('ln1_g')[0], 16), _cols(g('ln1_b')[0], 16), _cols(g('b_ff1')[0], 64),
        _cols(g('b_ff2')[0], 16), _cols(g('ln2_g')[0], 16), _cols(g('ln2_b')[0], 16), _cols(g('b_gate')[0], 16),
        _cols(g('ln3_g')[0], 16), _cols(g('ln3_b')[0], 16)], axis=1)
    assert pp.shape == (128, NP_IN)
    shared = dict(w_in=np.ascontiguousarray(g('w_in')[0]), w_out=np.ascontiguousarray(g('w_out')[0]),
                  w_ff1=np.ascontiguousarray(g('w_ff1')[0]), w_ff2=np.ascontiguousarray(g('w_ff2')[0]),
                  w_gate=np.ascontiguousarray(g('w_gate')[0]), w_ple=np.ascontiguousarray(g('w_ple')[0]),
                  pp=np.ascontiguousarray(pp), cdt=_chan_table(), ident=np.eye(128, dtype=np.float32))
    in_maps = []
    for c in range(8):
        b, h = c // 2, c % 2
        order = _src_tiles(h)
        seqs = {'p': xp[b], 's': xs[b]}
        xsrc = np.stack([seqs[q][t * 128:(t + 1) * 128] for (q, t) in order], axis=0)
        xhalo = np.zeros((NG, 32, 2048), dtype=np.float32)
        hmask = np.zeros((NG, 32), dtype=np.float32)
        pown = np.empty((NG, 128, 4, 256), dtype=np.float32)
        for gi in range(NG):
            if gi < 4:
                seq, S, g0, pseq = xp[b], 4096, 2048 * h + 512 * gi, pp_[b]
            else:
                seq, S, g0, pseq = xs[b], 2048, 1024 * h + 512 * (gi - 4), ps_[b]
            for r in range(30):
                pos = g0 - 15 + r if r < 15 else g0 + 512 + (r - 15)
                if 0 <= pos < S:
                    xhalo[gi, r] = seq[pos]
                    hmask[gi, r] = 1.0
            pown[gi] = pseq[g0:g0 + 512].reshape(4, 128, 256).transpose(1, 0, 2)
        cmp_, cms_ = _dft_tables(h)
        m = dict(shared)
        m.update(xsrc=np.ascontiguousarray(xsrc), xhalo=xhalo,
                 hmask=np.ascontiguousarray(np.broadcast_to(hmask.reshape(1, NG * 32), (128, NG * 32))),
                 pown=pown, cmat_p=cmp_, cmat_s=cms_)
        in_maps.append(m)
    return in_maps


_NC_CACHE = {}


def kernel(**inputs):
    in_maps = _prep(inputs)
    if 'nc' not in _NC_CACHE:
        _NC_CACHE['nc'] = build_program()
    nc = _NC_CACHE['nc']
    res = run_bass_kernel_spmd(nc, in_maps, core_ids=list(range(8)))
    y_prompt = np.empty((4, 4096, 2048), dtype=np.float32)
    y_sample = np.empty((4, 2048, 2048), dtype=np.float32)
    for c in range(8):
        b, h = c // 2, c % 2
        y = np.asarray(res.results[c]["yown"], dtype=np.float32).reshape(NOWN * 128, 2048)
        y_prompt[b, 2048 * h:2048 * h + 2048] = y[0:2048]
        y_sample[b, 1024 * h:1024 * h + 1024] = y[2048:3072]
    return (y_prompt, y_sample)
```

```python
import numpy as np
from collections import defaultdict
from contextlib import ExitStack
import concourse.bass as bass
import concourse.mybir as mybir
from concourse.bass_utils import run_bass_kernel_spmd

F32 = mybir.dt.float32
BF16 = mybir.dt.bfloat16
U8 = mybir.dt.uint8
AF = mybir.ActivationFunctionType
ALU = mybir.AluOpType

ALPHA = float(2.0 ** 0.25)
LN_EPS = 1e-5
NSRC = 48
NOWN = 24
NG = 6
ENG = ('pe', 'act', 'dve', 'pool', 'sp')
import os
NTT_DBG = int(os.environ.get('NTT_DBG', '5'))
NG_DBG = int(os.environ.get('NG_DBG', '6'))

PCOLS = {}
_off = 0
for _n, _w in (('emb_g', 16), ('emb_b', 16), ('convg', 8), ('convb', 8), ('bdw', 8), ('wdw', 248),
               ('ln1g', 16), ('ln1b', 16), ('bff1', 64), ('bff2', 16), ('ln2g', 16), ('ln2b', 16),
               ('bgate', 16), ('ln3g', 16), ('ln3b', 16),
               ('aeg', 16), ('aeb', 16), ('a1g', 16), ('a1b', 16)):
    PCOLS[_n] = _off
    _off += _w
NP_IN = PCOLS['aeg']
NP_ALL = _off


class Buf:
    __slots__ = ('rd', 'wr', 'name', 'excl')

    def __init__(self, name='', excl=False):
        self.rd = {}
        self.wr = None
        self.name = name
        self.excl = excl


class Prog:
    def __init__(self):
        self.q = {e: [] for e in ENG}
        self.tick = defaultdict(int)
        self.waited = defaultdict(int)
        self.semnames = set('S_' + e for e in ENG)

    def _wait(self, eng, deps):
        for name, v in deps.items():
            if eng == 'pe' and name == 'S_pe':
                continue
            if self.waited[(eng, name)] >= v:
                continue
            self.waited[(eng, name)] = v
            self.q[eng].append(('w', name, v))

    @staticmethod
    def _deps(reads, writes, eng=None):
        d = {}

        def add(n, v):
            if d.get(n, 0) < v:
                d[n] = v
        for b in reads:
            if b.wr is not None:
                add(*b.wr)
            if b.excl:
                for n, v in b.rd.items():
                    if n != 'S_' + str(eng):
                        add(n, v)
        for b in writes:
            if b.wr is not None:
                add(*b.wr)
            for n, v in b.rd.items():
                add(n, v)
        return d

    def _commit(self, tok, reads, writes):
        name, v = tok
        for b in reads:
            if b.rd.get(name, 0) < v:
                b.rd[name] = v
        for b in writes:
            b.wr = tok
            b.rd = {}

    def op(self, eng, fn, reads=(), writes=(), sem=None, inc=1):
        self._wait(eng, self._deps(reads, writes, eng))
        name = sem or ('S_' + eng)
        self.semnames.add(name)
        self.tick[name] += inc
        tok = (name, self.tick[name])
        self.q[eng].append(('o', fn, name, inc))
        self._commit(tok, reads, writes)
        return tok

    def dma(self, eng, fn, sem, reads=(), writes=()):
        return self.op(eng, fn, reads, writes, sem=sem, inc=16)

    def pe_group(self, fns, reads=(), writes=()):
        self._wait('pe', self._deps(reads, writes))
        for fn in fns[:-1]:
            self.q['pe'].append(('o', fn, None, 0))
        self.tick['S_pe'] += 1
        tok = ('S_pe', self.tick['S_pe'])
        self.q['pe'].append(('o', fns[-1], 'S_pe', 1))
        self._commit(tok, reads, writes)
        return tok

    def barrier(self):
        deps = {n: v for n, v in self.tick.items() if v > 0}
        for e in ENG:
            self._wait(e, deps)


class Ring:
    def __init__(self, n, name):
        self.bufs = [Buf('%s%d' % (name, i)) for i in range(n)]
        self.i = 0
        self.n = n

    def next(self):
        i = self.i
        self.i = (self.i + 1) % self.n
        return i, self.bufs[i]


def I(name, *args, **kw):
    return (name, args, kw)


def build_program(debug=None):
    nc = bass.Bass("TRN2", target_bir_lowering=False)
    P = Prog()
    dt_in = lambda name, shape: nc.dram_tensor(name, shape, F32, kind="ExternalInput").ap()
    xsrc = dt_in("xsrc", [NSRC, 128, 2048])
    xhalo = dt_in("xhalo", [NG, 32, 2048])
    hmask_d = dt_in("hmask", [128, NG * 32])
    pown = dt_in("pown", [NG, 128, 4, 256])
    cmat_p = dt_in("cmat_p", [16, 128, 2 * 32 * 128])
    cmat_s = dt_in("cmat_s", [8, 128, 2 * 16 * 128])
    w_in = dt_in("w_in", [2048, 3072])
    w_out = dt_in("w_out", [2048, 2048])
    w_ff1 = dt_in("w_ff1", [2048, 8192])
    w_ff2 = dt_in("w_ff2", [8192, 2048])
    w_gate = dt_in("w_gate", [2048, 2048])
    w_ple = dt_in("w_ple", [256, 2048])
    pp_d = dt_in("pp", [128, NP_IN])
    cdt_d = dt_in("cdt", [128, 2 * 2 * 256])
    ident_d = dt_in("ident", [128, 128])
    yown = nc.dram_tensor("yown", [NOWN, 128, 2048], F32, kind="ExternalOutput").ap()
    dbg = None
    if debug == 'U':
        dbg = nc.dram_tensor("dbg", [128, NSRC * 1024], BF16, kind="ExternalOutput").ap()
    elif debug == 'YT':
        dbg = nc.dram_tensor("dbg", [128, 8 * 3072], BF16, kind="ExternalOutput").ap()

    es = ExitStack()
    with es:
        SLAB = 211968
        slab = es.enter_context(nc.sbuf_tensor("slab", [128, SLAB], U8))
        base = nc.lookup_mloc(slab).addr

        def sb(name, shape, dtype, off):
            nb = int(np.prod(shape[1:])) * (4 if dtype == F32 else 2)
            assert off % 32 == 0, (name, off)
            assert off + nb <= SLAB, (name, off, nb)
            return nc.alloc_sbuf_tensor_at(name, shape, dtype, offset=base + off), off + ((nb + 31) // 32) * 32

        o = 0
        pp, o = sb("pp", [128, NP_ALL], F32, o)
        ident_f, o = sb("ident_f", [128, 128], F32, o)
        ident_b, o = sb("ident_b", [128, 128], BF16, o)
        onesD, o = sb("onesD", [128, 128], BF16, o)
        onesC, o = sb("onesC", [128, 128], BF16, o)
        cdt_b, o = sb("cdt_b", [128, 2, 2, 256], BF16, o)
        hmask, o = sb("hmask", [128, NG, 32], F32, o)
        small, o = sb("small", [128, 64], F32, o)
        assert o <= 8192, o
        O_YT = 8192
        YT, _ = sb("YT", [128, 8, 3072], BF16, O_YT)
        O_U = O_YT + 49152
        O_X = O_U + 98304

        ps = es.enter_context(nc.psum_tensor("ps", [128, 8, 512], F32))
        sems = {}

        def pcol(name, j=0):
            c = PCOLS[name] + j
            return pp[:, c:c + 1]

        def pcols(name, n=16):
            c = PCOLS[name]
            return pp[:, c:c + n]

        B_const = Buf('const')
        B_YT = [Buf('YT%d' % t) for t in range(NOWN)]

        def dbg_dump(stage, ap2d, dtype, bufs):
            if debug != stage:
                return False
            d_ = nc.dram_tensor("dbg", list(ap2d.shape), dtype, kind="ExternalOutput").ap()
            P.dma('sp', I('dma_start', out=d_, in_=ap2d), 'D_out', reads=bufs)
            return True
        B_small = [Buf('small0'), Buf('small1')]

        P.dma('sp', I('dma_start', out=pp[:, 0:NP_IN], in_=pp_d), 'D_init', writes=[B_const])
        P.dma('sp', I('dma_start', out=ident_f[:], in_=ident_d), 'D_init', writes=[B_const])
        P.dma('sp', I('dma_start', out=hmask[:].rearrange("p a b -> p (a b)"), in_=hmask_d), 'D_init', writes=[B_const])
        P.dma('pool', I('dma_start', out=cdt_b[:].rearrange("p a b c -> p (a b c)"), in_=cdt_d), 'D_init2', writes=[B_const])
        P.op('dve', I('tensor_copy', ident_b[:], ident_f[:]), reads=[B_const], writes=[B_const])
        P.op('dve', I('memset', onesD[:], 1.0 / 2048), writes=[B_const])
        P.op('dve', I('memset', onesC[:], 1.0 / 1024), writes=[B_const])
        P.op('dve', I('tensor_scalar', pcols('aeg'), pcols('emb_g'), ALPHA, None, ALU.mult), reads=[B_const], writes=[B_const])
        P.op('dve', I('tensor_scalar', pcols('aeb'), pcols('emb_b'), ALPHA, None, ALU.mult), reads=[B_const], writes=[B_const])
        P.op('dve', I('tensor_scalar', pcols('a1g'), pcols('ln1g'), ALPHA, None, ALU.mult), reads=[B_const], writes=[B_const])
        P.op('dve', I('scalar_tensor_tensor', pcols('a1b'), pcols('ln1b'), ALPHA, pcols('bff2'), ALU.mult, ALU.add), reads=[B_const], writes=[B_const])

        psbank = [Buf('psb%d' % i, excl=True) for i in range(8)]

        def ln_tile_stats(xt, xbuf, rows, si):
            scol = 32 * si
            bs = B_small[si]
            st = small[0:rows, scol:scol + 24]
            mv = small[0:rows, scol + 24:scol + 26]
            rstd = small[0:rows, scol + 26:scol + 27]
            nmr = small[0:rows, scol + 27:scol + 28]
            for c4 in range(4):
                P.op('dve', I('bn_stats', st[:, c4 * 6:(c4 + 1) * 6], xt[:, c4 * 512:(c4 + 1) * 512]), reads=[xbuf], writes=[bs])
            P.op('dve', I('bn_aggr', mv, st.rearrange("p (a b) -> p a b", a=4)), reads=[bs], writes=[bs])
            P.op('act', I('activation', out=rstd, in_=mv[:, 1:2], func=AF.Sqrt, bias=LN_EPS, scale=1.0), reads=[bs], writes=[bs])
            P.op('dve', I('reciprocal', rstd, rstd), reads=[bs], writes=[bs])
            P.op('dve', I('scalar_tensor_tensor', nmr, mv[:, 0:1], -1.0, rstd, ALU.mult, ALU.mult), reads=[bs], writes=[bs])
            P.op('act', I('activation', out=xt, in_=xt, func=AF.Identity, bias=nmr, scale=rstd), reads=[bs, xbuf], writes=[xbuf])

        def transpose_tile(xt, xbuf, rows, banks, evac):
            for q4 in range(4):
                bi = banks[q4]
                fns = []
                for i in range(4):
                    dk = q4 * 4 + i
                    fns.append(I('transpose', ps[:, bi, i * 128:i * 128 + rows], xt[:, dk * 128:(dk + 1) * 128], ident_f[0:rows, 0:rows]))
                P.pe_group(fns, reads=[xbuf, B_const], writes=[psbank[bi]])
                for i in range(4):
                    dk = q4 * 4 + i
                    evac(dk, ps[:, bi, i * 128:i * 128 + rows], psbank[bi])

        U, _ = sb("U", [128, NSRC, 1024], BF16, O_U)
        B_U = [Buf('U%d' % j) for j in range(NSRC)]
        Wf, _ = sb("Wf", [128, 16, 1024], BF16, O_YT)
        stgA = [sb("stgA%d" % i, [128, 2048], F32, O_YT + 32768 + i * 8192)[0] for i in range(2)]
        B_stgA = [Buf('stgA%d' % i) for i in range(2)]
        xnT = [sb("xnT%d" % i, [128, 16, 128], BF16, O_X + i * 4096)[0] for i in range(2)]
        B_xnTc = [[Buf('xnT%d_%d' % (i, k)) for k in range(16)] for i in range(2)]
        B_Wf = Buf('Wf')
        for q4 in range(4):
            P.dma('pool', I('dma_start', out=Wf[:, q4 * 4:(q4 + 1) * 4, :],
                            in_=w_in[q4 * 512:(q4 + 1) * 512, 0:1024].rearrange("(dk p) c -> p dk c", p=128)),
                  'D_wf', writes=[B_Wf])

        def A_stage1(j):
            s = j % 2
            P.dma('sp', I('dma_start', out=stgA[s][:], in_=xsrc[j]), 'D_stg%d' % s, writes=[B_stgA[s]])
            ln_tile_stats(stgA[s][:], B_stgA[s], 128, s)

            def evac(dk, psap, bb):
                if dk < 8:
                    P.op('dve', I('tensor_scalar', xnT[s][:, dk, :], psap, pcol('emb_g', dk), pcol('emb_b', dk), ALU.mult, ALU.add),
                         reads=[bb, B_const], writes=[B_xnTc[s][dk]])
                else:
                    P.op('act', I('activation', out=xnT[s][:, dk, :], in_=psap, func=AF.Identity, bias=pcol('emb_b', dk), scale=pcol('emb_g', dk)),
                         reads=[bb, B_const], writes=[B_xnTc[s][dk]])
            transpose_tile(stgA[s][:], B_stgA[s], 128, [0, 1, 2, 3], evac)

        def A_stage2(j):
            s = j % 2
            bk = [4 + 2 * s, 5 + 2 * s]
            fns = []
            for dk in range(16):
                for cb in range(2):
                    fns.append(I('matmul', ps[:, bk[cb], :], xnT[s][:, dk, :], Wf[:, dk, cb * 512:(cb + 1) * 512],
                                 start=(dk == 0), stop=(dk == 15)))
            P.pe_group(fns, reads=B_xnTc[s] + [B_Wf], writes=[psbank[bk[0]], psbank[bk[1]]])
            for cb in range(2):
                P.op('act', I('activation', out=U[:, j, cb * 512:(cb + 1) * 512], in_=ps[:, bk[cb], :], func=AF.Copy),
                     reads=[psbank[bk[cb]]], writes=[B_U[j]])

        for j in range(NSRC + 1):
            if j < NSRC:
                A_stage1(j)
            if j > 0:
                A_stage2(j - 1)

        if debug == 'U':
            P.dma('sp', I('dma_start', out=dbg, in_=U[:].rearrange("p a b -> p (a b)")), 'D_out', reads=B_U)
            return finish(nc, P, es, sems)

        P.barrier()
        cm = [sb("cm%d" % i, [128, 2, 32, 128], BF16, O_X + i * 16384)[0] for i in range(2)]
        B_cm = [Buf('cm%d' % i) for i in range(2)]
        o2 = O_X + 32768
        PQb = []
        for i in range(2):
            t_, o2 = sb("PQb%d" % i, [128, 2, 512], BF16, o2)
            PQb.append(t_)
        B_PQb = [Buf() for _ in range(2)]
        PQT = []
        for i in range(2):
            t_, o2 = sb("PQT%d" % i, [128, 8, 128], BF16, o2)
            PQT.append(t_)
        B_PQT = [Buf() for _ in range(2)]
        rnd = 0
        for t in range(NOWN):
            prompt = t < 16
            nsrc = 32 if prompt else 16
            j0 = 0 if prompt else 32
            s = t % 2
            if prompt:
                P.dma('pool', I('dma_start', out=cm[s][:].rearrange("p a b c -> p (a b c)"), in_=cmat_p[t]),
                      'D_cm%d' % s, writes=[B_cm[s]])
            else:
                P.dma('pool', I('dma_start', out=cm[s][:, :, 0:16, :], in_=cmat_s[t - 16].rearrange("p (a b c) -> p a b c", a=2, b=16)),
                      'D_cm%d' % s, writes=[B_cm[s]])
            for cb in range(2):
                r = rnd % 2
                rnd += 1
                bP, bQ = 2 * r, 2 * r + 1
                fns = []
                for jj in range(nsrc):
                    fns.append(I('matmul', ps[:, bP, :], cm[s][:, 0, jj, :], U[:, j0 + jj, cb * 512:(cb + 1) * 512],
                                 start=(jj == 0), stop=(jj == nsrc - 1)))
                    fns.append(I('matmul', ps[:, bQ, :], cm[s][:, 1, jj, :], U[:, j0 + jj, cb * 512:(cb + 1) * 512],
                                 start=(jj == 0), stop=(jj == nsrc - 1)))
                P.pe_group(fns, reads=[B_cm[s]], writes=[psbank[bP], psbank[bQ]])
                P.op('act', I('activation', out=PQb[r][:, 0, :], in_=ps[:, bP, :], func=AF.Copy), reads=[psbank[bP]], writes=[B_PQb[r]])
                P.op('dve', I('tensor_copy', PQb[r][:, 1, :], ps[:, bQ, :]), reads=[psbank[bQ]], writes=[B_PQb[r]])
                bT = 4 + r
                psT = ps[:, bT, :].bitcast(BF16)
                fns = []
                for pq in range(2):
                    for i in range(4):
                        fns.append(I('transpose', psT[:, (pq * 4 + i) * 128:(pq * 4 + i + 1) * 128],
                                     PQb[r][:, pq, i * 128:(i + 1) * 128], ident_b[:]))
                P.pe_group(fns, reads=[B_PQb[r], B_const], writes=[psbank[bT]])
                P.op('act', I('activation', out=PQT[r][:].rearrange("p a b -> p (a b)"), in_=psT, func=AF.Copy),
                     reads=[psbank[bT]], writes=[B_PQT[r]])
                bY = 6 + r
                fns = []
                for fgl in range(2):
                    for dp in range(2):
                        oc = (fgl * 2 + dp) * 128
                        k = 0
                        for kc in range(2):
                            for cs in range(2):
                                fns.append(I('matmul', ps[:, bY, oc:oc + 128], cdt_b[:, kc, cs, dp * 128:(dp + 1) * 128],
                                             PQT[r][:, cs * 4 + fgl * 2 + kc, :], start=(k == 0), stop=(k == 3)))
                                k += 1
                P.pe_group(fns, reads=[B_PQT[r], B_const], writes=[psbank[bY]])
                P.op('dve', I('tensor_copy', YT[:, cb * 4:(cb + 1) * 4, t * 128:(t + 1) * 128],
                              ps[:, bY, :].rearrange("p (a b) -> p a b", a=4)),
                     reads=[psbank[bY]], writes=[B_YT[t]])

        if debug == 'YT':
            P.dma('sp', I('dma_start', out=dbg, in_=YT[:].rearrange("p a b -> p (a b)")), 'D_out', reads=B_YT)
            return finish(nc, P, es, sems)

        P.barrier()
        o = O_U
        wple_b, o = sb("wple_b", [128, 2, 2048], BF16, o)
        xres, o = sb("xres", [128, 16, 512], F32, o)
        xbf, o = sb("xbf", [128, 16, 512], BF16, o)
        xbfh, o = sb("xbfh", [128, 16, 32], BF16, o)
        mstat, o = sb("mstat", [128, 512], F32, o)
        rstat, o = sb("rstat", [128, 512], F32, o)
        cT, o_after_cT = sb("cT", [128, 8, 512], F32, o)
        hid = [sb("hid%d" % i, [128, 8, 512], BF16, o + i * 8192)[0] for i in range(2)]
        o = o_after_cT
        hT, o_after_hT = sb("hT", [128, 8, 544], BF16, o)
        headsc, _ = sb("headsc", [128, 8, 512], BF16, o)
        o = o_after_hT
        NWR = 3
        wr = []
        for i in range(NWR):
            t_, o = sb("wr%d" % i, [128, 4096], BF16, o)
            wr.append(t_)
        stg = []
        for i in range(2):
            t_, o = sb("stg%d" % i, [128, 2048], F32, o)
            stg.append(t_)
        pstg, o = sb("pstg", [128, 4, 256], F32, o)
        pT, o = sb("pT", [128, 2, 512], BF16, o)
        NT = 3
        tmpf = []
        for i in range(NT):
            t_, o = sb("tmpf%d" % i, [128, 544], F32, o)
            tmpf.append(t_)
        NTB = 8
        tmpb = []
        for i in range(NTB):
            t_, o = sb("tmpb%d" % i, [128, 512], BF16, o)
            tmpb.append(t_)
        NDG = 16
        dg = []
        for i in range(NDG):
            t_, o = sb("dg%d" % i, [128, 128], BF16, o)
            dg.append(t_)
        print("phase B sbuf end", o, "of", SLAB)

        B_wple = Buf('wple')
        P.dma('pool', I('dma_start', out=wple_b[:], in_=w_ple.rearrange("(kc p) d -> p kc d", p=128)), 'D_wple', writes=[B_wple])
        B_xres = [Buf('xres%d' % i) for i in range(16)]
        B_xbf = [Buf('xbf%d' % i) for i in range(16)]
        B_xbfh = Buf('xbfh')
        B_m = Buf('m')
        B_cT = [Buf('cT%d' % i) for i in range(8)]
        B_hT = [Buf('hT%d' % i) for i in range(8)]
        B_hc = [Buf('hc%d' % i) for i in range(8)]
        B_hid = [[Buf('hid%d_%d' % (i, k)) for k in range(8)] for i in range(2)]
        R_wr = Ring(NWR, 'wr')
        R_stg = Ring(2, 'stg')
        B_pstg = Buf('pstg')
        B_pT = Buf('pT')
        R_tf = Ring(NT, 'tmpf')
        R_tb = Ring(NTB, 'tmpb')
        R_dg = Ring(NDG, 'dg')
        R_ps = Ring(6, 'psr')
        R_ps.bufs = psbank[0:6]
        BM, BS = 6, 7

        def _issue_w(spec):
            w_ap, r0, nk, c0s, ncol = spec[:5]
            if not isinstance(c0s, tuple):
                c0s = (c0s,)
            tot = ncol * len(c0s)
            i, b = R_wr.next()
            view = wr[i][:, 0:nk * tot].rearrange("p (a b) -> p a b", a=nk)
            for ci, c0 in enumerate(c0s):
                P.dma('pool', I('dma_start', out=view[:, :, ci * ncol:(ci + 1) * ncol],
                                in_=w_ap[r0:r0 + nk * 128, c0:c0 + ncol].rearrange("(k p) c -> p k c", p=128)),
                      'D_wr%d' % i, writes=[b])
            return view, b

        WS = {'specs': [], 'issued': [], 'k': 0}
        LOOK = NWR - 1

        def _ahead(n):
            while len(WS['issued']) < min(len(WS['specs']), WS['k'] + n):
                WS['issued'].append(_issue_w(WS['specs'][len(WS['issued'])]))

        def load_w(w_ap, r0, nk, c0, ncol):
            spec = WS['specs'][WS['k']]
            assert spec[1:] == (r0, nk, c0, ncol), (spec[1:], (r0, nk, c0, ncol))
            _ahead(1)
            r = WS['issued'][WS['k']]
            WS['k'] += 1
            _ahead(LOOK)
            return r

        for g_ in range(min(NG, NG_DBG)):
            for j_ in range(8):
                WS['specs'].append((w_in, 0, 16, (1024 + j_ * 128, 2048 + j_ * 128), 128))
            for u_ in range(8):
                WS['specs'].append((w_out, 0, 16, u_ * 256, 256))
            for q_ in range(9):
                if q_ < 8:
                    for u_ in range(4):
                        WS['specs'].append((w_ff1, 0, 16, (q_ * 8 + u_ * 2) * 128, 256))
                if q_ > 0:
                    for u_ in range(4):
                        WS['specs'].append((w_ff2, (q_ - 1) * 1024, 8, u_ * 512, 512))
            for u_ in range(8):
                WS['specs'].append((w_gate, 0, 16, u_ * 256, 256))

        pend_stats = []

        def flush_stats(keep=0):
            while len(pend_stats) > keep:
                fns, rd = pend_stats.pop(0)
                P.pe_group(fns, reads=rd, writes=[psbank[BM], psbank[BS]])

        def ln_fm_stats(r_ap, rbuf, c, nch, ones, defer=2):
            i1, b1 = R_tb.next()
            i2, b2 = R_tb.next()
            P.op('act', I('activation', out=tmpb[i1][:], in_=r_ap, func=AF.Copy), reads=[rbuf], writes=[b1])
            P.op('act', I('activation', out=tmpb[i2][:], in_=r_ap, func=AF.Square), reads=[rbuf], writes=[b2])
            pend_stats.append(([I('matmul', ps[:, BM, :], ones[:], tmpb[i1][:], start=(c == 0), stop=(c == nch - 1)),
                                I('matmul', ps[:, BS, :], ones[:], tmpb[i2][:], start=(c == 0), stop=(c == nch - 1))],
                               [b1, b2, B_const]))
            flush_stats(defer)

        def ln_fm_finish():
            flush_stats(0)
            i, b = R_tf.next()
            tv = tmpf[i][:, 0:512]
            P.op('act', I('activation', out=mstat[:], in_=ps[:, BM, :], func=AF.Copy), reads=[psbank[BM]], writes=[B_m])
            P.op('dve', I('tensor_tensor', tv, mstat[:], mstat[:], ALU.mult), reads=[B_m], writes=[b])
            P.op('dve', I('tensor_tensor', tv, ps[:, BS, :], tv, ALU.subtract), reads=[psbank[BS], b], writes=[b])
            P.op('act', I('activation', out=rstat[:], in_=tv, func=AF.Sqrt, bias=LN_EPS, scale=1.0), reads=[b], writes=[B_m])
            P.op('dve', I('reciprocal', rstat[:], rstat[:]), reads=[B_m], writes=[B_m])

        def ln_fm_center(r_ap, rbuf):
            i, b = R_tf.next()
            tv = tmpf[i][:, 0:512]
            P.op('dve', I('tensor_tensor', tv, r_ap, mstat[:], ALU.subtract), reads=[rbuf, B_m], writes=[b])
            P.op('dve', I('tensor_tensor', tv, tv, rstat[:], ALU.mult), reads=[B_m, b], writes=[b])
            return tv, b

        if dbg_dump('B0', YT[:].rearrange("p a b -> p (a b)"), BF16, B_YT + [B_wple]):
            return finish(nc, P, es, sems)
        for g in range(min(NG, NG_DBG)):
            t0 = 4 * g if g < 4 else 16 + 4 * (g - 4)
            j0 = t0 if g < 4 else 32 + (t0 - 16)
            for tt in range(NTT_DBG):
                si, sbuf_ = R_stg.next()
                rows = 128 if tt < 4 else 32
                if tt < 4:
                    P.dma('sp', I('dma_start', out=stg[si][:], in_=xsrc[j0 + tt]), 'D_stg%d' % si, writes=[sbuf_])
                else:
                    P.dma('sp', I('dma_start', out=stg[si][0:32, :], in_=xhalo[g]), 'D_stg%d' % si, writes=[sbuf_])
                xt = stg[si][0:rows, :]
                ln_tile_stats(xt, sbuf_, rows, tt % 2)
                banks = [R_ps.next()[0] for _ in range(4)]
                if tt < 4:
                    def evac(dk, psap, bb, tt=tt):
                        P.op('dve', I('tensor_scalar', xres[:, dk, tt * 128:(tt + 1) * 128], psap, pcol('aeg', dk), pcol('aeb', dk), ALU.mult, ALU.add),
                             reads=[bb, B_const], writes=[B_xres[dk]])
                        P.op('act', I('activation', out=xbf[:, dk, tt * 128:(tt + 1) * 128], in_=xres[:, dk, tt * 128:(tt + 1) * 128], func=AF.Copy, scale=1.0 / ALPHA),
                             reads=[B_xres[dk]], writes=[B_xbf[dk]])
                else:
                    def evac(dk, psap, bb):
                        P.op('act', I('activation', out=xbfh[:, dk, :], in_=psap, func=AF.Identity, bias=pcol('emb_b', dk), scale=pcol('emb_g', dk)),
                             reads=[bb, B_const], writes=[B_xbfh])
                transpose_tile(xt, sbuf_, rows, banks, evac)

            if g == 0 and dbg_dump('B1', xres[:].rearrange("p a b -> p (a b)"), F32, B_xres + B_xbf + [B_xbfh]):
                return finish(nc, P, es, sems)

            def conv_diags(j, k0, k1):
                r = []
                for k in range(k0, k1):
                    di, db = R_dg.next()
                    P.op('dve', I('tensor_scalar', dg[di][:], ident_f[:], pcol('wdw', j * 31 + k), None, ALU.mult),
                         reads=[B_const], writes=[db])
                    r.append((di, db))
                return r

            def conv_chunk(j, pre):
                bi, bb = R_ps.next()
                dl = list(pre)
                for k in range(31):
                    if k >= len(dl):
                        dl += conv_diags(j, k, k + 1)
                    di, db = dl[k]
                    P.pe_group([I('matmul', ps[:, bi, :], dg[di][:], hT[:, j, k:k + 512], start=(k == 0), stop=(k == 30))],
                               reads=[db, B_hT[j]], writes=[bb])
                P.op('act', I('activation', out=cT[:, j, :], in_=ps[:, bi, :], func=AF.Identity, bias=pcol('bdw', j), scale=1.0),
                     reads=[bb, B_const], writes=[B_cT[j]])
                ln_fm_stats(cT[:, j, :], B_cT[j], j, 8, onesC)

            for j in range(9):
                pre = conv_diags(j - 1, 0, NDG) if j > 0 else []
                if j < 8:
                    wvg, wb = load_w(w_in, 0, 16, (1024 + j * 128, 2048 + j * 128), 128)
                    wv, wg, wgb = wvg[:, :, 0:128], wvg[:, :, 128:256], wb
                    bv, bvb = R_ps.next()
                    bg, bgb = R_ps.next()
                    bh, bhb = R_ps.next()
                    fns = []
                    for dk in range(16):
                        fns.append(I('matmul', ps[:, bv, :], wv[:, dk, :], xbf[:, dk, :], start=(dk == 0), stop=(dk == 15)))
                    for dk in range(16):
                        fns.append(I('matmul', ps[:, bg, :], wg[:, dk, :], xbf[:, dk, :], start=(dk == 0), stop=(dk == 15)))
                    for dk in range(16):
                        fns.append(I('matmul', ps[:, bh, 0:32], wv[:, dk, :], xbfh[:, dk, :], start=(dk == 0), stop=(dk == 15)))
                    for dk in range(16):
                        fns.append(I('matmul', ps[:, bh, 32:64], wg[:, dk, :], xbfh[:, dk, :], start=(dk == 0), stop=(dk == 15)))
                    P.pe_group(fns, reads=[wb, B_xbfh] + B_xbf, writes=[bvb, bgb, bhb])
                    ti, tb = R_tf.next()
                    tf = tmpf[ti]
                    P.op('act', I('activation', out=tf[:, 0:512], in_=ps[:, bg, :], func=AF.Sigmoid), reads=[bgb], writes=[tb])
                    P.op('act', I('activation', out=tf[:, 512:544], in_=ps[:, bh, 32:64], func=AF.Sigmoid), reads=[bhb], writes=[tb])
                    P.op('dve', I('tensor_tensor', hT[:, j, 15:527], ps[:, bv, :], tf[:, 0:512], ALU.mult), reads=[bvb, tb], writes=[B_hT[j]])
                    P.op('dve', I('tensor_tensor', tf[:, 512:544], tf[:, 512:544], hmask[:, g, :], ALU.mult), reads=[tb, B_const], writes=[tb])
                    P.op('dve', I('tensor_tensor', hT[:, j, 0:15], ps[:, bh, 0:15], tf[:, 512:527], ALU.mult), reads=[bhb, tb], writes=[B_hT[j]])
                    P.op('dve', I('tensor_tensor', hT[:, j, 527:542], ps[:, bh, 15:30], tf[:, 527:542], ALU.mult), reads=[bhb, tb], writes=[B_hT[j]])
                if j > 0:
                    conv_chunk(j - 1, pre)

            if g == 0 and dbg_dump('B3', cT[:].rearrange("p a b -> p (a b)"), F32, B_cT + B_hT):
                return finish(nc, P, es, sems)

            ln_fm_finish()
            for j in range(8):
                tap, tbuf = ln_fm_center(cT[:, j, :], B_cT[j])
                P.op('act', I('activation', out=headsc[:, j, :], in_=tap, func=AF.Silu, bias=pcol('convb', j), scale=pcol('convg', j)),
                     reads=[tbuf, B_const] + B_hT, writes=[B_hc[j]])

            if g == 0 and dbg_dump('B4', headsc[:].rearrange("p a b -> p (a b)"), BF16, B_hc):
                return finish(nc, P, es, sems)
            for u in range(8):
                wv, wb = load_w(w_out, 0, 16, u * 256, 256)
                for m in range(2):
                    d = 2 * u + m
                    bi, bb = R_ps.next()
                    fns = []
                    for ck in range(16):
                        if ck < 8:
                            rhs = YT[:, ck, t0 * 128:t0 * 128 + 512]
                        else:
                            rhs = headsc[:, ck - 8, :]
                        fns.append(I('matmul', ps[:, bi, :], wv[:, ck, m * 128:(m + 1) * 128], rhs, start=(ck == 0), stop=(ck == 15)))
                    P.pe_group(fns, reads=[wb] + B_hc + B_YT[t0:t0 + 4], writes=[bb])
                    P.op('dve', I('tensor_tensor', xres[:, d, :], xres[:, d, :], ps[:, bi, :], ALU.add), reads=[bb, B_xres[d]], writes=[B_xres[d]])
                    ln_fm_stats(xres[:, d, :], B_xres[d], d, 16, onesD)

            if g == 0 and dbg_dump('B5', xres[:].rearrange("p a b -> p (a b)"), F32, B_xres):
                return finish(nc, P, es, sems)
            ln_fm_finish()
            for d in range(16):
                tap, tbuf = ln_fm_center(xres[:, d, :], B_xres[d])
                P.op('act', I('activation', out=xres[:, d, :], in_=tap, func=AF.Identity, bias=pcol('a1b', d), scale=pcol('a1g', d)),
                     reads=[tbuf, B_const], writes=[B_xres[d]])
                P.op('pool', I('tensor_scalar', xbf[:, d, :], tap, pcol('ln1g', d), pcol('ln1b', d), ALU.mult, ALU.add),
                     reads=[tbuf, B_const], writes=[B_xbf[d]])

            if g == 0 and dbg_dump('LN1', xres[:].rearrange("p a b -> p (a b)"), F32, B_xres + B_xbf):
                return finish(nc, P, es, sems)
            def ff1(q):
                hb = q % 2
                for u in range(4):
                    wv, wb = load_w(w_ff1, 0, 16, (q * 8 + u * 2) * 128, 256)
                    for m in range(2):
                        fl = u * 2 + m
                        f = q * 8 + fl
                        bi, bb = R_ps.next()
                        fns = []
                        for dk in range(16):
                            fns.append(I('matmul', ps[:, bi, :], wv[:, dk, m * 128:(m + 1) * 128], xbf[:, dk, :], start=(dk == 0), stop=(dk == 15)))
                        P.pe_group(fns, reads=[wb] + B_xbf, writes=[bb])
                        ti, tb = R_tf.next()
                        tv = tmpf[ti][:, 0:512]
                        P.op('act', I('activation', out=tv, in_=ps[:, bi, :], func=AF.Relu, bias=pcol('bff1', f), scale=1.0),
                             reads=[bb, B_const], writes=[tb])
                        P.op('dve', I('tensor_tensor', hid[hb][:, fl, :], tv, tv, ALU.mult), reads=[tb], writes=[B_hid[hb][fl]])

            def ff2(q):
                hb = q % 2
                for u in range(4):
                    wv, wb = load_w(w_ff2, q * 1024, 8, u * 512, 512)
                    for m in range(4):
                        d = u * 4 + m
                        bi, bb = R_ps.next()
                        fns = []
                        for fk in range(8):
                            fns.append(I('matmul', ps[:, bi, :], wv[:, fk, m * 128:(m + 1) * 128], hid[hb][:, fk, :], start=(fk == 0), stop=(fk == 7)))
                        P.pe_group(fns, reads=[wb] + B_hid[hb], writes=[bb])
                        P.op('dve', I('tensor_tensor', xres[:, d, :], xres[:, d, :], ps[:, bi, :], ALU.add), reads=[bb, B_xres[d]], writes=[B_xres[d]])
                        if q == 7:
                            ln_fm_stats(xres[:, d, :], B_xres[d], d, 16, onesD)

            for q in range(9):
                if q < 8:
                    ff1(q)
                if q > 0:
                    ff2(q - 1)
            if g == 0 and dbg_dump('FF', xres[:].rearrange("p a b -> p (a b)"), F32, B_xres):
                return finish(nc, P, es, sems)
            ln_fm_finish()
            for d in range(16):
                tap, tbuf = ln_fm_center(xres[:, d, :], B_xres[d])
                P.op('act', I('activation', out=xres[:, d, :], in_=tap, func=AF.Identity, bias=pcol('ln2b', d), scale=pcol('ln2g', d)),
                     reads=[tbuf, B_const], writes=[B_xres[d]])
                P.op('pool', I('tensor_scalar', xbf[:, d, :], tap, pcol('ln2g', d), pcol('ln2b', d), ALU.mult, ALU.add),
                     reads=[tbuf, B_const], writes=[B_xbf[d]])
            if g == 0 and dbg_dump('LN2', xres[:].rearrange("p a b -> p (a b)"), F32, B_xres + B_xbf):
                return finish(nc, P, es, sems)
            P.dma('sp', I('dma_start', out=pstg[:], in_=pown[g]), 'D_pstg', writes=[B_pstg])
            for tt in range(4):
                bi, bb = R_ps.next()
                P.pe_group([I('transpose', ps[:, bi, kc * 128:(kc + 1) * 128], pstg[:, tt, kc * 128:(kc + 1) * 128], ident_f[:]) for kc in range(2)],
                           reads=[B_pstg, B_const], writes=[bb])
                P.op('act', I('activation', out=pT[:, :, tt * 128:(tt + 1) * 128], in_=ps[:, bi, 0:256].rearrange("p (a b) -> p a b", a=2), func=AF.Copy),
                     reads=[bb], writes=[B_pT])
            for u in range(8):
                wv, wb = load_w(w_gate, 0, 16, u * 256, 256)
                for m in range(2):
                    d = 2 * u + m
                    bi, bb = R_ps.next()
                    be, beb = R_ps.next()
                    fns = []
                    for dk in range(16):
                        fns.append(I('matmul', ps[:, bi, :], wv[:, dk, m * 128:(m + 1) * 128], xbf[:, dk, :], start=(dk == 0), stop=(dk == 15)))
                    for kc in range(2):
                        fns.append(I('matmul', ps[:, be, :], wple_b[:, kc, d * 128:(d + 1) * 128], pT[:, kc, :], start=(kc == 0), stop=(kc == 1)))
                    P.pe_group(fns, reads=[wb, B_wple, B_pT] + B_xbf, writes=[bb, beb])
                    ti, tb = R_tf.next()
                    tv = tmpf[ti][:, 0:512]
                    P.op('act', I('activation', out=tv, in_=ps[:, bi, :], func=AF.Sigmoid, bias=pcol('bgate', d), scale=1.0),
                         reads=[bb, B_const], writes=[tb])
                    P.op('dve', I('tensor_tensor', tv, tv, ps[:, be, :], ALU.mult), reads=[beb, tb], writes=[tb])
                    P.op('dve', I('tensor_tensor', xres[:, d, :], xres[:, d, :], tv, ALU.add), reads=[tb, B_xres[d]], writes=[B_xres[d]])
                    ln_fm_stats(xres[:, d, :], B_xres[d], d, 16, onesD)
            if g == 0 and dbg_dump('G', xres[:].rearrange("p a b -> p (a b)"), F32, B_xres):
                return finish(nc, P, es, sems)
            ln_fm_finish()
            for d in range(16):
                tap, tbuf = ln_fm_center(xres[:, d, :], B_xres[d])
                P.op('act', I('activation', out=xres[:, d, :], in_=tap, func=AF.Identity, bias=pcol('ln3b', d), scale=pcol('ln3g', d)),
                     reads=[tbuf, B_const], writes=[B_xres[d]])
            if g == 0 and dbg_dump('LN3', xres[:].rearrange("p a b -> p (a b)"), F32, B_xres):
                return finish(nc, P, es, sems)
            for tt in range(4):
                si, sbuf_ = R_stg.next()
                for q4 in range(4):
                    bi, bb = R_ps.next()
                    P.pe_group([I('transpose', ps[:, bi, i * 128:(i + 1) * 128], xres[:, q4 * 4 + i, tt * 128:(tt + 1) * 128], ident_f[:]) for i in range(4)],
                               reads=[B_const] + B_xres[q4 * 4:q4 * 4 + 4], writes=[bb])
                    if q4 % 2 == 0:
                        P.op('act', I('activation', out=stg[si][:, q4 * 512:(q4 + 1) * 512], in_=ps[:, bi, :], func=AF.Copy), reads=[bb], writes=[sbuf_])
                    else:
                        P.op('dve', I('tensor_copy', stg[si][:, q4 * 512:(q4 + 1) * 512], ps[:, bi, :]), reads=[bb], writes=[sbuf_])
                P.dma('sp', I('dma_start', out=yown[t0 + tt], in_=stg[si][:]), 'D_stg%d' % si, reads=[sbuf_])

        return finish(nc, P, es, sems)


def finish(nc, P, es, sems):
    deps = {n: v for n, v in P.tick.items() if v > 0}
    P._wait('sp', deps)
    for name in sorted(P.semnames):
        sems[name] = es.enter_context(nc.semaphore(name))
    block = es.enter_context(nc.Block())

    def runner(items):
        def run(e):
            for it in items:
                if it[0] == 'w':
                    e.wait_ge(sems[it[1]], it[2])
                else:
                    nm, a, kw = it[1]
                    ins = getattr(e, nm)(*a, **kw)
                    if it[2] is not None:
                        ins.then_inc(sems[it[2]], it[3])
        return run
    block.sync(runner(P.q['sp']))
    block.scalar(runner(P.q['act']))
    block.vector(runner(P.q['dve']))
    block.gpsimd(runner(P.q['pool']))
    block.tensor(runner(P.q['pe']))
    return nc


def _src_tiles(h):
    o = [('p', 16 * h + j) for j in range(16)] + [('p', 16 * (1 - h) + j) for j in range(16)]
    o += [('s', 8 * h + j) for j in range(8)] + [('s', 8 * (1 - h) + j) for j in range(8)]
    return o


_CONST_CACHE = {}


def _dft_tables(h):
    if h in _CONST_CACHE:
        return _CONST_CACHE[h]
    order = _src_tiles(h)
    out = []
    for (S, nown_t, src_list, own_base) in ((4096, 16, [t for (q, t) in order if q == 'p'], 16 * h),
                                            (2048, 8, [t for (q, t) in order if q == 's'], 8 * h)):
        k = np.arange(S, dtype=np.float64)
        ctab = (np.cos(2 * np.pi * k / S) / np.sqrt(S)).astype(np.float32)
        stab = (np.sin(2 * np.pi * k / S) / np.sqrt(S)).astype(np.float32)
        nsrc = len(src_list)
        pos_src = (np.array(src_list, dtype=np.int64)[None, :] * 128 + np.arange(128, dtype=np.int64)[:, None])
        cm = np.empty((nown_t, 128, 2, nsrc, 128), dtype=np.float32)
        for t in range(nown_t):
            pos_own = (own_base + t) * 128 + np.arange(128, dtype=np.int64)
            idx = (pos_src[:, :, None] * pos_own[None, None, :]) % S
            cm[t, :, 0] = ctab[idx]
            cm[t, :, 1] = stab[idx]
        out.append(cm.reshape(nown_t, 128, 2 * nsrc * 128))
    _CONST_CACHE[h] = out
    return out


def _chan_table():
    d = np.arange(256, dtype=np.int64)
    idx = (d[:, None] * d[None, :]) % 256
    k = np.arange(256, dtype=np.float64)
    c = (np.cos(2 * np.pi * k / 256) / 16.0).astype(np.float32)[idx]
    s = (-np.sin(2 * np.pi * k / 256) / 16.0).astype(np.float32)[idx]
    t = np.stack([c, s], axis=1)
    t = t.reshape(2, 128, 2, 256).transpose(1, 0, 2, 3)
    return np.ascontiguousarray(t.reshape(128, 2 * 2 * 256))


def _cols(v, n):
    return np.ascontiguousarray(np.asarray(v, dtype=np.float32).reshape(n, 128).T)


def _prep(inputs):
    xp = np.asarray(inputs['x_prompt'], dtype=np.float32)
    xs = np.asarray(inputs['x_sample'], dtype=np.float32)
    pp_ = np.asarray(inputs['p_prompt'], dtype=np.float32)[0]
    ps_ = np.asarray(inputs['p_sample'], dtype=np.float32)[0]
    g = lambda n: np.asarray(inputs[n], dtype=np.float32)
    wdw = g('w_dw')[0]
    wdw_cols = np.ascontiguousarray(wdw.reshape(31, 8, 128).transpose(2, 1, 0).reshape(128, 248))
    pp = np.concatenate([
        _cols(g('emb_ln_g'), 16), _cols(g('emb_ln_b'), 16), _cols(g('conv_ln_g')[0], 8), _cols(g('conv_ln_b')[0], 8),
        _cols(g('b_dw')[0], 8), wdw_cols, _cols(g('ln1_g')[0], 16), _cols(g('ln1_b')[0], 16), _cols(g('b_ff1')[0], 64),
        _cols(g('b_ff2')[0], 16), _cols(g('ln2_g')[0], 16), _cols(g('ln2_b')[0], 16), _cols(g('b_gate')[0], 16),
        _cols(g('ln3_g')[0], 16), _cols(g('ln3_b')[0], 16)], axis=1)
    assert pp.shape == (128, NP_IN)
    shared = dict(w_in=np.ascontiguousarray(g('w_in')[0]), w_out=np.ascontiguousarray(g('w_out')[0]),
                  w_ff1=np.ascontiguousarray(g('w_ff1')[0]), w_ff2=np.ascontiguousarray(g('w_ff2')[0]),
                  w_gate=np.ascontiguousarray(g('w_gate')[0]), w_ple=np.ascontiguousarray(g('w_ple')[0]),
                  pp=np.ascontiguousarray(pp), cdt=_chan_table(), ident=np.eye(128, dtype=np.float32))
    in_maps = []
    for c in range(8):
        b, h = c // 2, c % 2
        order = _src_tiles(h)
        seqs = {'p': xp[b], 's': xs[b]}
        xsrc = np.stack([seqs[q][t * 128:(t + 1) * 128] for (q, t) in order], axis=0)
        xhalo = np.zeros((NG, 32, 2048), dtype=np.float32)
        hmask = np.zeros((NG, 32), dtype=np.float32)
        pown = np.empty((NG, 128, 4, 256), dtype=np.float32)
        for gi in range(NG):
            if gi < 4:
                seq, S, g0, pseq = xp[b], 4096, 2048 * h + 512 * gi, pp_[b]
            else:
                seq, S, g0, pseq = xs[b], 2048, 1024 * h + 512 * (gi - 4), ps_[b]
            for r in range(30):
                pos = g0 - 15 + r if r < 15 else g0 + 512 + (r - 15)
                if 0 <= pos < S:
                    xhalo[gi, r] = seq[pos]
                    hmask[gi, r] = 1.0
            pown[gi] = pseq[g0:g0 + 512].reshape(4, 128, 256).transpose(1, 0, 2)
        cmp_, cms_ = _dft_tables(h)
        m = dict(shared)
        m.update(xsrc=np.ascontiguousarray(xsrc), xhalo=xhalo,
                 hmask=np.ascontiguousarray(np.broadcast_to(hmask.reshape(1, NG * 32), (128, NG * 32))),
                 pown=pown, cmat_p=cmp_, cmat_s=cms_)
        in_maps.append(m)
    return in_maps


_NC_CACHE = {}


def kernel(**inputs):
    in_maps = _prep(inputs)
    if 'nc' not in _NC_CACHE:
        _NC_CACHE['nc'] = build_program()
    nc = _NC_CACHE['nc']
    res = run_bass_kernel_spmd(nc, in_maps, core_ids=list(range(8)))
    y_prompt = np.empty((4, 4096, 2048), dtype=np.float32)
    y_sample = np.empty((4, 2048, 2048), dtype=np.float32)
    for c in range(8):
        b, h = c // 2, c % 2
        y = np.asarray(res.results[c]["yown"], dtype=np.float32).reshape(NOWN * 128, 2048)
        y_prompt[b, 2048 * h:2048 * h + 2048] = y[0:2048]
        y_sample[b, 1024 * h:1024 * h + 1024] = y[2048:3072]
    return (y_prompt, y_sample)
```

```python
import numpy as np
from collections import defaultdict
from contextlib import ExitStack
import concourse.bass as bass
import concourse.mybir as mybir
from concourse.bass_utils import run_bass_kernel_spmd

F32 = mybir.dt.float32
BF16 = mybir.dt.bfloat16
U8 = mybir.dt.uint8
AF = mybir.ActivationFunctionType
ALU = mybir.AluOpType

ALPHA = float(2.0 ** 0.25)
LN_EPS = 1e-5
NSRC = 48
NOWN = 24
NG = 6
ENG = ('pe', 'act', 'dve', 'pool', 'sp')
import os
NTT_DBG = int(os.environ.get('NTT_DBG', '5'))
NG_DBG = int(os.environ.get('NG_DBG', '6'))

PCOLS = {}
_off = 0
for _n, _w in (('emb_g', 16), ('emb_b', 16), ('convg', 8), ('convb', 8), ('bdw', 8), ('wdw', 248),
               ('ln1g', 16), ('ln1b', 16), ('bff1', 64), ('bff2', 16), ('ln2g', 16), ('ln2b', 16),
               ('bgate', 16), ('ln3g', 16), ('ln3b', 16),
               ('aeg', 16), ('aeb', 16), ('a1g', 16), ('a1b', 16)):
    PCOLS[_n] = _off
    _off += _w
NP_IN = PCOLS['aeg']
NP_ALL = _off


class Buf:
    __slots__ = ('rd', 'wr', 'name', 'excl')

    def __init__(self, name='', excl=False):
        self.rd = {}
        self.wr = None
        self.name = name
        self.excl = excl


class Prog:
    def __init__(self):
        self.q = {e: [] for e in ENG}
        self.tick = defaultdict(int)
        self.waited = defaultdict(int)
        self.semnames = set('S_' + e for e in ENG)

    def _wait(self, eng, deps):
        for name, v in deps.items():
            if eng == 'pe' and name == 'S_pe':
                continue
            if self.waited[(eng, name)] >= v:
                continue
            self.waited[(eng, name)] = v
            self.q[eng].append(('w', name, v))

    @staticmethod
    def _deps(reads, writes, eng=None):
        d = {}

        def add(n, v):
            if d.get(n, 0) < v:
                d[n] = v
        for b in reads:
            if b.wr is not None:
                add(*b.wr)
            if b.excl:
                for n, v in b.rd.items():
                    if n != 'S_' + str(eng):
                        add(n, v)
        for b in writes:
            if b.wr is not None:
                add(*b.wr)
            for n, v in b.rd.items():
                add(n, v)
        return d

    def _commit(self, tok, reads, writes):
        name, v = tok
        for b in reads:
            if b.rd.get(name, 0) < v:
                b.rd[name] = v
        for b in writes:
            b.wr = tok
            b.rd = {}

    def op(self, eng, fn, reads=(), writes=(), sem=None, inc=1):
        self._wait(eng, self._deps(reads, writes, eng))
        name = sem or ('S_' + eng)
        self.semnames.add(name)
        self.tick[name] += inc
        tok = (name, self.tick[name])
        self.q[eng].append(('o', fn, name, inc))
        self._commit(tok, reads, writes)
        return tok

    def dma(self, eng, fn, sem, reads=(), writes=()):
        return self.op(eng, fn, reads, writes, sem=sem, inc=16)

    def pe_group(self, fns, reads=(), writes=()):
        self._wait('pe', self._deps(reads, writes))
        for fn in fns[:-1]:
            self.q['pe'].append(('o', fn, None, 0))
        self.tick['S_pe'] += 1
        tok = ('S_pe', self.tick['S_pe'])
        self.q['pe'].append(('o', fns[-1], 'S_pe', 1))
        self._commit(tok, reads, writes)
        return tok

    def barrier(self):
        deps = {n: v for n, v in self.tick.items() if v > 0}
        for e in ENG:
            self._wait(e, deps)


class Ring:
    def __init__(self, n, name):
        self.bufs = [Buf('%s%d' % (name, i)) for i in range(n)]
        self.i = 0
        self.n = n

    def next(self):
        i = self.i
        self.i = (self.i + 1) % self.n
        return i, self.bufs[i]


def I(name, *args, **kw):
    return (name, args, kw)


def build_program(debug=None):
    nc = bass.Bass("TRN2", target_bir_lowering=False)
    P = Prog()
    dt_in = lambda name, shape: nc.dram_tensor(name, shape, F32, kind="ExternalInput").ap()
    xsrc = dt_in("xsrc", [NSRC, 128, 2048])
    xhalo = dt_in("xhalo", [NG, 32, 2048])
    hmask_d = dt_in("hmask", [128, NG * 32])
    pown = dt_in("pown", [NG, 128, 4, 256])
    cmat_p = dt_in("cmat_p", [16, 128, 2 * 32 * 128])
    cmat_s = dt_in("cmat_s", [8, 128, 2 * 16 * 128])
    w_in = dt_in("w_in", [2048, 3072])
    w_out = dt_in("w_out", [2048, 2048])
    w_ff1 = dt_in("w_ff1", [2048, 8192])
    w_ff2 = dt_in("w_ff2", [8192, 2048])
    w_gate = dt_in("w_gate", [2048, 2048])
    w_ple = dt_in("w_ple", [256, 2048])
    pp_d = dt_in("pp", [128, NP_IN])
    cdt_d = dt_in("cdt", [128, 2 * 2 * 256])
    ident_d = dt_in("ident", [128, 128])
    yown = nc.dram_tensor("yown", [NOWN, 128, 2048], F32, kind="ExternalOutput").ap()
    dbg = None
    if debug == 'U':
        dbg = nc.dram_tensor("dbg", [128, NSRC * 1024], BF16, kind="ExternalOutput").ap()
    elif debug == 'YT':
        dbg = nc.dram_tensor("dbg", [128, 8 * 3072], BF16, kind="ExternalOutput").ap()

    es = ExitStack()
    with es:
        SLAB = 211968
        slab = es.enter_context(nc.sbuf_tensor("slab", [128, SLAB], U8))
        base = nc.lookup_mloc(slab).addr

        def sb(name, shape, dtype, off):
            nb = int(np.prod(shape[1:])) * (4 if dtype == F32 else 2)
            assert off % 32 == 0, (name, off)
            assert off + nb <= SLAB, (name, off, nb)
            return nc.alloc_sbuf_tensor_at(name, shape, dtype, offset=base + off), off + ((nb + 31) // 32) * 32

        o = 0
        pp, o = sb("pp", [128, NP_ALL], F32, o)
        ident_f, o = sb("ident_f", [128, 128], F32, o)
        ident_b, o = sb("ident_b", [128, 128], BF16, o)
        onesD, o = sb("onesD", [128, 128], BF16, o)
        onesC, o = sb("onesC", [128, 128], BF16, o)
        cdt_b, o = sb("cdt_b", [128, 2, 2, 256], BF16, o)
        hmask, o = sb("hmask", [128, NG, 32], F32, o)
        small, o = sb("small", [128, 64], F32, o)
        assert o <= 8192, o
        O_YT = 8192
        YT, _ = sb("YT", [128, 8, 3072], BF16, O_YT)
        O_U = O_YT + 49152
        O_X = O_U + 98304

        ps = es.enter_context(nc.psum_tensor("ps", [128, 8, 512], F32))
        sems = {}

        def pcol(name, j=0):
            c = PCOLS[name] + j
            return pp[:, c:c + 1]

        def pcols(name, n=16):
            c = PCOLS[name]
            return pp[:, c:c + n]

        B_const = Buf('const')
        B_YT = [Buf('YT%d' % t) for t in range(NOWN)]

        def dbg_dump(stage, ap2d, dtype, bufs):
            if debug != stage:
                return False
            d_ = nc.dram_tensor("dbg", list(ap2d.shape), dtype, kind="ExternalOutput").ap()
            P.dma('sp', I('dma_start', out=d_, in_=ap2d), 'D_out', reads=bufs)
            return True
        B_small = [Buf('small0'), Buf('small1')]

        P.dma('sp', I('dma_start', out=pp[:, 0:NP_IN], in_=pp_d), 'D_init', writes=[B_const])
        P.dma('sp', I('dma_start', out=ident_f[:], in_=ident_d), 'D_init', writes=[B_const])
        P.dma('sp', I('dma_start', out=hmask[:].rearrange("p a b -> p (a b)"), in_=hmask_d), 'D_init', writes=[B_const])
        P.dma('pool', I('dma_start', out=cdt_b[:].rearrange("p a b c -> p (a b c)"), in_=cdt_d), 'D_init2', writes=[B_const])
        P.op('dve', I('tensor_copy', ident_b[:], ident_f[:]), reads=[B_const], writes=[B_const])
        P.op('dve', I('memset', onesD[:], 1.0 / 2048), writes=[B_const])
        P.op('dve', I('memset', onesC[:], 1.0 / 1024), writes=[B_const])
        P.op('dve', I('tensor_scalar', pcols('aeg'), pcols('emb_g'), ALPHA, None, ALU.mult), reads=[B_const], writes=[B_const])
        P.op('dve', I('tensor_scalar', pcols('aeb'), pcols('emb_b'), ALPHA, None, ALU.mult), reads=[B_const], writes=[B_const])
        P.op('dve', I('tensor_scalar', pcols('a1g'), pcols('ln1g'), ALPHA, None, ALU.mult), reads=[B_const], writes=[B_const])
        P.op('dve', I('scalar_tensor_tensor', pcols('a1b'), pcols('ln1b'), ALPHA, pcols('bff2'), ALU.mult, ALU.add), reads=[B_const], writes=[B_const])

        psbank = [Buf('psb%d' % i, excl=True) for i in range(8)]

        def ln_tile_stats(xt, xbuf, rows, si):
            scol = 32 * si
            bs = B_small[si]
            st = small[0:rows, scol:scol + 24]
            mv = small[0:rows, scol + 24:scol + 26]
            rstd = small[0:rows, scol + 26:scol + 27]
            nmr = small[0:rows, scol + 27:scol + 28]
            for c4 in range(4):
                P.op('dve', I('bn_stats', st[:, c4 * 6:(c4 + 1) * 6], xt[:, c4 * 512:(c4 + 1) * 512]), reads=[xbuf], writes=[bs])
            P.op('dve', I('bn_aggr', mv, st.rearrange("p (a b) -> p a b", a=4)), reads=[bs], writes=[bs])
            P.op('act', I('activation', out=rstd, in_=mv[:, 1:2], func=AF.Sqrt, bias=LN_EPS, scale=1.0), reads=[bs], writes=[bs])
            P.op('dve', I('reciprocal', rstd, rstd), reads=[bs], writes=[bs])
            P.op('dve', I('scalar_tensor_tensor', nmr, mv[:, 0:1], -1.0, rstd, ALU.mult, ALU.mult), reads=[bs], writes=[bs])
            P.op('act', I('activation', out=xt, in_=xt, func=AF.Identity, bias=nmr, scale=rstd), reads=[bs, xbuf], writes=[xbuf])

        def transpose_tile(xt, xbuf, rows, banks, evac):
            for q4 in range(4):
                bi = banks[q4]
                fns = []
                for i in range(4):
                    dk = q4 * 4 + i
                    fns.append(I('transpose', ps[:, bi, i * 128:i * 128 + rows], xt[:, dk * 128:(dk + 1) * 128], ident_f[0:rows, 0:rows]))
                P.pe_group(fns, reads=[xbuf, B_const], writes=[psbank[bi]])
                for i in range(4):
                    dk = q4 * 4 + i
                    evac(dk, ps[:, bi, i * 128:i * 128 + rows], psbank[bi])

        U, _ = sb("U", [128, NSRC, 1024], BF16, O_U)
        B_U = [Buf('U%d' % j) for j in range(NSRC)]
        Wf, _ = sb("Wf", [128, 16, 1024], BF16, O_YT)
        stgA = [sb("stgA%d" % i, [128, 2048], F32, O_YT + 32768 + i * 8192)[0] for i in range(2)]
        B_stgA = [Buf('stgA%d' % i) for i in range(2)]
        xnT = [sb("xnT%d" % i, [128, 16, 128], BF16, O_X + i * 4096)[0] for i in range(2)]
        B_xnTc = [[Buf('xnT%d_%d' % (i, k)) for k in range(16)] for i in range(2)]
        B_Wf = Buf('Wf')
        for q4 in range(4):
            P.dma('pool', I('dma_start', out=Wf[:, q4 * 4:(q4 + 1) * 4, :],
                            in_=w_in[q4 * 512:(q4 + 1) * 512, 0:1024].rearrange("(dk p) c -> p dk c", p=128)),
                  'D_wf', writes=[B_Wf])

        def A_stage1a(j):
            s = j % 2
            P.dma('sp', I('dma_start', out=stgA[s][:], in_=xsrc[j]), 'D_stg%d' % s, writes=[B_stgA[s]])
            ln_tile_stats(stgA[s][:], B_stgA[s], 128, s)

        def A_stage1b(j):
            s = j % 2

            def evac(dk, psap, bb):
                if dk < 8:
                    P.op('dve', I('tensor_scalar', xnT[s][:, dk, :], psap, pcol('emb_g', dk), pcol('emb_b', dk), ALU.mult, ALU.add),
                         reads=[bb, B_const], writes=[B_xnTc[s][dk]])
                else:
                    P.op('act', I('activation', out=xnT[s][:, dk, :], in_=psap, func=AF.Identity, bias=pcol('emb_b', dk), scale=pcol('emb_g', dk)),
                         reads=[bb, B_const], writes=[B_xnTc[s][dk]])
            transpose_tile(stgA[s][:], B_stgA[s], 128, [0, 1, 2, 3], evac)

        def A_stage2(j):
            s = j % 2
            bk = [4 + 2 * s, 5 + 2 * s]
            fns = []
            for dk in range(16):
                for cb in range(2):
                    fns.append(I('matmul', ps[:, bk[cb], :], xnT[s][:, dk, :], Wf[:, dk, cb * 512:(cb + 1) * 512],
                                 start=(dk == 0), stop=(dk == 15)))
            P.pe_group(fns, reads=B_xnTc[s] + [B_Wf], writes=[psbank[bk[0]], psbank[bk[1]]])
            for cb in range(2):
                P.op('act', I('activation', out=U[:, j, cb * 512:(cb + 1) * 512], in_=ps[:, bk[cb], :], func=AF.Copy),
                     reads=[psbank[bk[cb]]], writes=[B_U[j]])

        for j in range(NSRC + 2):
            if j < NSRC:
                A_stage1a(j)
            if 1 <= j <= NSRC:
                A_stage1b(j - 1)
            if j >= 2:
                A_stage2(j - 2)

        if debug == 'U':
            P.dma('sp', I('dma_start', out=dbg, in_=U[:].rearrange("p a b -> p (a b)")), 'D_out', reads=B_U)
            return finish(nc, P, es, sems)

        P.barrier()
        cm = [sb("cm%d" % i, [128, 2, 32, 128], BF16, O_X + i * 16384)[0] for i in range(2)]
        B_cm = [Buf('cm%d' % i) for i in range(2)]
        o2 = O_X + 32768
        PQb = []
        for i in range(2):
            t_, o2 = sb("PQb%d" % i, [128, 2, 512], BF16, o2)
            PQb.append(t_)
        B_PQb = [Buf() for _ in range(2)]
        PQT = []
        for i in range(2):
            t_, o2 = sb("PQT%d" % i, [128, 8, 128], BF16, o2)
            PQT.append(t_)
        B_PQT = [Buf() for _ in range(2)]
        rnd = 0
        for t in range(NOWN):
            prompt = t < 16
            nsrc = 32 if prompt else 16
            j0 = 0 if prompt else 32
            s = t % 2
            if prompt:
                P.dma('pool', I('dma_start', out=cm[s][:].rearrange("p a b c -> p (a b c)"), in_=cmat_p[t]),
                      'D_cm%d' % s, writes=[B_cm[s]])
            else:
                P.dma('pool', I('dma_start', out=cm[s][:, :, 0:16, :], in_=cmat_s[t - 16].rearrange("p (a b c) -> p a b c", a=2, b=16)),
                      'D_cm%d' % s, writes=[B_cm[s]])
            for cb in range(2):
                r = rnd % 2
                rnd += 1
                bP, bQ = 2 * r, 2 * r + 1
                fns = []
                for jj in range(nsrc):
                    fns.append(I('matmul', ps[:, bP, :], cm[s][:, 0, jj, :], U[:, j0 + jj, cb * 512:(cb + 1) * 512],
                                 start=(jj == 0), stop=(jj == nsrc - 1)))
                    fns.append(I('matmul', ps[:, bQ, :], cm[s][:, 1, jj, :], U[:, j0 + jj, cb * 512:(cb + 1) * 512],
                                 start=(jj == 0), stop=(jj == nsrc - 1)))
                P.pe_group(fns, reads=[B_cm[s]], writes=[psbank[bP], psbank[bQ]])
                P.op('act', I('activation', out=PQb[r][:, 0, :], in_=ps[:, bP, :], func=AF.Copy), reads=[psbank[bP]], writes=[B_PQb[r]])
                P.op('dve', I('tensor_copy', PQb[r][:, 1, :], ps[:, bQ, :]), reads=[psbank[bQ]], writes=[B_PQb[r]])
                bT = 4 + r
                psT = ps[:, bT, :].bitcast(BF16)
                fns = []
                for pq in range(2):
                    for i in range(4):
                        fns.append(I('transpose', psT[:, (pq * 4 + i) * 128:(pq * 4 + i + 1) * 128],
                                     PQb[r][:, pq, i * 128:(i + 1) * 128], ident_b[:]))
                P.pe_group(fns, reads=[B_PQb[r], B_const], writes=[psbank[bT]])
                P.op('act', I('activation', out=PQT[r][:].rearrange("p a b -> p (a b)"), in_=psT, func=AF.Copy),
                     reads=[psbank[bT]], writes=[B_PQT[r]])
                bY = 6 + r
                fns = []
                for fgl in range(2):
                    for dp in range(2):
                        oc = (fgl * 2 + dp) * 128
                        k = 0
                        for kc in range(2):
                            for cs in range(2):
                                fns.append(I('matmul', ps[:, bY, oc:oc + 128], cdt_b[:, kc, cs, dp * 128:(dp + 1) * 128],
                                             PQT[r][:, cs * 4 + fgl * 2 + kc, :], start=(k == 0), stop=(k == 3)))
                                k += 1
                P.pe_group(fns, reads=[B_PQT[r], B_const], writes=[psbank[bY]])
                P.op('dve', I('tensor_copy', YT[:, cb * 4:(cb + 1) * 4, t * 128:(t + 1) * 128],
                              ps[:, bY, :].rearrange("p (a b) -> p a b", a=4)),
                     reads=[psbank[bY]], writes=[B_YT[t]])

        if debug == 'YT':
            P.dma('sp', I('dma_start', out=dbg, in_=YT[:].rearrange("p a b -> p (a b)")), 'D_out', reads=B_YT)
            return finish(nc, P, es, sems)

        P.barrier()
        o = O_U
        wple_b, o = sb("wple_b", [128, 2, 2048], BF16, o)
        xres, o = sb("xres", [128, 16, 512], F32, o)
        xbf, o = sb("xbf", [128, 16, 512], BF16, o)
        xbfh, o = sb("xbfh", [128, 16, 32], BF16, o)
        mstat, o = sb("mstat", [128, 512], F32, o)
        rstat, o = sb("rstat", [128, 512], F32, o)
        cT, o_after_cT = sb("cT", [128, 8, 512], F32, o)
        hid = [sb("hid%d" % i, [128, 8, 512], BF16, o + i * 8192)[0] for i in range(2)]
        o = o_after_cT
        hT, o_after_hT = sb("hT", [128, 8, 544], BF16, o)
        headsc, _ = sb("headsc", [128, 8, 512], BF16, o)
        o = o_after_hT
        NWR = 3
        wr = []
        for i in range(NWR):
            t_, o = sb("wr%d" % i, [128, 4096], BF16, o)
            wr.append(t_)
        stg = []
        for i in range(2):
            t_, o = sb("stg%d" % i, [128, 2048], F32, o)
            stg.append(t_)
        pstg, o = sb("pstg", [128, 4, 256], F32, o)
        pT, o = sb("pT", [128, 2, 512], BF16, o)
        NT = 3
        tmpf = []
        for i in range(NT):
            t_, o = sb("tmpf%d" % i, [128, 544], F32, o)
            tmpf.append(t_)
        NTB = 8
        tmpb = []
        for i in range(NTB):
            t_, o = sb("tmpb%d" % i, [128, 512], BF16, o)
            tmpb.append(t_)
        NDG = 16
        dg = []
        for i in range(NDG):
            t_, o = sb("dg%d" % i, [128, 128], BF16, o)
            dg.append(t_)
        print("phase B sbuf end", o, "of", SLAB)

        B_wple = Buf('wple')
        P.dma('pool', I('dma_start', out=wple_b[:], in_=w_ple.rearrange("(kc p) d -> p kc d", p=128)), 'D_wple', writes=[B_wple])
        B_xres = [Buf('xres%d' % i) for i in range(16)]
        B_xbf = [Buf('xbf%d' % i) for i in range(16)]
        B_xbfh = Buf('xbfh')
        B_m = Buf('m')
        B_cT = [Buf('cT%d' % i) for i in range(8)]
        B_hT = [Buf('hT%d' % i) for i in range(8)]
        B_hc = [Buf('hc%d' % i) for i in range(8)]
        B_hid = [[Buf('hid%d_%d' % (i, k)) for k in range(8)] for i in range(2)]
        R_wr = Ring(NWR, 'wr')
        R_stg = Ring(2, 'stg')
        B_pstg = Buf('pstg')
        B_pT = Buf('pT')
        R_tf = Ring(NT, 'tmpf')
        R_tb = Ring(NTB, 'tmpb')
        R_dg = Ring(NDG, 'dg')
        R_ps = Ring(6, 'psr')
        R_ps.bufs = psbank[0:6]
        BM, BS = 6, 7

        def _issue_w(spec):
            w_ap, r0, nk, c0s, ncol = spec[:5]
            if not isinstance(c0s, tuple):
                c0s = (c0s,)
            tot = ncol * len(c0s)
            i, b = R_wr.next()
            view = wr[i][:, 0:nk * tot].rearrange("p (a b) -> p a b", a=nk)
            for ci, c0 in enumerate(c0s):
                P.dma('pool', I('dma_start', out=view[:, :, ci * ncol:(ci + 1) * ncol],
                                in_=w_ap[r0:r0 + nk * 128, c0:c0 + ncol].rearrange("(k p) c -> p k c", p=128)),
                      'D_wr%d' % i, writes=[b])
            return view, b

        WS = {'specs': [], 'issued': [], 'k': 0}
        LOOK = NWR - 1

        def _ahead(n):
            while len(WS['issued']) < min(len(WS['specs']), WS['k'] + n):
                WS['issued'].append(_issue_w(WS['specs'][len(WS['issued'])]))

        def load_w(w_ap, r0, nk, c0, ncol):
            spec = WS['specs'][WS['k']]
            assert spec[1:] == (r0, nk, c0, ncol), (spec[1:], (r0, nk, c0, ncol))
            _ahead(1)
            r = WS['issued'][WS['k']]
            WS['k'] += 1
            _ahead(LOOK)
            return r

        for g_ in range(min(NG, NG_DBG)):
            for j_ in range(8):
                WS['specs'].append((w_in, 0, 16, (1024 + j_ * 128, 2048 + j_ * 128), 128))
            for u_ in range(8):
                WS['specs'].append((w_out, 0, 16, u_ * 256, 256))
            for q_ in range(9):
                if q_ < 8:
                    for u_ in range(4):
                        WS['specs'].append((w_ff1, 0, 16, (q_ * 8 + u_ * 2) * 128, 256))
                if q_ > 0:
                    for u_ in range(4):
                        WS['specs'].append((w_ff2, (q_ - 1) * 1024, 8, u_ * 512, 512))
            for u_ in range(8):
                WS['specs'].append((w_gate, 0, 16, u_ * 256, 256))

        pend_stats = []

        def flush_stats(keep=0):
            while len(pend_stats) > keep:
                fns, rd = pend_stats.pop(0)
                P.pe_group(fns, reads=rd, writes=[psbank[BM], psbank[BS]])

        def ln_fm_stats(r_ap, rbuf, c, nch, ones, defer=2):
            i1, b1 = R_tb.next()
            i2, b2 = R_tb.next()
            P.op('act', I('activation', out=tmpb[i1][:], in_=r_ap, func=AF.Copy), reads=[rbuf], writes=[b1])
            P.op('act', I('activation', out=tmpb[i2][:], in_=r_ap, func=AF.Square), reads=[rbuf], writes=[b2])
            pend_stats.append(([I('matmul', ps[:, BM, :], ones[:], tmpb[i1][:], start=(c == 0), stop=(c == nch - 1)),
                                I('matmul', ps[:, BS, :], ones[:], tmpb[i2][:], start=(c == 0), stop=(c == nch - 1))],
                               [b1, b2, B_const]))
            flush_stats(defer)

        def ln_fm_finish():
            flush_stats(0)
            i, b = R_tf.next()
            tv = tmpf[i][:, 0:512]
            P.op('act', I('activation', out=mstat[:], in_=ps[:, BM, :], func=AF.Copy), reads=[psbank[BM]], writes=[B_m])
            P.op('dve', I('tensor_tensor', tv, mstat[:], mstat[:], ALU.mult), reads=[B_m], writes=[b])
            P.op('dve', I('tensor_tensor', tv, ps[:, BS, :], tv, ALU.subtract), reads=[psbank[BS], b], writes=[b])
            P.op('act', I('activation', out=rstat[:], in_=tv, func=AF.Sqrt, bias=LN_EPS, scale=1.0), reads=[b], writes=[B_m])
            P.op('dve', I('reciprocal', rstat[:], rstat[:]), reads=[B_m], writes=[B_m])

        def ln_fm_center(r_ap, rbuf):
            i, b = R_tf.next()
            tv = tmpf[i][:, 0:512]
            P.op('dve', I('tensor_tensor', tv, r_ap, mstat[:], ALU.subtract), reads=[rbuf, B_m], writes=[b])
            P.op('dve', I('tensor_tensor', tv, tv, rstat[:], ALU.mult), reads=[B_m, b], writes=[b])
            return tv, b

        if dbg_dump('B0', YT[:].rearrange("p a b -> p (a b)"), BF16, B_YT + [B_wple]):
            return finish(nc, P, es, sems)
        for g in range(min(NG, NG_DBG)):
            t0 = 4 * g if g < 4 else 16 + 4 * (g - 4)
            j0 = t0 if g < 4 else 32 + (t0 - 16)
            b1 = {}

            def B1a(tt):
                si, sbuf_ = R_stg.next()
                rows = 128 if tt < 4 else 32
                if tt < 4:
                    P.dma('sp', I('dma_start', out=stg[si][:], in_=xsrc[j0 + tt]), 'D_stg%d' % si, writes=[sbuf_])
                else:
                    P.dma('sp', I('dma_start', out=stg[si][0:32, :], in_=xhalo[g]), 'D_stg%d' % si, writes=[sbuf_])
                xt = stg[si][0:rows, :]
                ln_tile_stats(xt, sbuf_, rows, tt % 2)
                b1[tt] = (xt, sbuf_, rows)

            for tt6 in range(NTT_DBG + 1):
                if tt6 < NTT_DBG:
                    B1a(tt6)
                if tt6 == 0:
                    continue
                tt = tt6 - 1
                xt, sbuf_, rows = b1[tt]
                banks = [R_ps.next()[0] for _ in range(4)]
                if tt < 4:
                    def evac(dk, psap, bb, tt=tt):
                        P.op('dve', I('tensor_scalar', xres[:, dk, tt * 128:(tt + 1) * 128], psap, pcol('aeg', dk), pcol('aeb', dk), ALU.mult, ALU.add),
                             reads=[bb, B_const], writes=[B_xres[dk]])
                        P.op('act', I('activation', out=xbf[:, dk, tt * 128:(tt + 1) * 128], in_=xres[:, dk, tt * 128:(tt + 1) * 128], func=AF.Copy, scale=1.0 / ALPHA),
                             reads=[B_xres[dk]], writes=[B_xbf[dk]])
                else:
                    def evac(dk, psap, bb):
                        P.op('act', I('activation', out=xbfh[:, dk, :], in_=psap, func=AF.Identity, bias=pcol('emb_b', dk), scale=pcol('emb_g', dk)),
                             reads=[bb, B_const], writes=[B_xbfh])
                transpose_tile(xt, sbuf_, rows, banks, evac)

            if g == 0 and dbg_dump('B1', xres[:].rearrange("p a b -> p (a b)"), F32, B_xres + B_xbf + [B_xbfh]):
                return finish(nc, P, es, sems)

            def conv_diags(j, k0, k1):
                r = []
                for k in range(k0, k1):
                    di, db = R_dg.next()
                    P.op('dve', I('tensor_scalar', dg[di][:], ident_f[:], pcol('wdw', j * 31 + k), None, ALU.mult),
                         reads=[B_const], writes=[db])
                    r.append((di, db))
                return r

            def conv_chunk(j, pre):
                bi, bb = R_ps.next()
                dl = list(pre)
                for k in range(31):
                    if k >= len(dl):
                        dl += conv_diags(j, k, k + 1)
                    di, db = dl[k]
                    P.pe_group([I('matmul', ps[:, bi, :], dg[di][:], hT[:, j, k:k + 512], start=(k == 0), stop=(k == 30))],
                               reads=[db, B_hT[j]], writes=[bb])
                P.op('act', I('activation', out=cT[:, j, :], in_=ps[:, bi, :], func=AF.Identity, bias=pcol('bdw', j), scale=1.0),
                     reads=[bb, B_const], writes=[B_cT[j]])
                ln_fm_stats(cT[:, j, :], B_cT[j], j, 8, onesC)

            for j in range(9):
                pre = conv_diags(j - 1, 0, NDG) if j > 0 else []
                if j < 8:
                    wvg, wb = load_w(w_in, 0, 16, (1024 + j * 128, 2048 + j * 128), 128)
                    wv, wg, wgb = wvg[:, :, 0:128], wvg[:, :, 128:256], wb
                    bv, bvb = R_ps.next()
                    bg, bgb = R_ps.next()
                    bh, bhb = R_ps.next()
                    fns = []
                    for dk in range(16):
                        fns.append(I('matmul', ps[:, bv, :], wv[:, dk, :], xbf[:, dk, :], start=(dk == 0), stop=(dk == 15)))
                    for dk in range(16):
                        fns.append(I('matmul', ps[:, bg, :], wg[:, dk, :], xbf[:, dk, :], start=(dk == 0), stop=(dk == 15)))
                    for dk in range(16):
                        fns.append(I('matmul', ps[:, bh, 0:32], wv[:, dk, :], xbfh[:, dk, :], start=(dk == 0), stop=(dk == 15)))
                    for dk in range(16):
                        fns.append(I('matmul', ps[:, bh, 32:64], wg[:, dk, :], xbfh[:, dk, :], start=(dk == 0), stop=(dk == 15)))
                    P.pe_group(fns, reads=[wb, B_xbfh] + B_xbf, writes=[bvb, bgb, bhb])
                    ti, tb = R_tf.next()
                    tf = tmpf[ti]
                    P.op('act', I('activation', out=tf[:, 0:512], in_=ps[:, bg, :], func=AF.Sigmoid), reads=[bgb], writes=[tb])
                    P.op('act', I('activation', out=tf[:, 512:544], in_=ps[:, bh, 32:64], func=AF.Sigmoid), reads=[bhb], writes=[tb])
                    P.op('dve', I('tensor_tensor', hT[:, j, 15:527], ps[:, bv, :], tf[:, 0:512], ALU.mult), reads=[bvb, tb], writes=[B_hT[j]])
                    P.op('dve', I('tensor_tensor', tf[:, 512:544], tf[:, 512:544], hmask[:, g, :], ALU.mult), reads=[tb, B_const], writes=[tb])
                    P.op('dve', I('tensor_tensor', hT[:, j, 0:15], ps[:, bh, 0:15], tf[:, 512:527], ALU.mult), reads=[bhb, tb], writes=[B_hT[j]])
                    P.op('dve', I('tensor_tensor', hT[:, j, 527:542], ps[:, bh, 15:30], tf[:, 527:542], ALU.mult), reads=[bhb, tb], writes=[B_hT[j]])
                if j > 0:
                    conv_chunk(j - 1, pre)

            if g == 0 and dbg_dump('B3', cT[:].rearrange("p a b -> p (a b)"), F32, B_cT + B_hT):
                return finish(nc, P, es, sems)

            ln_fm_finish()
            for j in range(8):
                tap, tbuf = ln_fm_center(cT[:, j, :], B_cT[j])
                P.op('act', I('activation', out=headsc[:, j, :], in_=tap, func=AF.Silu, bias=pcol('convb', j), scale=pcol('convg', j)),
                     reads=[tbuf, B_const] + B_hT, writes=[B_hc[j]])

            if g == 0 and dbg_dump('B4', headsc[:].rearrange("p a b -> p (a b)"), BF16, B_hc):
                return finish(nc, P, es, sems)
            for u in range(8):
                wv, wb = load_w(w_out, 0, 16, u * 256, 256)
                for m in range(2):
                    d = 2 * u + m
                    bi, bb = R_ps.next()
                    fns = []
                    for ck in range(16):
                        if ck < 8:
                            rhs = YT[:, ck, t0 * 128:t0 * 128 + 512]
                        else:
                            rhs = headsc[:, ck - 8, :]
                        fns.append(I('matmul', ps[:, bi, :], wv[:, ck, m * 128:(m + 1) * 128], rhs, start=(ck == 0), stop=(ck == 15)))
                    P.pe_group(fns, reads=[wb] + B_hc + B_YT[t0:t0 + 4], writes=[bb])
                    P.op('dve', I('tensor_tensor', xres[:, d, :], xres[:, d, :], ps[:, bi, :], ALU.add), reads=[bb, B_xres[d]], writes=[B_xres[d]])
                    ln_fm_stats(xres[:, d, :], B_xres[d], d, 16, onesD)

            if g == 0 and dbg_dump('B5', xres[:].rearrange("p a b -> p (a b)"), F32, B_xres):
                return finish(nc, P, es, sems)
            ln_fm_finish()
            for d in range(16):
                tap, tbuf = ln_fm_center(xres[:, d, :], B_xres[d])
                P.op('act', I('activation', out=xres[:, d, :], in_=tap, func=AF.Identity, bias=pcol('a1b', d), scale=pcol('a1g', d)),
                     reads=[tbuf, B_const], writes=[B_xres[d]])
                P.op('pool', I('tensor_scalar', xbf[:, d, :], tap, pcol('ln1g', d), pcol('ln1b', d), ALU.mult, ALU.add),
                     reads=[tbuf, B_const], writes=[B_xbf[d]])

            if g == 0 and dbg_dump('LN1', xres[:].rearrange("p a b -> p (a b)"), F32, B_xres + B_xbf):
                return finish(nc, P, es, sems)
            def ff1(q):
                hb = q % 2
                for u in range(4):
                    wv, wb = load_w(w_ff1, 0, 16, (q * 8 + u * 2) * 128, 256)
                    for m in range(2):
                        fl = u * 2 + m
                        f = q * 8 + fl
                        bi, bb = R_ps.next()
                        fns = []
                        for dk in range(16):
                            fns.append(I('matmul', ps[:, bi, :], wv[:, dk, m * 128:(m + 1) * 128], xbf[:, dk, :], start=(dk == 0), stop=(dk == 15)))
                        P.pe_group(fns, reads=[wb] + B_xbf, writes=[bb])
                        ti, tb = R_tf.next()
                        tv = tmpf[ti][:, 0:512]
                        P.op('act', I('activation', out=tv, in_=ps[:, bi, :], func=AF.Relu, bias=pcol('bff1', f), scale=1.0),
                             reads=[bb, B_const], writes=[tb])
                        P.op('dve', I('tensor_tensor', hid[hb][:, fl, :], tv, tv, ALU.mult), reads=[tb], writes=[B_hid[hb][fl]])

            def ff2(q):
                hb = q % 2
                for u in range(4):
                    wv, wb = load_w(w_ff2, q * 1024, 8, u * 512, 512)
                    for m in range(4):
                        d = u * 4 + m
                        bi, bb = R_ps.next()
                        fns = []
                        for fk in range(8):
                            fns.append(I('matmul', ps[:, bi, :], wv[:, fk, m * 128:(m + 1) * 128], hid[hb][:, fk, :], start=(fk == 0), stop=(fk == 7)))
                        P.pe_group(fns, reads=[wb] + B_hid[hb], writes=[bb])
                        P.op('dve', I('tensor_tensor', xres[:, d, :], xres[:, d, :], ps[:, bi, :], ALU.add), reads=[bb, B_xres[d]], writes=[B_xres[d]])
                        if q == 7:
                            ln_fm_stats(xres[:, d, :], B_xres[d], d, 16, onesD)

            for q in range(9):
                if q < 8:
                    ff1(q)
                if q > 0:
                    ff2(q - 1)
            if g == 0 and dbg_dump('FF', xres[:].rearrange("p a b -> p (a b)"), F32, B_xres):
                return finish(nc, P, es, sems)
            ln_fm_finish()
            for d in range(16):
                tap, tbuf = ln_fm_center(xres[:, d, :], B_xres[d])
                P.op('act', I('activation', out=xres[:, d, :], in_=tap, func=AF.Identity, bias=pcol('ln2b', d), scale=pcol('ln2g', d)),
                     reads=[tbuf, B_const], writes=[B_xres[d]])
                P.op('pool', I('tensor_scalar', xbf[:, d, :], tap, pcol('ln2g', d), pcol('ln2b', d), ALU.mult, ALU.add),
                     reads=[tbuf, B_const], writes=[B_xbf[d]])
            if g == 0 and dbg_dump('LN2', xres[:].rearrange("p a b -> p (a b)"), F32, B_xres + B_xbf):
                return finish(nc, P, es, sems)
            P.dma('sp', I('dma_start', out=pstg[:], in_=pown[g]), 'D_pstg', writes=[B_pstg])
            for tt in range(4):
                bi, bb = R_ps.next()
                P.pe_group([I('transpose', ps[:, bi, kc * 128:(kc + 1) * 128], pstg[:, tt, kc * 128:(kc + 1) * 128], ident_f[:]) for kc in range(2)],
                           reads=[B_pstg, B_const], writes=[bb])
                P.op('act', I('activation', out=pT[:, :, tt * 128:(tt + 1) * 128], in_=ps[:, bi, 0:256].rearrange("p (a b) -> p a b", a=2), func=AF.Copy),
                     reads=[bb], writes=[B_pT])
            for u in range(8):
                wv, wb = load_w(w_gate, 0, 16, u * 256, 256)
                for m in range(2):
                    d = 2 * u + m
                    bi, bb = R_ps.next()
                    be, beb = R_ps.next()
                    fns = []
                    for dk in range(16):
                        fns.append(I('matmul', ps[:, bi, :], wv[:, dk, m * 128:(m + 1) * 128], xbf[:, dk, :], start=(dk == 0), stop=(dk == 15)))
                    for kc in range(2):
                        fns.append(I('matmul', ps[:, be, :], wple_b[:, kc, d * 128:(d + 1) * 128], pT[:, kc, :], start=(kc == 0), stop=(kc == 1)))
                    P.pe_group(fns, reads=[wb, B_wple, B_pT] + B_xbf, writes=[bb, beb])
                    ti, tb = R_tf.next()
                    tv = tmpf[ti][:, 0:512]
                    P.op('act', I('activation', out=tv, in_=ps[:, bi, :], func=AF.Sigmoid, bias=pcol('bgate', d), scale=1.0),
                         reads=[bb, B_const], writes=[tb])
                    P.op('dve', I('tensor_tensor', tv, tv, ps[:, be, :], ALU.mult), reads=[beb, tb], writes=[tb])
                    P.op('dve', I('tensor_tensor', xres[:, d, :], xres[:, d, :], tv, ALU.add), reads=[tb, B_xres[d]], writes=[B_xres[d]])
                    ln_fm_stats(xres[:, d, :], B_xres[d], d, 16, onesD)
            if g == 0 and dbg_dump('G', xres[:].rearrange("p a b -> p (a b)"), F32, B_xres):
                return finish(nc, P, es, sems)
            ln_fm_finish()
            for d in range(16):
                tap, tbuf = ln_fm_center(xres[:, d, :], B_xres[d])
                P.op('act', I('activation', out=xres[:, d, :], in_=tap, func=AF.Identity, bias=pcol('ln3b', d), scale=pcol('ln3g', d)),
                     reads=[tbuf, B_const], writes=[B_xres[d]])
            if g == 0 and dbg_dump('LN3', xres[:].rearrange("p a b -> p (a b)"), F32, B_xres):
                return finish(nc, P, es, sems)
            for tt in range(4):
                si, sbuf_ = R_stg.next()
                for q4 in range(4):
                    bi, bb = R_ps.next()
                    P.pe_group([I('transpose', ps[:, bi, i * 128:(i + 1) * 128], xres[:, q4 * 4 + i, tt * 128:(tt + 1) * 128], ident_f[:]) for i in range(4)],
                               reads=[B_const] + B_xres[q4 * 4:q4 * 4 + 4], writes=[bb])
                    if q4 % 2 == 0:
                        P.op('act', I('activation', out=stg[si][:, q4 * 512:(q4 + 1) * 512], in_=ps[:, bi, :], func=AF.Copy), reads=[bb], writes=[sbuf_])
                    else:
                        P.op('dve', I('tensor_copy', stg[si][:, q4 * 512:(q4 + 1) * 512], ps[:, bi, :]), reads=[bb], writes=[sbuf_])
                P.dma('sp', I('dma_start', out=yown[t0 + tt], in_=stg[si][:]), 'D_stg%d' % si, reads=[sbuf_])

        return finish(nc, P, es, sems)


def finish(nc, P, es, sems):
    deps = {n: v for n, v in P.tick.items() if v > 0}
    P._wait('sp', deps)
    for name in sorted(P.semnames):
        sems[name] = es.enter_context(nc.semaphore(name))
    block = es.enter_context(nc.Block())

    def runner(items):
        def run(e):
            for it in items:
                if it[0] == 'w':
                    e.wait_ge(sems[it[1]], it[2])
                else:
                    nm, a, kw = it[1]
                    ins = getattr(e, nm)(*a, **kw)
                    if it[2] is not None:
                        ins.then_inc(sems[it[2]], it[3])
        return run
    block.sync(runner(P.q['sp']))
    block.scalar(runner(P.q['act']))
    block.vector(runner(P.q['dve']))
    block.gpsimd(runner(P.q['pool']))
    block.tensor(runner(P.q['pe']))
    return nc


def _src_tiles(h):
    o = [('p', 16 * h + j) for j in range(16)] + [('p', 16 * (1 - h) + j) for j in range(16)]
    o += [('s', 8 * h + j) for j in range(8)] + [('s', 8 * (1 - h) + j) for j in range(8)]
    return o


_CONST_CACHE = {}


def _dft_tables(h):
    if h in _CONST_CACHE:
        return _CONST_CACHE[h]
    order = _src_tiles(h)
    out = []
    for (S, nown_t, src_list, own_base) in ((4096, 16, [t for (q, t) in order if q == 'p'], 16 * h),
                                            (2048, 8, [t for (q, t) in order if q == 's'], 8 * h)):
        k = np.arange(S, dtype=np.float64)
        ctab = (np.cos(2 * np.pi * k / S) / np.sqrt(S)).astype(np.float32)
        stab = (np.sin(2 * np.pi * k / S) / np.sqrt(S)).astype(np.float32)
        nsrc = len(src_list)
        pos_src = (np.array(src_list, dtype=np.int64)[None, :] * 128 + np.arange(128, dtype=np.int64)[:, None])
        cm = np.empty((nown_t, 128, 2, nsrc, 128), dtype=np.float32)
        for t in range(nown_t):
            pos_own = (own_base + t) * 128 + np.arange(128, dtype=np.int64)
            idx = (pos_src[:, :, None] * pos_own[None, None, :]) % S
            cm[t, :, 0] = ctab[idx]
            cm[t, :, 1] = stab[idx]
        out.append(cm.reshape(nown_t, 128, 2 * nsrc * 128))
    _CONST_CACHE[h] = out
    return out


def _chan_table():
    d = np.arange(256, dtype=np.int64)
    idx = (d[:, None] * d[None, :]) % 256
    k = np.arange(256, dtype=np.float64)
    c = (np.cos(2 * np.pi * k / 256) / 16.0).astype(np.float32)[idx]
    s = (-np.sin(2 * np.pi * k / 256) / 16.0).astype(np.float32)[idx]
    t = np.stack([c, s], axis=1)
    t = t.reshape(2, 128, 2, 256).transpose(1, 0, 2, 3)
    return np.ascontiguousarray(t.reshape(128, 2 * 2 * 256))


def _cols(v, n):
    return np.ascontiguousarray(np.asarray(v, dtype=np.float32).reshape(n, 128).T)


def _prep(inputs):
    xp = np.asarray(inputs['x_prompt'], dtype=np.float32)
    xs = np.asarray(inputs['x_sample'], dtype=np.float32)
    pp_ = np.asarray(inputs['p_prompt'], dtype=np.float32)[0]
    ps_ = np.asarray(inputs['p_sample'], dtype=np.float32)[0]
    g = lambda n: np.asarray(inputs[n], dtype=np.float32)
    wdw = g('w_dw')[0]
    wdw_cols = np.ascontiguousarray(wdw.reshape(31, 8, 128).transpose(2, 1, 0).reshape(128, 248))
    pp = np.concatenate([
        _cols(g('emb_ln_g'), 16), _cols(g('emb_ln_b'), 16), _cols(g('conv_ln_g')[0], 8), _cols(g('conv_ln_b')[0], 8),
        _cols(g('b_dw')[0], 8), wdw_cols, _cols(g('ln1_g')[0], 16), _cols(g('ln1_b')[0], 16), _cols(g('b_ff1')[0], 64),
        _cols(g('b_ff2')[0], 16), _cols(g('ln2_g')[0], 16), _cols(g('ln2_b')[0], 16), _cols(g('b_gate')[0], 16),
        _cols(g('ln3_g')[0], 16), _cols(g('ln3_b')[0], 16)], axis=1)
    assert pp.shape == (128, NP_IN)
    shared = dict(w_in=np.ascontiguousarray(g('w_in')[0]), w_out=np.ascontiguousarray(g('w_out')[0]),
                  w_ff1=np.ascontiguousarray(g('w_ff1')[0]), w_ff2=np.ascontiguousarray(g('w_ff2')[0]),
                  w_gate=np.ascontiguousarray(g('w_gate')[0]), w_ple=np.ascontiguousarray(g('w_ple')[0]),
                  pp=np.ascontiguousarray(pp), cdt=_chan_table(), ident=np.eye(128, dtype=np.float32))
    in_maps = []
    for c in range(8):
        b, h = c // 2, c % 2
        order = _src_tiles(h)
        seqs = {'p': xp[b], 's': xs[b]}
        xsrc = np.stack([seqs[q][t * 128:(t + 1) * 128] for (q, t) in order], axis=0)
        xhalo = np.zeros((NG, 32, 2048), dtype=np.float32)
        hmask = np.zeros((NG, 32), dtype=np.float32)
        pown = np.empty((NG, 128, 4, 256), dtype=np.float32)
        for gi in range(NG):
            if gi < 4:
                seq, S, g0, pseq = xp[b], 4096, 2048 * h + 512 * gi, pp_[b]
            else:
                seq, S, g0, pseq = xs[b], 2048, 1024 * h + 512 * (gi - 4), ps_[b]
            for r in range(30):
                pos = g0 - 15 + r if r < 15 else g0 + 512 + (r - 15)
                if 0 <= pos < S:
                    xhalo[gi, r] = seq[pos]
                    hmask[gi, r] = 1.0
            pown[gi] = pseq[g0:g0 + 512].reshape(4, 128, 256).transpose(1, 0, 2)
        cmp_, cms_ = _dft_tables(h)
        m = dict(shared)
        m.update(xsrc=np.ascontiguousarray(xsrc), xhalo=xhalo,
                 hmask=np.ascontiguousarray(np.broadcast_to(hmask.reshape(1, NG * 32), (128, NG * 32))),
                 pown=pown, cmat_p=cmp_, cmat_s=cms_)
        in_maps.append(m)
    return in_maps


_NC_CACHE = {}


def kernel(**inputs):
    in_maps = _prep(inputs)
    if 'nc' not in _NC_CACHE:
        _NC_CACHE['nc'] = build_program()
    nc = _NC_CACHE['nc']
    res = run_bass_kernel_spmd(nc, in_maps, core_ids=list(range(8)))
    y_prompt = np.empty((4, 4096, 2048), dtype=np.float32)
    y_sample = np.empty((4, 2048, 2048), dtype=np.float32)
    for c in range(8):
        b, h = c // 2, c % 2
        y = np.asarray(res.results[c]["yown"], dtype=np.float32).reshape(NOWN * 128, 2048)
        y_prompt[b, 2048 * h:2048 * h + 2048] = y[0:2048]
        y_sample[b, 1024 * h:1024 * h + 1024] = y[2048:3072]
    return (y_prompt, y_sample)
```

```python
import numpy as np
from collections import defaultdict
from contextlib import ExitStack
import concourse.bass as bass
import concourse.mybir as mybir
from concourse.bass_utils import run_bass_kernel_spmd

F32 = mybir.dt.float32
BF16 = mybir.dt.bfloat16
U8 = mybir.dt.uint8
AF = mybir.ActivationFunctionType
ALU = mybir.AluOpType

ALPHA = float(2.0 ** 0.25)
LN_EPS = 1e-5
NSRC = 48
NOWN = 24
NG = 6
ENG = ('pe', 'act', 'dve', 'pool', 'sp')
import os
NTT_DBG = int(os.environ.get('NTT_DBG', '5'))
NG_DBG = int(os.environ.get('NG_DBG', '6'))

PCOLS = {}
_off = 0
for _n, _w in (('emb_g', 16), ('emb_b', 16), ('convg', 8), ('convb', 8), ('bdw', 8), ('wdw', 248),
               ('ln1g', 16), ('ln1b', 16), ('bff1', 64), ('bff2', 16), ('ln2g', 16), ('ln2b', 16),
               ('bgate', 16), ('ln3g', 16), ('ln3b', 16),
               ('aeg', 16), ('aeb', 16), ('a1g', 16), ('a1b', 16)):
    PCOLS[_n] = _off
    _off += _w
NP_IN = PCOLS['aeg']
NP_ALL = _off


class Buf:
    __slots__ = ('rd', 'wr', 'name', 'excl')

    def __init__(self, name='', excl=False):
        self.rd = {}
        self.wr = None
        self.name = name
        self.excl = excl


class Prog:
    def __init__(self):
        self.q = {e: [] for e in ENG}
        self.tick = defaultdict(int)
        self.waited = defaultdict(int)
        self.semnames = set('S_' + e for e in ENG)

    def _wait(self, eng, deps):
        for name, v in deps.items():
            if eng == 'pe' and name == 'S_pe':
                continue
            if self.waited[(eng, name)] >= v:
                continue
            self.waited[(eng, name)] = v
            self.q[eng].append(('w', name, v))

    @staticmethod
    def _deps(reads, writes, eng=None):
        d = {}

        def add(n, v):
            if d.get(n, 0) < v:
                d[n] = v
        for b in reads:
            if b.wr is not None:
                add(*b.wr)
            if b.excl:
                for n, v in b.rd.items():
                    if n != 'S_' + str(eng):
                        add(n, v)
        for b in writes:
            if b.wr is not None:
                add(*b.wr)
            for n, v in b.rd.items():
                add(n, v)
        return d

    def _commit(self, tok, reads, writes):
        name, v = tok
        for b in reads:
            if b.rd.get(name, 0) < v:
                b.rd[name] = v
        for b in writes:
            b.wr = tok
            b.rd = {}

    def op(self, eng, fn, reads=(), writes=(), sem=None, inc=1):
        self._wait(eng, self._deps(reads, writes, eng))
        name = sem or ('S_' + eng)
        self.semnames.add(name)
        self.tick[name] += inc
        tok = (name, self.tick[name])
        self.q[eng].append(('o', fn, name, inc))
        self._commit(tok, reads, writes)
        return tok

    def dma(self, eng, fn, sem, reads=(), writes=()):
        return self.op(eng, fn, reads, writes, sem=sem, inc=16)

    def pe_group(self, fns, reads=(), writes=()):
        self._wait('pe', self._deps(reads, writes))
        for fn in fns[:-1]:
            self.q['pe'].append(('o', fn, None, 0))
        self.tick['S_pe'] += 1
        tok = ('S_pe', self.tick['S_pe'])
        self.q['pe'].append(('o', fns[-1], 'S_pe', 1))
        self._commit(tok, reads, writes)
        return tok

    def barrier(self):
        deps = {n: v for n, v in self.tick.items() if v > 0}
        for e in ENG:
            self._wait(e, deps)


class Ring:
    def __init__(self, n, name):
        self.bufs = [Buf('%s%d' % (name, i)) for i in range(n)]
        self.i = 0
        self.n = n

    def next(self):
        i = self.i
        self.i = (self.i + 1) % self.n
        return i, self.bufs[i]


def I(name, *args, **kw):
    return (name, args, kw)


def build_program(debug=None):
    nc = bass.Bass("TRN2", target_bir_lowering=False)
    P = Prog()
    dt_in = lambda name, shape: nc.dram_tensor(name, shape, F32, kind="ExternalInput").ap()
    xsrc = dt_in("xsrc", [NSRC, 128, 2048])
    xhalo = dt_in("xhalo", [NG, 32, 2048])
    hmask_d = dt_in("hmask", [128, NG * 32])
    pown = dt_in("pown", [NG, 128, 4, 256])
    cmat_p = dt_in("cmat_p", [16, 128, 2 * 32 * 128])
    cmat_s = dt_in("cmat_s", [8, 128, 2 * 16 * 128])
    w_in = dt_in("w_in", [2048, 3072])
    w_out = dt_in("w_out", [2048, 2048])
    w_ff1 = dt_in("w_ff1", [2048, 8192])
    w_ff2 = dt_in("w_ff2", [8192, 2048])
    w_gate = dt_in("w_gate", [2048, 2048])
    w_ple = dt_in("w_ple", [256, 2048])
    pp_d = dt_in("pp", [128, NP_IN])
    cdt_d = dt_in("cdt", [128, 2 * 2 * 256])
    ident_d = dt_in("ident", [128, 128])
    yown = nc.dram_tensor("yown", [NOWN, 128, 2048], F32, kind="ExternalOutput").ap()
    dbg = None
    if debug == 'U':
        dbg = nc.dram_tensor("dbg", [128, NSRC * 1024], BF16, kind="ExternalOutput").ap()
    elif debug == 'YT':
        dbg = nc.dram_tensor("dbg", [128, 8 * 3072], BF16, kind="ExternalOutput").ap()

    es = ExitStack()
    with es:
        SLAB = 211968
        slab = es.enter_context(nc.sbuf_tensor("slab", [128, SLAB], U8))
        base = nc.lookup_mloc(slab).addr

        def sb(name, shape, dtype, off):
            nb = int(np.prod(shape[1:])) * (4 if dtype == F32 else 2)
            assert off % 32 == 0, (name, off)
            assert off + nb <= SLAB, (name, off, nb)
            return nc.alloc_sbuf_tensor_at(name, shape, dtype, offset=base + off), off + ((nb + 31) // 32) * 32

        o = 0
        pp, o = sb("pp", [128, NP_ALL], F32, o)
        ident_f, o = sb("ident_f", [128, 128], F32, o)
        ident_b, o = sb("ident_b", [128, 128], BF16, o)
        onesD, o = sb("onesD", [128, 128], BF16, o)
        onesC, o = sb("onesC", [128, 128], BF16, o)
        cdt_b, o = sb("cdt_b", [128, 2, 2, 256], BF16, o)
        hmask, o = sb("hmask", [128, NG, 32], F32, o)
        small, o = sb("small", [128, 64], F32, o)
        assert o <= 8192, o
        O_YT = 8192
        YT, _ = sb("YT", [128, 8, 3072], BF16, O_YT)
        O_U = O_YT + 49152
        O_X = O_U + 98304

        ps = es.enter_context(nc.psum_tensor("ps", [128, 8, 512], F32))
        sems = {}

        def pcol(name, j=0):
            c = PCOLS[name] + j
            return pp[:, c:c + 1]

        def pcols(name, n=16):
            c = PCOLS[name]
            return pp[:, c:c + n]

        B_const = Buf('const')
        B_YT = [Buf('YT%d' % t) for t in range(NOWN)]

        def dbg_dump(stage, ap2d, dtype, bufs):
            if debug != stage:
                return False
            d_ = nc.dram_tensor("dbg", list(ap2d.shape), dtype, kind="ExternalOutput").ap()
            P.dma('sp', I('dma_start', out=d_, in_=ap2d), 'D_out', reads=bufs)
            return True
        B_small = [Buf('small0'), Buf('small1')]

        P.dma('sp', I('dma_start', out=pp[:, 0:NP_IN], in_=pp_d), 'D_init', writes=[B_const])
        P.dma('sp', I('dma_start', out=ident_f[:], in_=ident_d), 'D_init', writes=[B_const])
        P.dma('sp', I('dma_start', out=hmask[:].rearrange("p a b -> p (a b)"), in_=hmask_d), 'D_init', writes=[B_const])
        P.dma('pool', I('dma_start', out=cdt_b[:].rearrange("p a b c -> p (a b c)"), in_=cdt_d), 'D_init2', writes=[B_const])
        P.op('dve', I('tensor_copy', ident_b[:], ident_f[:]), reads=[B_const], writes=[B_const])
        P.op('dve', I('memset', onesD[:], 1.0 / 2048), writes=[B_const])
        P.op('dve', I('memset', onesC[:], 1.0 / 1024), writes=[B_const])
        P.op('dve', I('tensor_scalar', pcols('aeg'), pcols('emb_g'), ALPHA, None, ALU.mult), reads=[B_const], writes=[B_const])
        P.op('dve', I('tensor_scalar', pcols('aeb'), pcols('emb_b'), ALPHA, None, ALU.mult), reads=[B_const], writes=[B_const])
        P.op('dve', I('tensor_scalar', pcols('a1g'), pcols('ln1g'), ALPHA, None, ALU.mult), reads=[B_const], writes=[B_const])
        P.op('dve', I('scalar_tensor_tensor', pcols('a1b'), pcols('ln1b'), ALPHA, pcols('bff2'), ALU.mult, ALU.add), reads=[B_const], writes=[B_const])

        psbank = [Buf('psb%d' % i, excl=True) for i in range(8)]

        def ln_tile_stats(xt, xbuf, rows, si):
            scol = 32 * si
            bs = B_small[si]
            st = small[0:rows, scol:scol + 24]
            mv = small[0:rows, scol + 24:scol + 26]
            rstd = small[0:rows, scol + 26:scol + 27]
            nmr = small[0:rows, scol + 27:scol + 28]
            for c4 in range(4):
                P.op('dve', I('bn_stats', st[:, c4 * 6:(c4 + 1) * 6], xt[:, c4 * 512:(c4 + 1) * 512]), reads=[xbuf], writes=[bs])
            P.op('dve', I('bn_aggr', mv, st.rearrange("p (a b) -> p a b", a=4)), reads=[bs], writes=[bs])
            P.op('act', I('activation', out=rstd, in_=mv[:, 1:2], func=AF.Sqrt, bias=LN_EPS, scale=1.0), reads=[bs], writes=[bs])
            P.op('dve', I('reciprocal', rstd, rstd), reads=[bs], writes=[bs])
            P.op('dve', I('scalar_tensor_tensor', nmr, mv[:, 0:1], -1.0, rstd, ALU.mult, ALU.mult), reads=[bs], writes=[bs])
            P.op('act', I('activation', out=xt, in_=xt, func=AF.Identity, bias=nmr, scale=rstd), reads=[bs, xbuf], writes=[xbuf])

        def transpose_tile(xt, xbuf, rows, banks, evac):
            for q4 in range(4):
                bi = banks[q4]
                fns = []
                for i in range(4):
                    dk = q4 * 4 + i
                    fns.append(I('transpose', ps[:, bi, i * 128:i * 128 + rows], xt[:, dk * 128:(dk + 1) * 128], ident_f[0:rows, 0:rows]))
                P.pe_group(fns, reads=[xbuf, B_const], writes=[psbank[bi]])
                for i in range(4):
                    dk = q4 * 4 + i
                    evac(dk, ps[:, bi, i * 128:i * 128 + rows], psbank[bi])

        U, _ = sb("U", [128, NSRC, 1024], BF16, O_U)
        B_U = [Buf('U%d' % j) for j in range(NSRC)]
        Wf, _ = sb("Wf", [128, 16, 1024], BF16, O_YT)
        stgA = [sb("stgA%d" % i, [128, 2048], F32, O_YT + 32768 + i * 8192)[0] for i in range(2)]
        B_stgA = [Buf('stgA%d' % i) for i in range(2)]
        xnT = [sb("xnT%d" % i, [128, 16, 128], BF16, O_X + i * 4096)[0] for i in range(2)]
        B_xnTc = [[Buf('xnT%d_%d' % (i, k)) for k in range(16)] for i in range(2)]
        B_Wf = Buf('Wf')
        for q4 in range(4):
            P.dma('pool', I('dma_start', out=Wf[:, q4 * 4:(q4 + 1) * 4, :],
                            in_=w_in[q4 * 512:(q4 + 1) * 512, 0:1024].rearrange("(dk p) c -> p dk c", p=128)),
                  'D_wf', writes=[B_Wf])

        def A_stage1a(j):
            s = j % 2
            P.dma('sp', I('dma_start', out=stgA[s][:], in_=xsrc[j]), 'D_stg%d' % s, writes=[B_stgA[s]])
            ln_tile_stats(stgA[s][:], B_stgA[s], 128, s)

        def A_stage1b(j):
            s = j % 2

            def evac(dk, psap, bb):
                if dk < 8:
                    P.op('dve', I('tensor_scalar', xnT[s][:, dk, :], psap, pcol('emb_g', dk), pcol('emb_b', dk), ALU.mult, ALU.add),
                         reads=[bb, B_const], writes=[B_xnTc[s][dk]])
                else:
                    P.op('act', I('activation', out=xnT[s][:, dk, :], in_=psap, func=AF.Identity, bias=pcol('emb_b', dk), scale=pcol('emb_g', dk)),
                         reads=[bb, B_const], writes=[B_xnTc[s][dk]])
            transpose_tile(stgA[s][:], B_stgA[s], 128, [0, 1, 2, 3], evac)

        def A_stage2(j):
            s = j % 2
            bk = [4 + 2 * s, 5 + 2 * s]
            fns = []
            for dk in range(16):
                for cb in range(2):
                    fns.append(I('matmul', ps[:, bk[cb], :], xnT[s][:, dk, :], Wf[:, dk, cb * 512:(cb + 1) * 512],
                                 start=(dk == 0), stop=(dk == 15)))
            P.pe_group(fns, reads=B_xnTc[s] + [B_Wf], writes=[psbank[bk[0]], psbank[bk[1]]])
            for cb in range(2):
                P.op('act', I('activation', out=U[:, j, cb * 512:(cb + 1) * 512], in_=ps[:, bk[cb], :], func=AF.Copy),
                     reads=[psbank[bk[cb]]], writes=[B_U[j]])

        for j in range(NSRC + 2):
            if j < NSRC:
                A_stage1a(j)
            if 1 <= j <= NSRC:
                A_stage1b(j - 1)
            if j >= 2:
                A_stage2(j - 2)

        if debug == 'U':
            P.dma('sp', I('dma_start', out=dbg, in_=U[:].rearrange("p a b -> p (a b)")), 'D_out', reads=B_U)
            return finish(nc, P, es, sems)

        P.barrier()
        cm = [sb("cm%d" % i, [128, 2, 32, 128], BF16, O_X + i * 16384)[0] for i in range(2)]
        B_cm = [Buf('cm%d' % i) for i in range(2)]
        o2 = O_X + 32768
        PQb = []
        for i in range(2):
            t_, o2 = sb("PQb%d" % i, [128, 2, 512], BF16, o2)
            PQb.append(t_)
        B_PQb = [Buf() for _ in range(2)]
        PQT = []
        for i in range(2):
            t_, o2 = sb("PQT%d" % i, [128, 8, 128], BF16, o2)
            PQT.append(t_)
        B_PQT = [Buf() for _ in range(2)]
        rounds = []
        cm_slot = {}

        def Ap_PQ(t, cb, r):
            prompt = t < 16
            nsrc = 32 if prompt else 16
            j0 = 0 if prompt else 32
            s = t % 2
            if cb == 0:
                if prompt:
                    P.dma('pool', I('dma_start', out=cm[s][:].rearrange("p a b c -> p (a b c)"), in_=cmat_p[t]),
                          'D_cm%d' % s, writes=[B_cm[s]])
                else:
                    P.dma('pool', I('dma_start', out=cm[s][:, :, 0:16, :], in_=cmat_s[t - 16].rearrange("p (a b c) -> p a b c", a=2, b=16)),
                          'D_cm%d' % s, writes=[B_cm[s]])
            bP, bQ = 2 * r, 2 * r + 1
            fns = []
            for jj in range(nsrc):
                fns.append(I('matmul', ps[:, bP, :], cm[s][:, 0, jj, :], U[:, j0 + jj, cb * 512:(cb + 1) * 512],
                             start=(jj == 0), stop=(jj == nsrc - 1)))
                fns.append(I('matmul', ps[:, bQ, :], cm[s][:, 1, jj, :], U[:, j0 + jj, cb * 512:(cb + 1) * 512],
                             start=(jj == 0), stop=(jj == nsrc - 1)))
            P.pe_group(fns, reads=[B_cm[s]], writes=[psbank[bP], psbank[bQ]])
            P.op('act', I('activation', out=PQb[r][:, 0, :], in_=ps[:, bP, :], func=AF.Copy), reads=[psbank[bP]], writes=[B_PQb[r]])
            P.op('dve', I('tensor_copy', PQb[r][:, 1, :], ps[:, bQ, :]), reads=[psbank[bQ]], writes=[B_PQb[r]])

        def Ap_T(t, cb, r):
            bT = 4 + r
            psT = ps[:, bT, :].bitcast(BF16)
            fns = []
            for pq in range(2):
                for i in range(4):
                    fns.append(I('transpose', psT[:, (pq * 4 + i) * 128:(pq * 4 + i + 1) * 128],
                                 PQb[r][:, pq, i * 128:(i + 1) * 128], ident_b[:]))
            P.pe_group(fns, reads=[B_PQb[r], B_const], writes=[psbank[bT]])
            P.op('act', I('activation', out=PQT[r][:].rearrange("p a b -> p (a b)"), in_=psT, func=AF.Copy),
                 reads=[psbank[bT]], writes=[B_PQT[r]])

        def Ap_Y(t, cb, r):
            bY = 6 + r
            fns = []
            for fgl in range(2):
                for dp in range(2):
                    oc = (fgl * 2 + dp) * 128
                    k = 0
                    for kc in range(2):
                        for cs in range(2):
                            fns.append(I('matmul', ps[:, bY, oc:oc + 128], cdt_b[:, kc, cs, dp * 128:(dp + 1) * 128],
                                         PQT[r][:, cs * 4 + fgl * 2 + kc, :], start=(k == 0), stop=(k == 3)))
                            k += 1
            P.pe_group(fns, reads=[B_PQT[r], B_const], writes=[psbank[bY]])
            P.op('dve', I('tensor_copy', YT[:, cb * 4:(cb + 1) * 4, t * 128:(t + 1) * 128],
                          ps[:, bY, :].rearrange("p (a b) -> p a b", a=4)),
                 reads=[psbank[bY]], writes=[B_YT[t]])

        for t in range(NOWN):
            for cb in range(2):
                rounds.append((t, cb, len(rounds) % 2))
        nr = len(rounds)
        for i in range(nr + 2):
            if i < nr:
                Ap_PQ(*rounds[i])
            if 1 <= i <= nr:
                Ap_T(*rounds[i - 1])
            if i >= 2:
                Ap_Y(*rounds[i - 2])

        if debug == 'YT':
            P.dma('sp', I('dma_start', out=dbg, in_=YT[:].rearrange("p a b -> p (a b)")), 'D_out', reads=B_YT)
            return finish(nc, P, es, sems)

        P.barrier()
        o = O_U
        wple_b, o = sb("wple_b", [128, 2, 2048], BF16, o)
        xres, o = sb("xres", [128, 16, 512], F32, o)
        xbf, o = sb("xbf", [128, 16, 512], BF16, o)
        xbfh, o = sb("xbfh", [128, 16, 32], BF16, o)
        mstat, o = sb("mstat", [128, 512], F32, o)
        rstat, o = sb("rstat", [128, 512], F32, o)
        cT, o_after_cT = sb("cT", [128, 8, 512], F32, o)
        hid = [sb("hid%d" % i, [128, 8, 512], BF16, o + i * 8192)[0] for i in range(2)]
        o = o_after_cT
        hT, o_after_hT = sb("hT", [128, 8, 544], BF16, o)
        headsc, _ = sb("headsc", [128, 8, 512], BF16, o)
        o = o_after_hT
        NWR = 3
        wr = []
        for i in range(NWR):
            t_, o = sb("wr%d" % i, [128, 4096], BF16, o)
            wr.append(t_)
        stg = []
        for i in range(2):
            t_, o = sb("stg%d" % i, [128, 2048], F32, o)
            stg.append(t_)
        pstg, o = sb("pstg", [128, 4, 256], F32, o)
        pT, o = sb("pT", [128, 2, 512], BF16, o)
        NT = 3
        tmpf = []
        for i in range(NT):
            t_, o = sb("tmpf%d" % i, [128, 544], F32, o)
            tmpf.append(t_)
        NTB = 8
        tmpb = []
        for i in range(NTB):
            t_, o = sb("tmpb%d" % i, [128, 512], BF16, o)
            tmpb.append(t_)
        NDG = 16
        dg = []
        for i in range(NDG):
            t_, o = sb("dg%d" % i, [128, 128], BF16, o)
            dg.append(t_)
        print("phase B sbuf end", o, "of", SLAB)

        B_wple = Buf('wple')
        P.dma('pool', I('dma_start', out=wple_b[:], in_=w_ple.rearrange("(kc p) d -> p kc d", p=128)), 'D_wple', writes=[B_wple])
        B_xres = [Buf('xres%d' % i) for i in range(16)]
        B_xbf = [Buf('xbf%d' % i) for i in range(16)]
        B_xbfh = Buf('xbfh')
        B_m = Buf('m')
        B_cT = [Buf('cT%d' % i) for i in range(8)]
        B_hT = [Buf('hT%d' % i) for i in range(8)]
        B_hc = [Buf('hc%d' % i) for i in range(8)]
        B_hid = [[Buf('hid%d_%d' % (i, k)) for k in range(8)] for i in range(2)]
        R_wr = Ring(NWR, 'wr')
        R_stg = Ring(2, 'stg')
        B_pstg = Buf('pstg')
        B_pT = Buf('pT')
        R_tf = Ring(NT, 'tmpf')
        R_tb = Ring(NTB, 'tmpb')
        R_dg = Ring(NDG, 'dg')
        R_ps = Ring(6, 'psr')
        R_ps.bufs = psbank[0:6]
        BM, BS = 6, 7

        def _issue_w(spec):
            w_ap, r0, nk, c0s, ncol = spec[:5]
            if not isinstance(c0s, tuple):
                c0s = (c0s,)
            tot = ncol * len(c0s)
            i, b = R_wr.next()
            view = wr[i][:, 0:nk * tot].rearrange("p (a b) -> p a b", a=nk)
            for ci, c0 in enumerate(c0s):
                P.dma('pool', I('dma_start', out=view[:, :, ci * ncol:(ci + 1) * ncol],
                                in_=w_ap[r0:r0 + nk * 128, c0:c0 + ncol].rearrange("(k p) c -> p k c", p=128)),
                      'D_wr%d' % i, writes=[b])
            return view, b

        WS = {'specs': [], 'issued': [], 'k': 0}
        LOOK = NWR - 1

        def _ahead(n):
            while len(WS['issued']) < min(len(WS['specs']), WS['k'] + n):
                WS['issued'].append(_issue_w(WS['specs'][len(WS['issued'])]))

        def load_w(w_ap, r0, nk, c0, ncol, look=None):
            spec = WS['specs'][WS['k']]
            assert spec[1:] == (r0, nk, c0, ncol), (spec[1:], (r0, nk, c0, ncol))
            _ahead(1)
            r = WS['issued'][WS['k']]
            WS['k'] += 1
            _ahead(LOOK if look is None else look)
            return r

        for g_ in range(min(NG, NG_DBG)):
            for j_ in range(8):
                WS['specs'].append((w_in, 0, 16, (1024 + j_ * 128, 2048 + j_ * 128), 128))
            for u_ in range(8):
                WS['specs'].append((w_out, 0, 16, u_ * 256, 256))
            for q_ in range(9):
                if q_ < 8:
                    for u_ in range(4):
                        WS['specs'].append((w_ff1, 0, 16, (q_ * 8 + u_ * 2) * 128, 256))
                if q_ > 0:
                    for u_ in range(4):
                        WS['specs'].append((w_ff2, (q_ - 1) * 1024, 8, u_ * 512, 512))
            for u_ in range(8):
                WS['specs'].append((w_gate, 0, 16, u_ * 256, 256))

        pend_stats = []

        def flush_stats(keep=0):
            while len(pend_stats) > keep:
                fns, rd = pend_stats.pop(0)
                P.pe_group(fns, reads=rd, writes=[psbank[BM], psbank[BS]])

        def ln_fm_stats(r_ap, rbuf, c, nch, ones, defer=2):
            i1, b1 = R_tb.next()
            i2, b2 = R_tb.next()
            P.op('act', I('activation', out=tmpb[i1][:], in_=r_ap, func=AF.Copy), reads=[rbuf], writes=[b1])
            P.op('act', I('activation', out=tmpb[i2][:], in_=r_ap, func=AF.Square), reads=[rbuf], writes=[b2])
            pend_stats.append(([I('matmul', ps[:, BM, :], ones[:], tmpb[i1][:], start=(c == 0), stop=(c == nch - 1)),
                                I('matmul', ps[:, BS, :], ones[:], tmpb[i2][:], start=(c == 0), stop=(c == nch - 1))],
                               [b1, b2, B_const]))
            flush_stats(defer)

        def ln_fm_finish():
            flush_stats(0)
            i, b = R_tf.next()
            tv = tmpf[i][:, 0:512]
            P.op('act', I('activation', out=mstat[:], in_=ps[:, BM, :], func=AF.Copy), reads=[psbank[BM]], writes=[B_m])
            P.op('dve', I('tensor_tensor', tv, mstat[:], mstat[:], ALU.mult), reads=[B_m], writes=[b])
            P.op('dve', I('tensor_tensor', tv, ps[:, BS, :], tv, ALU.subtract), reads=[psbank[BS], b], writes=[b])
            P.op('act', I('activation', out=rstat[:], in_=tv, func=AF.Sqrt, bias=LN_EPS, scale=1.0), reads=[b], writes=[B_m])
            P.op('dve', I('reciprocal', rstat[:], rstat[:]), reads=[B_m], writes=[B_m])

        def ln_fm_center(r_ap, rbuf):
            i, b = R_tf.next()
            tv = tmpf[i][:, 0:512]
            P.op('dve', I('tensor_tensor', tv, r_ap, mstat[:], ALU.subtract), reads=[rbuf, B_m], writes=[b])
            P.op('dve', I('tensor_tensor', tv, tv, rstat[:], ALU.mult), reads=[B_m, b], writes=[b])
            return tv, b

        if dbg_dump('B0', YT[:].rearrange("p a b -> p (a b)"), BF16, B_YT + [B_wple]):
            return finish(nc, P, es, sems)
        for g in range(min(NG, NG_DBG)):
            t0 = 4 * g if g < 4 else 16 + 4 * (g - 4)
            j0 = t0 if g < 4 else 32 + (t0 - 16)
            b1 = {}

            def B1a(tt):
                si, sbuf_ = R_stg.next()
                rows = 128 if tt < 4 else 32
                if tt < 4:
                    P.dma('sp', I('dma_start', out=stg[si][:], in_=xsrc[j0 + tt]), 'D_stg%d' % si, writes=[sbuf_])
                else:
                    P.dma('sp', I('dma_start', out=stg[si][0:32, :], in_=xhalo[g]), 'D_stg%d' % si, writes=[sbuf_])
                xt = stg[si][0:rows, :]
                ln_tile_stats(xt, sbuf_, rows, tt % 2)
                b1[tt] = (xt, sbuf_, rows)

            for tt6 in range(NTT_DBG + 1):
                if tt6 < NTT_DBG:
                    B1a(tt6)
                if tt6 == 0:
                    continue
                tt = tt6 - 1
                xt, sbuf_, rows = b1[tt]
                banks = [R_ps.next()[0] for _ in range(4)]
                if tt < 4:
                    def evac(dk, psap, bb, tt=tt):
                        P.op('dve', I('tensor_scalar', xres[:, dk, tt * 128:(tt + 1) * 128], psap, pcol('aeg', dk), pcol('aeb', dk), ALU.mult, ALU.add),
                             reads=[bb, B_const], writes=[B_xres[dk]])
                        P.op('act', I('activation', out=xbf[:, dk, tt * 128:(tt + 1) * 128], in_=xres[:, dk, tt * 128:(tt + 1) * 128], func=AF.Copy, scale=1.0 / ALPHA),
                             reads=[B_xres[dk]], writes=[B_xbf[dk]])
                else:
                    def evac(dk, psap, bb):
                        P.op('act', I('activation', out=xbfh[:, dk, :], in_=psap, func=AF.Identity, bias=pcol('emb_b', dk), scale=pcol('emb_g', dk)),
                             reads=[bb, B_const], writes=[B_xbfh])
                transpose_tile(xt, sbuf_, rows, banks, evac)

            if g == 0 and dbg_dump('B1', xres[:].rearrange("p a b -> p (a b)"), F32, B_xres + B_xbf + [B_xbfh]):
                return finish(nc, P, es, sems)

            def conv_diags(j, k0, k1):
                r = []
                for k in range(k0, k1):
                    di, db = R_dg.next()
                    P.op('dve', I('tensor_scalar', dg[di][:], ident_f[:], pcol('wdw', j * 31 + k), None, ALU.mult),
                         reads=[B_const], writes=[db])
                    r.append((di, db))
                return r

            def conv_chunk(j, pre):
                bi, bb = R_ps.next()
                dl = list(pre)
                for k in range(31):
                    if k >= len(dl):
                        dl += conv_diags(j, k, k + 1)
                    di, db = dl[k]
                    P.pe_group([I('matmul', ps[:, bi, :], dg[di][:], hT[:, j, k:k + 512], start=(k == 0), stop=(k == 30))],
                               reads=[db, B_hT[j]], writes=[bb])
                P.op('act', I('activation', out=cT[:, j, :], in_=ps[:, bi, :], func=AF.Identity, bias=pcol('bdw', j), scale=1.0),
                     reads=[bb, B_const], writes=[B_cT[j]])
                ln_fm_stats(cT[:, j, :], B_cT[j], j, 8, onesC)

            for j in range(9):
                pre = conv_diags(j - 1, 0, NDG) if j > 0 else []
                if j < 8:
                    wvg, wb = load_w(w_in, 0, 16, (1024 + j * 128, 2048 + j * 128), 128)
                    wv, wg, wgb = wvg[:, :, 0:128], wvg[:, :, 128:256], wb
                    bv, bvb = R_ps.next()
                    bg, bgb = R_ps.next()
                    bh, bhb = R_ps.next()
                    fns = []
                    for dk in range(16):
                        fns.append(I('matmul', ps[:, bv, :], wv[:, dk, :], xbf[:, dk, :], start=(dk == 0), stop=(dk == 15)))
                    for dk in range(16):
                        fns.append(I('matmul', ps[:, bg, :], wg[:, dk, :], xbf[:, dk, :], start=(dk == 0), stop=(dk == 15)))
                    for dk in range(16):
                        fns.append(I('matmul', ps[:, bh, 0:32], wv[:, dk, :], xbfh[:, dk, :], start=(dk == 0), stop=(dk == 15)))
                    for dk in range(16):
                        fns.append(I('matmul', ps[:, bh, 32:64], wg[:, dk, :], xbfh[:, dk, :], start=(dk == 0), stop=(dk == 15)))
                    P.pe_group(fns, reads=[wb, B_xbfh] + B_xbf, writes=[bvb, bgb, bhb])
                    ti, tb = R_tf.next()
                    tf = tmpf[ti]
                    P.op('act', I('activation', out=tf[:, 0:512], in_=ps[:, bg, :], func=AF.Sigmoid), reads=[bgb], writes=[tb])
                    P.op('act', I('activation', out=tf[:, 512:544], in_=ps[:, bh, 32:64], func=AF.Sigmoid), reads=[bhb], writes=[tb])
                    P.op('dve', I('tensor_tensor', hT[:, j, 15:527], ps[:, bv, :], tf[:, 0:512], ALU.mult), reads=[bvb, tb], writes=[B_hT[j]])
                    P.op('dve', I('tensor_tensor', tf[:, 512:544], tf[:, 512:544], hmask[:, g, :], ALU.mult), reads=[tb, B_const], writes=[tb])
                    P.op('dve', I('tensor_tensor', hT[:, j, 0:15], ps[:, bh, 0:15], tf[:, 512:527], ALU.mult), reads=[bhb, tb], writes=[B_hT[j]])
                    P.op('dve', I('tensor_tensor', hT[:, j, 527:542], ps[:, bh, 15:30], tf[:, 527:542], ALU.mult), reads=[bhb, tb], writes=[B_hT[j]])
                if j > 0:
                    conv_chunk(j - 1, pre)

            if g == 0 and dbg_dump('B3', cT[:].rearrange("p a b -> p (a b)"), F32, B_cT + B_hT):
                return finish(nc, P, es, sems)

            ln_fm_finish()
            for j in range(8):
                tap, tbuf = ln_fm_center(cT[:, j, :], B_cT[j])
                P.op('act', I('activation', out=headsc[:, j, :], in_=tap, func=AF.Silu, bias=pcol('convb', j), scale=pcol('convg', j)),
                     reads=[tbuf, B_const] + B_hT, writes=[B_hc[j]])

            if g == 0 and dbg_dump('B4', headsc[:].rearrange("p a b -> p (a b)"), BF16, B_hc):
                return finish(nc, P, es, sems)
            def wout_evac(d, bi, bb):
                P.op('dve', I('tensor_tensor', xres[:, d, :], xres[:, d, :], ps[:, bi, :], ALU.add), reads=[bb, B_xres[d]], writes=[B_xres[d]])
                ln_fm_stats(xres[:, d, :], B_xres[d], d, 16, onesD)

            head = []
            for u in range(3):
                wv, wb = load_w(w_out, 0, 16, u * 256, 256, look=0)
                for m in range(2):
                    d = 2 * u + m
                    bi, bb = R_ps.next()
                    P.pe_group([I('matmul', ps[:, bi, :], wv[:, ck, m * 128:(m + 1) * 128], YT[:, ck, t0 * 128:t0 * 128 + 512], start=(ck == 0), stop=False)
                                for ck in range(8)], reads=[wb] + B_YT[t0:t0 + 4], writes=[bb])
                    head.append((d, m, wv, wb, bi, bb))
            for (d, m, wv, wb, bi, bb) in head:
                P.pe_group([I('matmul', ps[:, bi, :], wv[:, ck, m * 128:(m + 1) * 128], headsc[:, ck - 8, :], start=False, stop=(ck == 15))
                            for ck in range(8, 16)], reads=[wb] + B_hc, writes=[bb])
                wout_evac(d, bi, bb)
            _ahead(LOOK)
            for u in range(3, 8):
                wv, wb = load_w(w_out, 0, 16, u * 256, 256)
                for m in range(2):
                    d = 2 * u + m
                    bi, bb = R_ps.next()
                    fns = []
                    for ck in range(16):
                        if ck < 8:
                            rhs = YT[:, ck, t0 * 128:t0 * 128 + 512]
                        else:
                            rhs = headsc[:, ck - 8, :]
                        fns.append(I('matmul', ps[:, bi, :], wv[:, ck, m * 128:(m + 1) * 128], rhs, start=(ck == 0), stop=(ck == 15)))
                    P.pe_group(fns, reads=[wb] + B_hc + B_YT[t0:t0 + 4], writes=[bb])
                    wout_evac(d, bi, bb)

            if g == 0 and dbg_dump('B5', xres[:].rearrange("p a b -> p (a b)"), F32, B_xres):
                return finish(nc, P, es, sems)
            ln_fm_finish()
            for d in range(16):
                tap, tbuf = ln_fm_center(xres[:, d, :], B_xres[d])
                P.op('act', I('activation', out=xres[:, d, :], in_=tap, func=AF.Identity, bias=pcol('a1b', d), scale=pcol('a1g', d)),
                     reads=[tbuf, B_const], writes=[B_xres[d]])
                P.op('pool', I('tensor_scalar', xbf[:, d, :], tap, pcol('ln1g', d), pcol('ln1b', d), ALU.mult, ALU.add),
                     reads=[tbuf, B_const], writes=[B_xbf[d]])

            if g == 0 and dbg_dump('LN1', xres[:].rearrange("p a b -> p (a b)"), F32, B_xres + B_xbf):
                return finish(nc, P, es, sems)
            def ff1_evac(q, fl, bi, bb):
                hb = q % 2
                f = q * 8 + fl
                ti, tb = R_tf.next()
                tv = tmpf[ti][:, 0:512]
                P.op('act', I('activation', out=tv, in_=ps[:, bi, :], func=AF.Relu, bias=pcol('bff1', f), scale=1.0),
                     reads=[bb, B_const], writes=[tb])
                P.op('dve', I('tensor_tensor', hid[hb][:, fl, :], tv, tv, ALU.mult), reads=[tb], writes=[B_hid[hb][fl]])

            def ff1(q):
                u0 = 0
                if q == 0:
                    outs = []
                    for u in range(3):
                        wv, wb = load_w(w_ff1, 0, 16, (q * 8 + u * 2) * 128, 256, look=0)
                        for m in range(2):
                            bi, bb = R_ps.next()
                            outs.append((u * 2 + m, m, wv, wb, bi, bb))
                    for dk in range(16):
                        P.pe_group([I('matmul', ps[:, bi, :], wv[:, dk, m * 128:(m + 1) * 128], xbf[:, dk, :], start=(dk == 0), stop=(dk == 15))
                                    for (fl, m, wv, wb, bi, bb) in outs],
                                   reads=[B_xbf[dk]] + [o_[3] for o_ in outs], writes=[o_[5] for o_ in outs])
                    for (fl, m, wv, wb, bi, bb) in outs:
                        ff1_evac(q, fl, bi, bb)
                    _ahead(LOOK)
                    u0 = 3
                for u in range(u0, 4):
                    wv, wb = load_w(w_ff1, 0, 16, (q * 8 + u * 2) * 128, 256)
                    for m in range(2):
                        fl = u * 2 + m
                        bi, bb = R_ps.next()
                        fns = []
                        for dk in range(16):
                            fns.append(I('matmul', ps[:, bi, :], wv[:, dk, m * 128:(m + 1) * 128], xbf[:, dk, :], start=(dk == 0), stop=(dk == 15)))
                        P.pe_group(fns, reads=[wb] + B_xbf, writes=[bb])
                        ff1_evac(q, fl, bi, bb)

            def ff2(q):
                hb = q % 2
                for u in range(4):
                    wv, wb = load_w(w_ff2, q * 1024, 8, u * 512, 512)
                    for m in range(4):
                        d = u * 4 + m
                        bi, bb = R_ps.next()
                        fns = []
                        for fk in range(8):
                            fns.append(I('matmul', ps[:, bi, :], wv[:, fk, m * 128:(m + 1) * 128], hid[hb][:, fk, :], start=(fk == 0), stop=(fk == 7)))
                        P.pe_group(fns, reads=[wb] + B_hid[hb], writes=[bb])
                        P.op('dve', I('tensor_tensor', xres[:, d, :], xres[:, d, :], ps[:, bi, :], ALU.add), reads=[bb, B_xres[d]], writes=[B_xres[d]])
                        if q == 7:
                            ln_fm_stats(xres[:, d, :], B_xres[d], d, 16, onesD)

            for q in range(9):
                if q < 8:
                    ff1(q)
                if q > 0:
                    ff2(q - 1)
            if g == 0 and dbg_dump('FF', xres[:].rearrange("p a b -> p (a b)"), F32, B_xres):
                return finish(nc, P, es, sems)
            ln_fm_finish()
            for d in range(16):
                tap, tbuf = ln_fm_center(xres[:, d, :], B_xres[d])
                P.op('act', I('activation', out=xres[:, d, :], in_=tap, func=AF.Identity, bias=pcol('ln2b', d), scale=pcol('ln2g', d)),
                     reads=[tbuf, B_const], writes=[B_xres[d]])
                P.op('pool', I('tensor_scalar', xbf[:, d, :], tap, pcol('ln2g', d), pcol('ln2b', d), ALU.mult, ALU.add),
                     reads=[tbuf, B_const], writes=[B_xbf[d]])
            if g == 0 and dbg_dump('LN2', xres[:].rearrange("p a b -> p (a b)"), F32, B_xres + B_xbf):
                return finish(nc, P, es, sems)
            P.dma('sp', I('dma_start', out=pstg[:], in_=pown[g]), 'D_pstg', writes=[B_pstg])
            for tt in range(4):
                bi, bb = R_ps.next()
                P.pe_group([I('transpose', ps[:, bi, kc * 128:(kc + 1) * 128], pstg[:, tt, kc * 128:(kc + 1) * 128], ident_f[:]) for kc in range(2)],
                           reads=[B_pstg, B_const], writes=[bb])
                P.op('act', I('activation', out=pT[:, :, tt * 128:(tt + 1) * 128], in_=ps[:, bi, 0:256].rearrange("p (a b) -> p a b", a=2), func=AF.Copy),
                     reads=[bb], writes=[B_pT])
            for u in range(8):
                wv, wb = load_w(w_gate, 0, 16, u * 256, 256)
                for m in range(2):
                    d = 2 * u + m
                    bi, bb = R_ps.next()
                    be, beb = R_ps.next()
                    fns = []
                    for dk in range(16):
                        fns.append(I('matmul', ps[:, bi, :], wv[:, dk, m * 128:(m + 1) * 128], xbf[:, dk, :], start=(dk == 0), stop=(dk == 15)))
                    for kc in range(2):
                        fns.append(I('matmul', ps[:, be, :], wple_b[:, kc, d * 128:(d + 1) * 128], pT[:, kc, :], start=(kc == 0), stop=(kc == 1)))
                    P.pe_group(fns, reads=[wb, B_wple, B_pT] + B_xbf, writes=[bb, beb])
                    ti, tb = R_tf.next()
                    tv = tmpf[ti][:, 0:512]
                    P.op('act', I('activation', out=tv, in_=ps[:, bi, :], func=AF.Sigmoid, bias=pcol('bgate', d), scale=1.0),
                         reads=[bb, B_const], writes=[tb])
                    P.op('dve', I('tensor_tensor', tv, tv, ps[:, be, :], ALU.mult), reads=[beb, tb], writes=[tb])
                    P.op('dve', I('tensor_tensor', xres[:, d, :], xres[:, d, :], tv, ALU.add), reads=[tb, B_xres[d]], writes=[B_xres[d]])
                    ln_fm_stats(xres[:, d, :], B_xres[d], d, 16, onesD)
            if g == 0 and dbg_dump('G', xres[:].rearrange("p a b -> p (a b)"), F32, B_xres):
                return finish(nc, P, es, sems)
            ln_fm_finish()
            for d in range(16):
                tap, tbuf = ln_fm_center(xres[:, d, :], B_xres[d])
                P.op('act', I('activation', out=xres[:, d, :], in_=tap, func=AF.Identity, bias=pcol('ln3b', d), scale=pcol('ln3g', d)),
                     reads=[tbuf, B_const], writes=[B_xres[d]])
            if g == 0 and dbg_dump('LN3', xres[:].rearrange("p a b -> p (a b)"), F32, B_xres):
                return finish(nc, P, es, sems)
            for tt in range(4):
                si, sbuf_ = R_stg.next()
                for q4 in range(4):
                    bi, bb = R_ps.next()
                    P.pe_group([I('transpose', ps[:, bi, i * 128:(i + 1) * 128], xres[:, q4 * 4 + i, tt * 128:(tt + 1) * 128], ident_f[:]) for i in range(4)],
                               reads=[B_const] + B_xres[q4 * 4:q4 * 4 + 4], writes=[bb])
                    if q4 % 2 == 0:
                        P.op('act', I('activation', out=stg[si][:, q4 * 512:(q4 + 1) * 512], in_=ps[:, bi, :], func=AF.Copy), reads=[bb], writes=[sbuf_])
                    else:
                        P.op('dve', I('tensor_copy', stg[si][:, q4 * 512:(q4 + 1) * 512], ps[:, bi, :]), reads=[bb], writes=[sbuf_])
                P.dma('sp', I('dma_start', out=yown[t0 + tt], in_=stg[si][:]), 'D_stg%d' % si, reads=[sbuf_])

        return finish(nc, P, es, sems)


def finish(nc, P, es, sems):
    deps = {n: v for n, v in P.tick.items() if v > 0}
    P._wait('sp', deps)
    for name in sorted(P.semnames):
        sems[name] = es.enter_context(nc.semaphore(name))
    block = es.enter_context(nc.Block())

    def runner(items):
        def run(e):
            for it in items:
                if it[0] == 'w':
                    e.wait_ge(sems[it[1]], it[2])
                else:
                    nm, a, kw = it[1]
                    ins = getattr(e, nm)(*a, **kw)
                    if it[2] is not None:
                        ins.then_inc(sems[it[2]], it[3])
        return run
    block.sync(runner(P.q['sp']))
    block.scalar(runner(P.q['act']))
    block.vector(runner(P.q['dve']))
    block.gpsimd(runner(P.q['pool']))
    block.tensor(runner(P.q['pe']))
    return nc


def _src_tiles(h):
    o = [('p', 16 * h + j) for j in range(16)] + [('p', 16 * (1 - h) + j) for j in range(16)]
    o += [('s', 8 * h + j) for j in range(8)] + [('s', 8 * (1 - h) + j) for j in range(8)]
    return o


_CONST_CACHE = {}


def _dft_tables(h):
    if h in _CONST_CACHE:
        return _CONST_CACHE[h]
    order = _src_tiles(h)
    out = []
    for (S, nown_t, src_list, own_base) in ((4096, 16, [t for (q, t) in order if q == 'p'], 16 * h),
                                            (2048, 8, [t for (q, t) in order if q == 's'], 8 * h)):
        k = np.arange(S, dtype=np.float64)
        ctab = (np.cos(2 * np.pi * k / S) / np.sqrt(S)).astype(np.float32)
        stab = (np.sin(2 * np.pi * k / S) / np.sqrt(S)).astype(np.float32)
        nsrc = len(src_list)
        pos_src = (np.array(src_list, dtype=np.int64)[None, :] * 128 + np.arange(128, dtype=np.int64)[:, None])
        cm = np.empty((nown_t, 128, 2, nsrc, 128), dtype=np.float32)
        for t in range(nown_t):
            pos_own = (own_base + t) * 128 + np.arange(128, dtype=np.int64)
            idx = (pos_src[:, :, None] * pos_own[None, None, :]) % S
            cm[t, :, 0] = ctab[idx]
            cm[t, :, 1] = stab[idx]
        out.append(cm.reshape(nown_t, 128, 2 * nsrc * 128))
    _CONST_CACHE[h] = out
    return out


def _chan_table():
    d = np.arange(256, dtype=np.int64)
    idx = (d[:, None] * d[None, :]) % 256
    k = np.arange(256, dtype=np.float64)
    c = (np.cos(2 * np.pi * k / 256) / 16.0).astype(np.float32)[idx]
    s = (-np.sin(2 * np.pi * k / 256) / 16.0).astype(np.float32)[idx]
    t = np.stack([c, s], axis=1)
    t = t.reshape(2, 128, 2, 256).transpose(1, 0, 2, 3)
    return np.ascontiguousarray(t.reshape(128, 2 * 2 * 256))


def _cols(v, n):
    return np.ascontiguousarray(np.asarray(v, dtype=np.float32).reshape(n, 128).T)


def _prep(inputs):
    xp = np.asarray(inputs['x_prompt'], dtype=np.float32)
    xs = np.asarray(inputs['x_sample'], dtype=np.float32)
    pp_ = np.asarray(inputs['p_prompt'], dtype=np.float32)[0]
    ps_ = np.asarray(inputs['p_sample'], dtype=np.float32)[0]
    g = lambda n: np.asarray(inputs[n], dtype=np.float32)
    wdw = g('w_dw')[0]
    wdw_cols = np.ascontiguousarray(wdw.reshape(31, 8, 128).transpose(2, 1, 0).reshape(128, 248))
    pp = np.concatenate([
        _cols(g('emb_ln_g'), 16), _cols(g('emb_ln_b'), 16), _cols(g('conv_ln_g')[0], 8), _cols(g('conv_ln_b')[0], 8),
        _cols(g('b_dw')[0], 8), wdw_cols, _cols(g('ln1_g')[0], 16), _cols(g('ln1_b')[0], 16), _cols(g('b_ff1')[0], 64),
        _cols(g('b_ff2')[0], 16), _cols(g('ln2_g')[0], 16), _cols(g('ln2_b')[0], 16), _cols(g('b_gate')[0], 16),
        _cols(g('ln3_g')[0], 16), _cols(g('ln3_b')[0], 16)], axis=1)
    assert pp.shape == (128, NP_IN)
    shared = dict(w_in=np.ascontiguousarray(g('w_in')[0]), w_out=np.ascontiguousarray(g('w_out')[0]),
                  w_ff1=np.ascontiguousarray(g('w_ff1')[0]), w_ff2=np.ascontiguousarray(g('w_ff2')[0]),
                  w_gate=np.ascontiguousarray(g('w_gate')[0]), w_ple=np.ascontiguousarray(g('w_ple')[0]),
                  pp=np.ascontiguousarray(pp), cdt=_chan_table(), ident=np.eye(128, dtype=np.float32))
    in_maps = []
    for c in range(8):
        b, h = c // 2, c % 2
        order = _src_tiles(h)
        seqs = {'p': xp[b], 's': xs[b]}
        xsrc = np.stack([seqs[q][t * 128:(t + 1) * 128] for (q, t) in order], axis=0)
        xhalo = np.zeros((NG, 32, 2048), dtype=np.float32)
        hmask = np.zeros((NG, 32), dtype=np.float32)
        pown = np.empty((NG, 128, 4, 256), dtype=np.float32)
        for gi in range(NG):
            if gi < 4:
                seq, S, g0, pseq = xp[b], 4096, 2048 * h + 512 * gi, pp_[b]
            else:
                seq, S, g0, pseq = xs[b], 2048, 1024 * h + 512 * (gi - 4), ps_[b]
            for r in range(30):
                pos = g0 - 15 + r if r < 15 else g0 + 512 + (r - 15)
                if 0 <= pos < S:
                    xhalo[gi, r] = seq[pos]
                    hmask[gi, r] = 1.0
            pown[gi] = pseq[g0:g0 + 512].reshape(4, 128, 256).transpose(1, 0, 2)
        cmp_, cms_ = _dft_tables(h)
        m = dict(shared)
        m.update(xsrc=np.ascontiguousarray(xsrc), xhalo=xhalo,
                 hmask=np.ascontiguousarray(np.broadcast_to(hmask.reshape(1, NG * 32), (128, NG * 32))),
                 pown=pown, cmat_p=cmp_, cmat_s=cms_)
        in_maps.append(m)
    return in_maps


_NC_CACHE = {}


def kernel(**inputs):
    in_maps = _prep(inputs)
    if 'nc' not in _NC_CACHE:
        _NC_CACHE['nc'] = build_program()
    nc = _NC_CACHE['nc']
    res = run_bass_kernel_spmd(nc, in_maps, core_ids=list(range(8)))
    y_prompt = np.empty((4, 4096, 2048), dtype=np.float32)
    y_sample = np.empty((4, 2048, 2048), dtype=np.float32)
    for c in range(8):
        b, h = c // 2, c % 2
        y = np.asarray(res.results[c]["yown"], dtype=np.float32).reshape(NOWN * 128, 2048)
        y_prompt[b, 2048 * h:2048 * h + 2048] = y[0:2048]
        y_sample[b, 1024 * h:1024 * h + 1024] = y[2048:3072]
    return (y_prompt, y_sample)
```

```python
import numpy as np
from collections import defaultdict
from contextlib import ExitStack
import concourse.bass as bass
import concourse.mybir as mybir
from concourse.bass_utils import run_bass_kernel_spmd

F32 = mybir.dt.float32
BF16 = mybir.dt.bfloat16
U8 = mybir.dt.uint8
AF = mybir.ActivationFunctionType
ALU = mybir.AluOpType

ALPHA = float(2.0 ** 0.25)
LN_EPS = 1e-5
NSRC = 48
NOWN = 24
NG = 6
ENG = ('pe', 'act', 'dve', 'pool', 'sp')
import os
NTT_DBG = int(os.environ.get('NTT_DBG', '5'))
NG_DBG = int(os.environ.get('NG_DBG', '6'))

PCOLS = {}
_off = 0
for _n, _w in (('emb_g', 16), ('emb_b', 16), ('convg', 8), ('convb', 8), ('bdw', 8), ('wdw', 248),
               ('ln1g', 16), ('ln1b', 16), ('bff1', 64), ('bff2', 16), ('ln2g', 16), ('ln2b', 16),
               ('bgate', 16), ('ln3g', 16), ('ln3b', 16),
               ('aeg', 16), ('aeb', 16), ('a1g', 16), ('a1b', 16)):
    PCOLS[_n] = _off
    _off += _w
NP_IN = PCOLS['aeg']
NP_ALL = _off


class Buf:
    __slots__ = ('rd', 'wr', 'name', 'excl')

    def __init__(self, name='', excl=False):
        self.rd = {}
        self.wr = None
        self.name = name
        self.excl = excl


class Prog:
    def __init__(self):
        self.q = {e: [] for e in ENG}
        self.tick = defaultdict(int)
        self.waited = defaultdict(int)
        self.semnames = set('S_' + e for e in ENG)

    def _wait(self, eng, deps):
        for name, v in deps.items():
            if eng == 'pe' and name == 'S_pe':
                continue
            if self.waited[(eng, name)] >= v:
                continue
            self.waited[(eng, name)] = v
            self.q[eng].append(('w', name, v))

    @staticmethod
    def _deps(reads, writes, eng=None):
        d = {}

        def add(n, v):
            if d.get(n, 0) < v:
                d[n] = v
        for b in reads:
            if b.wr is not None:
                add(*b.wr)
            if b.excl:
                for n, v in b.rd.items():
                    if n != 'S_' + str(eng):
                        add(n, v)
        for b in writes:
            if b.wr is not None:
                add(*b.wr)
            for n, v in b.rd.items():
                add(n, v)
        return d

    def _commit(self, tok, reads, writes):
        name, v = tok
        for b in reads:
            if b.rd.get(name, 0) < v:
                b.rd[name] = v
        for b in writes:
            b.wr = tok
            b.rd = {}

    def op(self, eng, fn, reads=(), writes=(), sem=None, inc=1):
        self._wait(eng, self._deps(reads, writes, eng))
        name = sem or ('S_' + eng)
        self.semnames.add(name)
        self.tick[name] += inc
        tok = (name, self.tick[name])
        self.q[eng].append(('o', fn, name, inc))
        self._commit(tok, reads, writes)
        return tok

    def dma(self, eng, fn, sem, reads=(), writes=()):
        return self.op(eng, fn, reads, writes, sem=sem, inc=16)

    def pe_group(self, fns, reads=(), writes=()):
        self._wait('pe', self._deps(reads, writes))
        for fn in fns[:-1]:
            self.q['pe'].append(('o', fn, None, 0))
        self.tick['S_pe'] += 1
        tok = ('S_pe', self.tick['S_pe'])
        self.q['pe'].append(('o', fns[-1], 'S_pe', 1))
        self._commit(tok, reads, writes)
        return tok

    def barrier(self):
        deps = {n: v for n, v in self.tick.items() if v > 0}
        for e in ENG:
            self._wait(e, deps)


class Ring:
    def __init__(self, n, name):
        self.bufs = [Buf('%s%d' % (name, i)) for i in range(n)]
        self.i = 0
        self.n = n

    def next(self):
        i = self.i
        self.i = (self.i + 1) % self.n
        return i, self.bufs[i]


def I(name, *args, **kw):
    return (name, args, kw)


def build_program(debug=None):
    nc = bass.Bass("TRN2", target_bir_lowering=False)
    P = Prog()
    dt_in = lambda name, shape: nc.dram_tensor(name, shape, F32, kind="ExternalInput").ap()
    xsrc = dt_in("xsrc", [NSRC, 128, 2048])
    xhalo = dt_in("xhalo", [NG, 32, 2048])
    hmask_d = dt_in("hmask", [128, NG * 32])
    pown = dt_in("pown", [NG, 128, 4, 256])
    cmat_p = dt_in("cmat_p", [16, 128, 2 * 32 * 128])
    cmat_s = dt_in("cmat_s", [8, 128, 2 * 16 * 128])
    w_in = dt_in("w_in", [2048, 3072])
    w_out = dt_in("w_out", [2048, 2048])
    w_ff1 = dt_in("w_ff1", [2048, 8192])
    w_ff2 = dt_in("w_ff2", [8192, 2048])
    w_gate = dt_in("w_gate", [2048, 2048])
    w_ple = dt_in("w_ple", [256, 2048])
    pp_d = dt_in("pp", [128, NP_IN])
    cdt_d = dt_in("cdt", [128, 2 * 2 * 256])
    ident_d = dt_in("ident", [128, 128])
    yown = nc.dram_tensor("yown", [NOWN, 128, 2048], F32, kind="ExternalOutput").ap()
    dbg = None
    if debug == 'U':
        dbg = nc.dram_tensor("dbg", [128, NSRC * 1024], BF16, kind="ExternalOutput").ap()
    elif debug == 'YT':
        dbg = nc.dram_tensor("dbg", [128, 8 * 3072], BF16, kind="ExternalOutput").ap()

    es = ExitStack()
    with es:
        SLAB = 211968
        slab = es.enter_context(nc.sbuf_tensor("slab", [128, SLAB], U8))
        base = nc.lookup_mloc(slab).addr

        def sb(name, shape, dtype, off):
            nb = int(np.prod(shape[1:])) * (4 if dtype == F32 else 2)
            assert off % 32 == 0, (name, off)
            assert off + nb <= SLAB, (name, off, nb)
            return nc.alloc_sbuf_tensor_at(name, shape, dtype, offset=base + off), off + ((nb + 31) // 32) * 32

        o = 0
        pp, o = sb("pp", [128, NP_ALL], F32, o)
        ident_f, o = sb("ident_f", [128, 128], F32, o)
        ident_b, o = sb("ident_b", [128, 128], BF16, o)
        onesD, o = sb("onesD", [128, 128], BF16, o)
        onesC, o = sb("onesC", [128, 128], BF16, o)
        cdt_b, o = sb("cdt_b", [128, 2, 2, 256], BF16, o)
        hmask, o = sb("hmask", [128, NG, 32], F32, o)
        small, o = sb("small", [128, 64], F32, o)
        assert o <= 8192, o
        O_YT = 8192
        YT, _ = sb("YT", [128, 8, 3072], BF16, O_YT)
        O_U = O_YT + 49152
        O_X = O_U + 98304

        ps = es.enter_context(nc.psum_tensor("ps", [128, 8, 512], F32))
        sems = {}

        def pcol(name, j=0):
            c = PCOLS[name] + j
            return pp[:, c:c + 1]

        def pcols(name, n=16):
            c = PCOLS[name]
            return pp[:, c:c + n]

        B_const = Buf('const')
        B_YT = [Buf('YT%d' % t) for t in range(NOWN)]

        def dbg_dump(stage, ap2d, dtype, bufs):
            if debug != stage:
                return False
            d_ = nc.dram_tensor("dbg", list(ap2d.shape), dtype, kind="ExternalOutput").ap()
            P.dma('sp', I('dma_start', out=d_, in_=ap2d), 'D_out', reads=bufs)
            return True
        B_small = [Buf('small0'), Buf('small1')]

        P.dma('sp', I('dma_start', out=pp[:, 0:NP_IN], in_=pp_d), 'D_init', writes=[B_const])
        P.dma('sp', I('dma_start', out=ident_f[:], in_=ident_d), 'D_init', writes=[B_const])
        P.dma('sp', I('dma_start', out=hmask[:].rearrange("p a b -> p (a b)"), in_=hmask_d), 'D_init', writes=[B_const])
        P.dma('pool', I('dma_start', out=cdt_b[:].rearrange("p a b c -> p (a b c)"), in_=cdt_d), 'D_init2', writes=[B_const])
        P.op('dve', I('tensor_copy', ident_b[:], ident_f[:]), reads=[B_const], writes=[B_const])
        P.op('dve', I('memset', onesD[:], 1.0 / 2048), writes=[B_const])
        P.op('dve', I('memset', onesC[:], 1.0 / 1024), writes=[B_const])
        P.op('dve', I('tensor_scalar', pcols('aeg'), pcols('emb_g'), ALPHA, None, ALU.mult), reads=[B_const], writes=[B_const])
        P.op('dve', I('tensor_scalar', pcols('aeb'), pcols('emb_b'), ALPHA, None, ALU.mult), reads=[B_const], writes=[B_const])
        P.op('dve', I('tensor_scalar', pcols('a1g'), pcols('ln1g'), ALPHA, None, ALU.mult), reads=[B_const], writes=[B_const])
        P.op('dve', I('scalar_tensor_tensor', pcols('a1b'), pcols('ln1b'), ALPHA, pcols('bff2'), ALU.mult, ALU.add), reads=[B_const], writes=[B_const])

        psbank = [Buf('psb%d' % i, excl=True) for i in range(8)]

        unit_specs = []
        for j_ in range(8):
            unit_specs.append((w_in, 0, 16, (1024 + j_ * 128, 2048 + j_ * 128), 128))
        for u_ in range(8):
            unit_specs.append((w_out, 0, 16, u_ * 256, 256))
        for q_ in range(9):
            if q_ < 8:
                for u_ in range(4):
                    unit_specs.append((w_ff1, 0, 16, (q_ * 8 + u_ * 2) * 128, 256))
            if q_ > 0:
                for u_ in range(4):
                    unit_specs.append((w_ff2, (q_ - 1) * 1024, 8, u_ * 512, 512))
        for u_ in range(8):
            unit_specs.append((w_gate, 0, 16, u_ * 256, 256))
        NU = len(unit_specs)
        wsc = nc.dram_tensor("wsc", [NU, 128, 4096], BF16, kind="Internal").ap()
        B_wsc = [Buf('wsc%d' % u) for u in range(NU)]
        NCV = 4
        B_cv = [Buf('cv%d' % i) for i in range(NCV)]
        cvstate = {'u': 0}

        def conv_units(n):
            for _ in range(n):
                u = cvstate['u']
                if u >= NU:
                    return
                cvstate['u'] = u + 1
                w_ap, r0, nk, c0s, ncol = unit_specs[u]
                if not isinstance(c0s, tuple):
                    c0s = (c0s,)
                tot = ncol * len(c0s)
                dst = wsc[u].rearrange("p (a b) -> p a b", a=nk)
                si = u % NCV
                for ci, c0 in enumerate(c0s):
                    P.dma('pool', I('dma_start', out=dst[:, :, ci * ncol:(ci + 1) * ncol],
                                    in_=w_ap[r0:r0 + nk * 128, c0:c0 + ncol].rearrange("(k p) c -> p k c", p=128)),
                          'D_cv%d' % si, writes=[B_wsc[u], B_cv[si]])

        def ln_tile_stats(xt, xbuf, rows, si):
            scol = 32 * si
            bs = B_small[si]
            st = small[0:rows, scol:scol + 24]
            mv = small[0:rows, scol + 24:scol + 26]
            rstd = small[0:rows, scol + 26:scol + 27]
            nmr = small[0:rows, scol + 27:scol + 28]
            for c4 in range(4):
                P.op('dve', I('bn_stats', st[:, c4 * 6:(c4 + 1) * 6], xt[:, c4 * 512:(c4 + 1) * 512]), reads=[xbuf], writes=[bs])
            P.op('dve', I('bn_aggr', mv, st.rearrange("p (a b) -> p a b", a=4)), reads=[bs], writes=[bs])
            P.op('act', I('activation', out=rstd, in_=mv[:, 1:2], func=AF.Sqrt, bias=LN_EPS, scale=1.0), reads=[bs], writes=[bs])
            P.op('dve', I('reciprocal', rstd, rstd), reads=[bs], writes=[bs])
            P.op('dve', I('scalar_tensor_tensor', nmr, mv[:, 0:1], -1.0, rstd, ALU.mult, ALU.mult), reads=[bs], writes=[bs])
            P.op('act', I('activation', out=xt, in_=xt, func=AF.Identity, bias=nmr, scale=rstd), reads=[bs, xbuf], writes=[xbuf])

        def transpose_tile(xt, xbuf, rows, banks, evac):
            for q4 in range(4):
                bi = banks[q4]
                fns = []
                for i in range(4):
                    dk = q4 * 4 + i
                    fns.append(I('transpose', ps[:, bi, i * 128:i * 128 + rows], xt[:, dk * 128:(dk + 1) * 128], ident_f[0:rows, 0:rows]))
                P.pe_group(fns, reads=[xbuf, B_const], writes=[psbank[bi]])
                for i in range(4):
                    dk = q4 * 4 + i
                    evac(dk, ps[:, bi, i * 128:i * 128 + rows], psbank[bi])

        U, _ = sb("U", [128, NSRC, 1024], BF16, O_U)
        B_U = [Buf('U%d' % j) for j in range(NSRC)]
        Wf, _ = sb("Wf", [128, 16, 1024], BF16, O_YT)
        stgA = [sb("stgA%d" % i, [128, 2048], F32, O_YT + 32768 + i * 8192)[0] for i in range(2)]
        B_stgA = [Buf('stgA%d' % i) for i in range(2)]
        xnT = [sb("xnT%d" % i, [128, 16, 128], BF16, O_X + i * 4096)[0] for i in range(2)]
        B_xnTc = [[Buf('xnT%d_%d' % (i, k)) for k in range(16)] for i in range(2)]
        B_Wf = Buf('Wf')
        for q4 in range(4):
            P.dma('pool', I('dma_start', out=Wf[:, q4 * 4:(q4 + 1) * 4, :],
                            in_=w_in[q4 * 512:(q4 + 1) * 512, 0:1024].rearrange("(dk p) c -> p dk c", p=128)),
                  'D_wf', writes=[B_Wf])

        def A_stage1a(j):
            s = j % 2
            P.dma('sp', I('dma_start', out=stgA[s][:], in_=xsrc[j]), 'D_stg%d' % s, writes=[B_stgA[s]])
            ln_tile_stats(stgA[s][:], B_stgA[s], 128, s)

        def A_stage1b(j):
            s = j % 2

            def evac(dk, psap, bb):
                if dk < 8:
                    P.op('dve', I('tensor_scalar', xnT[s][:, dk, :], psap, pcol('emb_g', dk), pcol('emb_b', dk), ALU.mult, ALU.add),
                         reads=[bb, B_const], writes=[B_xnTc[s][dk]])
                else:
                    P.op('act', I('activation', out=xnT[s][:, dk, :], in_=psap, func=AF.Identity, bias=pcol('emb_b', dk), scale=pcol('emb_g', dk)),
                         reads=[bb, B_const], writes=[B_xnTc[s][dk]])
            transpose_tile(stgA[s][:], B_stgA[s], 128, [0, 1, 2, 3], evac)

        def A_stage2(j):
            s = j % 2
            bk = [4 + 2 * s, 5 + 2 * s]
            fns = []
            for dk in range(16):
                for cb in range(2):
                    fns.append(I('matmul', ps[:, bk[cb], :], xnT[s][:, dk, :], Wf[:, dk, cb * 512:(cb + 1) * 512],
                                 start=(dk == 0), stop=(dk == 15)))
            P.pe_group(fns, reads=B_xnTc[s] + [B_Wf], writes=[psbank[bk[0]], psbank[bk[1]]])
            for cb in range(2):
                P.op('act', I('activation', out=U[:, j, cb * 512:(cb + 1) * 512], in_=ps[:, bk[cb], :], func=AF.Copy),
                     reads=[psbank[bk[cb]]], writes=[B_U[j]])

        for j in range(NSRC + 2):
            if j < NSRC:
                A_stage1a(j)
                if j >= 2:
                    conv_units(1)
            if 1 <= j <= NSRC:
                A_stage1b(j - 1)
            if j >= 2:
                A_stage2(j - 2)

        if debug == 'U':
            P.dma('sp', I('dma_start', out=dbg, in_=U[:].rearrange("p a b -> p (a b)")), 'D_out', reads=B_U)
            return finish(nc, P, es, sems)

        P.barrier()
        cm = [sb("cm%d" % i, [128, 2, 32, 128], BF16, O_X + i * 16384)[0] for i in range(2)]
        B_cm = [Buf('cm%d' % i) for i in range(2)]
        o2 = O_X + 32768
        PQb = []
        for i in range(2):
            t_, o2 = sb("PQb%d" % i, [128, 2, 512], BF16, o2)
            PQb.append(t_)
        B_PQb = [Buf() for _ in range(2)]
        PQT = []
        for i in range(2):
            t_, o2 = sb("PQT%d" % i, [128, 8, 128], BF16, o2)
            PQT.append(t_)
        B_PQT = [Buf() for _ in range(2)]
        rounds = []
        cm_slot = {}

        def Ap_PQ(t, cb, r):
            prompt = t < 16
            nsrc = 32 if prompt else 16
            j0 = 0 if prompt else 32
            s = t % 2
            if cb == 0:
                if prompt:
                    P.dma('pool', I('dma_start', out=cm[s][:].rearrange("p a b c -> p (a b c)"), in_=cmat_p[t]),
                          'D_cm%d' % s, writes=[B_cm[s]])
                else:
                    P.dma('pool', I('dma_start', out=cm[s][:, :, 0:16, :], in_=cmat_s[t - 16].rearrange("p (a b c) -> p a b c", a=2, b=16)),
                          'D_cm%d' % s, writes=[B_cm[s]])
            bP, bQ = 2 * r, 2 * r + 1
            fns = []
            for jj in range(nsrc):
                fns.append(I('matmul', ps[:, bP, :], cm[s][:, 0, jj, :], U[:, j0 + jj, cb * 512:(cb + 1) * 512],
                             start=(jj == 0), stop=(jj == nsrc - 1)))
                fns.append(I('matmul', ps[:, bQ, :], cm[s][:, 1, jj, :], U[:, j0 + jj, cb * 512:(cb + 1) * 512],
                             start=(jj == 0), stop=(jj == nsrc - 1)))
            P.pe_group(fns, reads=[B_cm[s]], writes=[psbank[bP], psbank[bQ]])
            P.op('act', I('activation', out=PQb[r][:, 0, :], in_=ps[:, bP, :], func=AF.Copy), reads=[psbank[bP]], writes=[B_PQb[r]])
            P.op('dve', I('tensor_copy', PQb[r][:, 1, :], ps[:, bQ, :]), reads=[psbank[bQ]], writes=[B_PQb[r]])

        def Ap_T(t, cb, r):
            bT = 4 + r
            psT = ps[:, bT, :].bitcast(BF16)
            fns = []
            for pq in range(2):
                for i in range(4):
                    fns.append(I('transpose', psT[:, (pq * 4 + i) * 128:(pq * 4 + i + 1) * 128],
                                 PQb[r][:, pq, i * 128:(i + 1) * 128], ident_b[:]))
            P.pe_group(fns, reads=[B_PQb[r], B_const], writes=[psbank[bT]])
            P.op('act', I('activation', out=PQT[r][:].rearrange("p a b -> p (a b)"), in_=psT, func=AF.Copy),
                 reads=[psbank[bT]], writes=[B_PQT[r]])

        def Ap_Y(t, cb, r):
            bY = 6 + r
            fns = []
            for fgl in range(2):
                for dp in range(2):
                    oc = (fgl * 2 + dp) * 128
                    k = 0
                    for kc in range(2):
                        for cs in range(2):
                            fns.append(I('matmul', ps[:, bY, oc:oc + 128], cdt_b[:, kc, cs, dp * 128:(dp + 1) * 128],
                                         PQT[r][:, cs * 4 + fgl * 2 + kc, :], start=(k == 0), stop=(k == 3)))
                            k += 1
            P.pe_group(fns, reads=[B_PQT[r], B_const], writes=[psbank[bY]])
            P.op('dve', I('tensor_copy', YT[:, cb * 4:(cb + 1) * 4, t * 128:(t + 1) * 128],
                          ps[:, bY, :].rearrange("p (a b) -> p a b", a=4)),
                 reads=[psbank[bY]], writes=[B_YT[t]])

        for t in range(NOWN):
            for cb in range(2):
                rounds.append((t, cb, len(rounds) % 2))
        nr = len(rounds)
        for i in range(nr + 2):
            if i < nr:
                Ap_PQ(*rounds[i])
                conv_units(1)
            if 1 <= i <= nr:
                Ap_T(*rounds[i - 1])
            if i >= 2:
                Ap_Y(*rounds[i - 2])

        if debug == 'YT':
            P.dma('sp', I('dma_start', out=dbg, in_=YT[:].rearrange("p a b -> p (a b)")), 'D_out', reads=B_YT)
            return finish(nc, P, es, sems)

        P.barrier()
        o = O_U
        wple_b, o = sb("wple_b", [128, 2, 2048], BF16, o)
        xres, o = sb("xres", [128, 16, 512], F32, o)
        xbf, o = sb("xbf", [128, 16, 512], BF16, o)
        xbfh, o = sb("xbfh", [128, 16, 32], BF16, o)
        mstat, o = sb("mstat", [128, 512], F32, o)
        rstat, o = sb("rstat", [128, 512], F32, o)
        cT, o_after_cT = sb("cT", [128, 8, 512], F32, o)
        hid = [sb("hid%d" % i, [128, 8, 512], BF16, o + i * 8192)[0] for i in range(2)]
        o = o_after_cT
        hT, o_after_hT = sb("hT", [128, 8, 544], BF16, o)
        headsc, _ = sb("headsc", [128, 8, 512], BF16, o)
        o = o_after_hT
        NWR = 3
        wr = []
        for i in range(NWR):
            t_, o = sb("wr%d" % i, [128, 4096], BF16, o)
            wr.append(t_)
        stg = []
        for i in range(2):
            t_, o = sb("stg%d" % i, [128, 2048], F32, o)
            stg.append(t_)
        pstg, o = sb("pstg", [128, 4, 256], F32, o)
        pT, o = sb("pT", [128, 2, 512], BF16, o)
        NT = 3
        tmpf = []
        for i in range(NT):
            t_, o = sb("tmpf%d" % i, [128, 544], F32, o)
            tmpf.append(t_)
        NTB = 8
        tmpb = []
        for i in range(NTB):
            t_, o = sb("tmpb%d" % i, [128, 512], BF16, o)
            tmpb.append(t_)
        NDG = 16
        dg = []
        for i in range(NDG):
            t_, o = sb("dg%d" % i, [128, 128], BF16, o)
            dg.append(t_)
        print("phase B sbuf end", o, "of", SLAB)

        B_wple = Buf('wple')
        P.dma('pool', I('dma_start', out=wple_b[:], in_=w_ple.rearrange("(kc p) d -> p kc d", p=128)), 'D_wple', writes=[B_wple])
        B_xres = [Buf('xres%d' % i) for i in range(16)]
        B_xbf = [Buf('xbf%d' % i) for i in range(16)]
        B_xbfh = Buf('xbfh')
        B_m = Buf('m')
        B_cT = [Buf('cT%d' % i) for i in range(8)]
        B_hT = [Buf('hT%d' % i) for i in range(8)]
        B_hc = [Buf('hc%d' % i) for i in range(8)]
        B_hid = [[Buf('hid%d_%d' % (i, k)) for k in range(8)] for i in range(2)]
        R_wr = Ring(NWR, 'wr')
        R_stg = Ring(2, 'stg')
        B_pstg = Buf('pstg')
        B_pT = Buf('pT')
        R_tf = Ring(NT, 'tmpf')
        R_tb = Ring(NTB, 'tmpb')
        R_dg = Ring(NDG, 'dg')
        R_ps = Ring(6, 'psr')
        R_ps.bufs = psbank[0:6]
        BM, BS = 6, 7

        def _issue_w(spec):
            w_ap, r0, nk, c0s, ncol = spec[:5]
            u = len(WS['issued']) % NU
            assert unit_specs[u][1:] == spec[1:5]
            tot = ncol * (len(c0s) if isinstance(c0s, tuple) else 1)
            assert nk * tot == 4096
            i, b = R_wr.next()
            view = wr[i][:, 0:nk * tot].rearrange("p (a b) -> p a b", a=nk)
            P.dma('sp', I('dma_start', out=wr[i][:], in_=wsc[u]), 'D_wr%d' % i, reads=[B_wsc[u]], writes=[b])
            return view, b

        WS = {'specs': [], 'issued': [], 'k': 0}
        LOOK = NWR - 1

        def _ahead(n):
            while len(WS['issued']) < min(len(WS['specs']), WS['k'] + n):
                WS['issued'].append(_issue_w(WS['specs'][len(WS['issued'])]))

        def load_w(w_ap, r0, nk, c0, ncol, look=None):
            spec = WS['specs'][WS['k']]
            assert spec[1:] == (r0, nk, c0, ncol), (spec[1:], (r0, nk, c0, ncol))
            _ahead(1)
            r = WS['issued'][WS['k']]
            WS['k'] += 1
            _ahead(LOOK if look is None else look)
            return r

        conv_units(NU)
        for g_ in range(min(NG, NG_DBG)):
            WS['specs'] += unit_specs

        pend_stats = []

        def flush_stats(keep=0):
            while len(pend_stats) > keep:
                fns, rd = pend_stats.pop(0)
                P.pe_group(fns, reads=rd, writes=[psbank[BM], psbank[BS]])

        def ln_fm_stats(r_ap, rbuf, c, nch, ones, defer=2):
            i1, b1 = R_tb.next()
            i2, b2 = R_tb.next()
            P.op('act', I('activation', out=tmpb[i1][:], in_=r_ap, func=AF.Copy), reads=[rbuf], writes=[b1])
            P.op('act', I('activation', out=tmpb[i2][:], in_=r_ap, func=AF.Square), reads=[rbuf], writes=[b2])
            pend_stats.append(([I('matmul', ps[:, BM, :], ones[:], tmpb[i1][:], start=(c == 0), stop=(c == nch - 1)),
                                I('matmul', ps[:, BS, :], ones[:], tmpb[i2][:], start=(c == 0), stop=(c == nch - 1))],
                               [b1, b2, B_const]))
            flush_stats(defer)

        def ln_fm_finish():
            flush_stats(0)
            i, b = R_tf.next()
            tv = tmpf[i][:, 0:512]
            P.op('act', I('activation', out=mstat[:], in_=ps[:, BM, :], func=AF.Copy), reads=[psbank[BM]], writes=[B_m])
            P.op('dve', I('tensor_tensor', tv, mstat[:], mstat[:], ALU.mult), reads=[B_m], writes=[b])
            P.op('dve', I('tensor_tensor', tv, ps[:, BS, :], tv, ALU.subtract), reads=[psbank[BS], b], writes=[b])
            P.op('act', I('activation', out=rstat[:], in_=tv, func=AF.Sqrt, bias=LN_EPS, scale=1.0), reads=[b], writes=[B_m])
            P.op('dve', I('reciprocal', rstat[:], rstat[:]), reads=[B_m], writes=[B_m])

        def ln_fm_center(r_ap, rbuf):
            i, b = R_tf.next()
            tv = tmpf[i][:, 0:512]
            P.op('dve', I('tensor_tensor', tv, r_ap, mstat[:], ALU.subtract), reads=[rbuf, B_m], writes=[b])
            P.op('dve', I('tensor_tensor', tv, tv, rstat[:], ALU.mult), reads=[B_m, b], writes=[b])
            return tv, b

        if dbg_dump('B0', YT[:].rearrange("p a b -> p (a b)"), BF16, B_YT + [B_wple]):
            return finish(nc, P, es, sems)
        for g in range(min(NG, NG_DBG)):
            t0 = 4 * g if g < 4 else 16 + 4 * (g - 4)
            j0 = t0 if g < 4 else 32 + (t0 - 16)
            b1 = {}

            def B1a(tt):
                si, sbuf_ = R_stg.next()
                rows = 128 if tt < 4 else 32
                if tt < 4:
                    P.dma('sp', I('dma_start', out=stg[si][:], in_=xsrc[j0 + tt]), 'D_stg%d' % si, writes=[sbuf_])
                else:
                    P.dma('sp', I('dma_start', out=stg[si][0:32, :], in_=xhalo[g]), 'D_stg%d' % si, writes=[sbuf_])
                xt = stg[si][0:rows, :]
                ln_tile_stats(xt, sbuf_, rows, tt % 2)
                b1[tt] = (xt, sbuf_, rows)

            for tt6 in range(NTT_DBG + 1):
                if tt6 < NTT_DBG:
                    B1a(tt6)
                if tt6 == 0:
                    continue
                tt = tt6 - 1
                xt, sbuf_, rows = b1[tt]
                banks = [R_ps.next()[0] for _ in range(4)]
                if tt < 4:
                    def evac(dk, psap, bb, tt=tt):
                        P.op('dve', I('tensor_scalar', xres[:, dk, tt * 128:(tt + 1) * 128], psap, pcol('aeg', dk), pcol('aeb', dk), ALU.mult, ALU.add),
                             reads=[bb, B_const], writes=[B_xres[dk]])
                        P.op('act', I('activation', out=xbf[:, dk, tt * 128:(tt + 1) * 128], in_=xres[:, dk, tt * 128:(tt + 1) * 128], func=AF.Copy, scale=1.0 / ALPHA),
                             reads=[B_xres[dk]], writes=[B_xbf[dk]])
                else:
                    def evac(dk, psap, bb):
                        P.op('act', I('activation', out=xbfh[:, dk, :], in_=psap, func=AF.Identity, bias=pcol('emb_b', dk), scale=pcol('emb_g', dk)),
                             reads=[bb, B_const], writes=[B_xbfh])
                transpose_tile(xt, sbuf_, rows, banks, evac)

            if g == 0 and dbg_dump('B1', xres[:].rearrange("p a b -> p (a b)"), F32, B_xres + B_xbf + [B_xbfh]):
                return finish(nc, P, es, sems)

            def conv_diags(j, k0, k1):
                r = []
                for k in range(k0, k1):
                    di, db = R_dg.next()
                    P.op('dve', I('tensor_scalar', dg[di][:], ident_f[:], pcol('wdw', j * 31 + k), None, ALU.mult),
                         reads=[B_const], writes=[db])
                    r.append((di, db))
                return r

            def conv_chunk(j, pre):
                bi, bb = R_ps.next()
                dl = list(pre)
                for k in range(31):
                    if k >= len(dl):
                        dl += conv_diags(j, k, k + 1)
                    di, db = dl[k]
                    P.pe_group([I('matmul', ps[:, bi, :], dg[di][:], hT[:, j, k:k + 512], start=(k == 0), stop=(k == 30))],
                               reads=[db, B_hT[j]], writes=[bb])
                P.op('act', I('activation', out=cT[:, j, :], in_=ps[:, bi, :], func=AF.Identity, bias=pcol('bdw', j), scale=1.0),
                     reads=[bb, B_const], writes=[B_cT[j]])
                ln_fm_stats(cT[:, j, :], B_cT[j], j, 8, onesC)

            for j in range(9):
                pre = conv_diags(j - 1, 0, NDG) if j > 0 else []
                if j < 8:
                    wvg, wb = load_w(w_in, 0, 16, (1024 + j * 128, 2048 + j * 128), 128)
                    wv, wg, wgb = wvg[:, :, 0:128], wvg[:, :, 128:256], wb
                    bv, bvb = R_ps.next()
                    bg, bgb = R_ps.next()
                    bh, bhb = R_ps.next()
                    fns = []
                    for dk in range(16):
                        fns.append(I('matmul', ps[:, bv, :], wv[:, dk, :], xbf[:, dk, :], start=(dk == 0), stop=(dk == 15)))
                    for dk in range(16):
                        fns.append(I('matmul', ps[:, bg, :], wg[:, dk, :], xbf[:, dk, :], start=(dk == 0), stop=(dk == 15)))
                    for dk in range(16):
                        fns.append(I('matmul', ps[:, bh, 0:32], wv[:, dk, :], xbfh[:, dk, :], start=(dk == 0), stop=(dk == 15)))
                    for dk in range(16):
                        fns.append(I('matmul', ps[:, bh, 32:64], wg[:, dk, :], xbfh[:, dk, :], start=(dk == 0), stop=(dk == 15)))
                    P.pe_group(fns, reads=[wb, B_xbfh] + B_xbf, writes=[bvb, bgb, bhb])
                    ti, tb = R_tf.next()
                    tf = tmpf[ti]
                    P.op('act', I('activation', out=tf[:, 0:512], in_=ps[:, bg, :], func=AF.Sigmoid), reads=[bgb], writes=[tb])
                    P.op('act', I('activation', out=tf[:, 512:544], in_=ps[:, bh, 32:64], func=AF.Sigmoid), reads=[bhb], writes=[tb])
                    P.op('dve', I('tensor_tensor', hT[:, j, 15:527], ps[:, bv, :], tf[:, 0:512], ALU.mult), reads=[bvb, tb], writes=[B_hT[j]])
                    P.op('dve', I('tensor_tensor', tf[:, 512:544], tf[:, 512:544], hmask[:, g, :], ALU.mult), reads=[tb, B_const], writes=[tb])
                    P.op('dve', I('tensor_tensor', hT[:, j, 0:15], ps[:, bh, 0:15], tf[:, 512:527], ALU.mult), reads=[bhb, tb], writes=[B_hT[j]])
                    P.op('dve', I('tensor_tensor', hT[:, j, 527:542], ps[:, bh, 15:30], tf[:, 527:542], ALU.mult), reads=[bhb, tb], writes=[B_hT[j]])
                if j > 0:
                    conv_chunk(j - 1, pre)

            if g == 0 and dbg_dump('B3', cT[:].rearrange("p a b -> p (a b)"), F32, B_cT + B_hT):
                return finish(nc, P, es, sems)

            ln_fm_finish()
            for j in range(8):
                tap, tbuf = ln_fm_center(cT[:, j, :], B_cT[j])
                P.op('act', I('activation', out=headsc[:, j, :], in_=tap, func=AF.Silu, bias=pcol('convb', j), scale=pcol('convg', j)),
                     reads=[tbuf, B_const] + B_hT, writes=[B_hc[j]])

            if g == 0 and dbg_dump('B4', headsc[:].rearrange("p a b -> p (a b)"), BF16, B_hc):
                return finish(nc, P, es, sems)
            def wout_evac(d, bi, bb):
                P.op('dve', I('tensor_tensor', xres[:, d, :], xres[:, d, :], ps[:, bi, :], ALU.add), reads=[bb, B_xres[d]], writes=[B_xres[d]])
                ln_fm_stats(xres[:, d, :], B_xres[d], d, 16, onesD)

            head = []
            for u in range(3):
                wv, wb = load_w(w_out, 0, 16, u * 256, 256, look=0)
                for m in range(2):
                    d = 2 * u + m
                    bi, bb = R_ps.next()
                    P.pe_group([I('matmul', ps[:, bi, :], wv[:, ck, m * 128:(m + 1) * 128], YT[:, ck, t0 * 128:t0 * 128 + 512], start=(ck == 0), stop=False)
                                for ck in range(8)], reads=[wb] + B_YT[t0:t0 + 4], writes=[bb])
                    head.append((d, m, wv, wb, bi, bb))
            for (d, m, wv, wb, bi, bb) in head:
                P.pe_group([I('matmul', ps[:, bi, :], wv[:, ck, m * 128:(m + 1) * 128], headsc[:, ck - 8, :], start=False, stop=(ck == 15))
                            for ck in range(8, 16)], reads=[wb] + B_hc, writes=[bb])
                wout_evac(d, bi, bb)
            _ahead(LOOK)
            for u in range(3, 8):
                wv, wb = load_w(w_out, 0, 16, u * 256, 256)
                for m in range(2):
                    d = 2 * u + m
                    bi, bb = R_ps.next()
                    fns = []
                    for ck in range(16):
                        if ck < 8:
                            rhs = YT[:, ck, t0 * 128:t0 * 128 + 512]
                        else:
                            rhs = headsc[:, ck - 8, :]
                        fns.append(I('matmul', ps[:, bi, :], wv[:, ck, m * 128:(m + 1) * 128], rhs, start=(ck == 0), stop=(ck == 15)))
                    P.pe_group(fns, reads=[wb] + B_hc + B_YT[t0:t0 + 4], writes=[bb])
                    wout_evac(d, bi, bb)

            if g == 0 and dbg_dump('B5', xres[:].rearrange("p a b -> p (a b)"), F32, B_xres):
                return finish(nc, P, es, sems)
            ln_fm_finish()
            for d in range(16):
                tap, tbuf = ln_fm_center(xres[:, d, :], B_xres[d])
                P.op('act', I('activation', out=xres[:, d, :], in_=tap, func=AF.Identity, bias=pcol('a1b', d), scale=pcol('a1g', d)),
                     reads=[tbuf, B_const], writes=[B_xres[d]])
                P.op('pool', I('tensor_scalar', xbf[:, d, :], tap, pcol('ln1g', d), pcol('ln1b', d), ALU.mult, ALU.add),
                     reads=[tbuf, B_const], writes=[B_xbf[d]])

            if g == 0 and dbg_dump('LN1', xres[:].rearrange("p a b -> p (a b)"), F32, B_xres + B_xbf):
                return finish(nc, P, es, sems)
            def ff1_evac(q, fl, bi, bb):
                hb = q % 2
                f = q * 8 + fl
                ti, tb = R_tf.next()
                tv = tmpf[ti][:, 0:512]
                P.op('act', I('activation', out=tv, in_=ps[:, bi, :], func=AF.Relu, bias=pcol('bff1', f), scale=1.0),
                     reads=[bb, B_const], writes=[tb])
                P.op('dve', I('tensor_tensor', hid[hb][:, fl, :], tv, tv, ALU.mult), reads=[tb], writes=[B_hid[hb][fl]])

            def ff1(q):
                u0 = 0
                if q == 0:
                    outs = []
                    for u in range(3):
                        wv, wb = load_w(w_ff1, 0, 16, (q * 8 + u * 2) * 128, 256, look=0)
                        for m in range(2):
                            bi, bb = R_ps.next()
                            outs.append((u * 2 + m, m, wv, wb, bi, bb))
                    for dk in range(16):
                        P.pe_group([I('matmul', ps[:, bi, :], wv[:, dk, m * 128:(m + 1) * 128], xbf[:, dk, :], start=(dk == 0), stop=(dk == 15))
                                    for (fl, m, wv, wb, bi, bb) in outs],
                                   reads=[B_xbf[dk]] + [o_[3] for o_ in outs], writes=[o_[5] for o_ in outs])
                    for (fl, m, wv, wb, bi, bb) in outs:
                        ff1_evac(q, fl, bi, bb)
                    _ahead(LOOK)
                    u0 = 3
                for u in range(u0, 4):
                    wv, wb = load_w(w_ff1, 0, 16, (q * 8 + u * 2) * 128, 256)
                    for m in range(2):
                        fl = u * 2 + m
                        bi, bb = R_ps.next()
                        fns = []
                        for dk in range(16):
                            fns.append(I('matmul', ps[:, bi, :], wv[:, dk, m * 128:(m + 1) * 128], xbf[:, dk, :], start=(dk == 0), stop=(dk == 15)))
                        P.pe_group(fns, reads=[wb] + B_xbf, writes=[bb])
                        ff1_evac(q, fl, bi, bb)

            def ff2(q):
                hb = q % 2
                for u in range(4):
                    wv, wb = load_w(w_ff2, q * 1024, 8, u * 512, 512)
                    for m in range(4):
                        d = u * 4 + m
                        bi, bb = R_ps.next()
                        fns = []
                        for fk in range(8):
                            fns.append(I('matmul', ps[:, bi, :], wv[:, fk, m * 128:(m + 1) * 128], hid[hb][:, fk, :], start=(fk == 0), stop=(fk == 7)))
                        P.pe_group(fns, reads=[wb] + B_hid[hb], writes=[bb])
                        P.op('dve', I('tensor_tensor', xres[:, d, :], xres[:, d, :], ps[:, bi, :], ALU.add), reads=[bb, B_xres[d]], writes=[B_xres[d]])
                        if q == 7:
                            ln_fm_stats(xres[:, d, :], B_xres[d], d, 16, onesD)

            for q in range(9):
                if q < 8:
                    ff1(q)
                if q > 0:
                    ff2(q - 1)
            if g == 0 and dbg_dump('FF', xres[:].rearrange("p a b -> p (a b)"), F32, B_xres):
                return finish(nc, P, es, sems)
            ln_fm_finish()
            for d in range(16):
                tap, tbuf = ln_fm_center(xres[:, d, :], B_xres[d])
                P.op('act', I('activation', out=xres[:, d, :], in_=tap, func=AF.Identity, bias=pcol('ln2b', d), scale=pcol('ln2g', d)),
                     reads=[tbuf, B_const], writes=[B_xres[d]])
                P.op('pool', I('tensor_scalar', xbf[:, d, :], tap, pcol('ln2g', d), pcol('ln2b', d), ALU.mult, ALU.add),
                     reads=[tbuf, B_const], writes=[B_xbf[d]])
            if g == 0 and dbg_dump('LN2', xres[:].rearrange("p a b -> p (a b)"), F32, B_xres + B_xbf):
                return finish(nc, P, es, sems)
            P.dma('sp', I('dma_start', out=pstg[:], in_=pown[g]), 'D_pstg', writes=[B_pstg])
            for tt in range(4):
                bi, bb = R_ps.next()
                P.pe_group([I('transpose', ps[:, bi, kc * 128:(kc + 1) * 128], pstg[:, tt, kc * 128:(kc + 1) * 128], ident_f[:]) for kc in range(2)],
                           reads=[B_pstg, B_const], writes=[bb])
                P.op('act', I('activation', out=pT[:, :, tt * 128:(tt + 1) * 128], in_=ps[:, bi, 0:256].rearrange("p (a b) -> p a b", a=2), func=AF.Copy),
                     reads=[bb], writes=[B_pT])
            for u in range(8):
                wv, wb = load_w(w_gate, 0, 16, u * 256, 256)
                for m in range(2):
                    d = 2 * u + m
                    bi, bb = R_ps.next()
                    be, beb = R_ps.next()
                    fns = []
                    for dk in range(16):
                        fns.append(I('matmul', ps[:, bi, :], wv[:, dk, m * 128:(m + 1) * 128], xbf[:, dk, :], start=(dk == 0), stop=(dk == 15)))
                    for kc in range(2):
                        fns.append(I('matmul', ps[:, be, :], wple_b[:, kc, d * 128:(d + 1) * 128], pT[:, kc, :], start=(kc == 0), stop=(kc == 1)))
                    P.pe_group(fns, reads=[wb, B_wple, B_pT] + B_xbf, writes=[bb, beb])
                    ti, tb = R_tf.next()
                    tv = tmpf[ti][:, 0:512]
                    P.op('act', I('activation', out=tv, in_=ps[:, bi, :], func=AF.Sigmoid, bias=pcol('bgate', d), scale=1.0),
                         reads=[bb, B_const], writes=[tb])
                    P.op('dve', I('tensor_tensor', tv, tv, ps[:, be, :], ALU.mult), reads=[beb, tb], writes=[tb])
                    P.op('dve', I('tensor_tensor', xres[:, d, :], xres[:, d, :], tv, ALU.add), reads=[tb, B_xres[d]], writes=[B_xres[d]])
                    ln_fm_stats(xres[:, d, :], B_xres[d], d, 16, onesD)
            if g == 0 and dbg_dump('G', xres[:].rearrange("p a b -> p (a b)"), F32, B_xres):
                return finish(nc, P, es, sems)
            ln_fm_finish()
            for d in range(16):
                tap, tbuf = ln_fm_center(xres[:, d, :], B_xres[d])
                P.op('act', I('activation', out=xres[:, d, :], in_=tap, func=AF.Identity, bias=pcol('ln3b', d), scale=pcol('ln3g', d)),
                     reads=[tbuf, B_const], writes=[B_xres[d]])
            if g == 0 and dbg_dump('LN3', xres[:].rearrange("p a b -> p (a b)"), F32, B_xres):
                return finish(nc, P, es, sems)
            for tt in range(4):
                si, sbuf_ = R_stg.next()
                for q4 in range(4):
                    bi, bb = R_ps.next()
                    P.pe_group([I('transpose', ps[:, bi, i * 128:(i + 1) * 128], xres[:, q4 * 4 + i, tt * 128:(tt + 1) * 128], ident_f[:]) for i in range(4)],
                               reads=[B_const] + B_xres[q4 * 4:q4 * 4 + 4], writes=[bb])
                    if q4 % 2 == 0:
                        P.op('act', I('activation', out=stg[si][:, q4 * 512:(q4 + 1) * 512], in_=ps[:, bi, :], func=AF.Copy), reads=[bb], writes=[sbuf_])
                    else:
                        P.op('dve', I('tensor_copy', stg[si][:, q4 * 512:(q4 + 1) * 512], ps[:, bi, :]), reads=[bb], writes=[sbuf_])
                P.dma('sp', I('dma_start', out=yown[t0 + tt], in_=stg[si][:]), 'D_stg%d' % si, reads=[sbuf_])

        return finish(nc, P, es, sems)


def finish(nc, P, es, sems):
    deps = {n: v for n, v in P.tick.items() if v > 0}
    P._wait('sp', deps)
    for name in sorted(P.semnames):
        sems[name] = es.enter_context(nc.semaphore(name))
    block = es.enter_context(nc.Block())

    def runner(items):
        def run(e):
            for it in items:
                if it[0] == 'w':
                    e.wait_ge(sems[it[1]], it[2])
                else:
                    nm, a, kw = it[1]
                    ins = getattr(e, nm)(*a, **kw)
                    if it[2] is not None:
                        ins.then_inc(sems[it[2]], it[3])
        return run
    block.sync(runner(P.q['sp']))
    block.scalar(runner(P.q['act']))
    block.vector(runner(P.q['dve']))
    block.gpsimd(runner(P.q['pool']))
    block.tensor(runner(P.q['pe']))
    return nc


def _src_tiles(h):
    o = [('p', 16 * h + j) for j in range(16)] + [('p', 16 * (1 - h) + j) for j in range(16)]
    o += [('s', 8 * h + j) for j in range(8)] + [('s', 8 * (1 - h) + j) for j in range(8)]
    return o


_CONST_CACHE = {}


def _dft_tables(h):
    if h in _CONST_CACHE:
        return _CONST_CACHE[h]
    order = _src_tiles(h)
    out = []
    for (S, nown_t, src_list, own_base) in ((4096, 16, [t for (q, t) in order if q == 'p'], 16 * h),
                                            (2048, 8, [t for (q, t) in order if q == 's'], 8 * h)):
        k = np.arange(S, dtype=np.float64)
        ctab = (np.cos(2 * np.pi * k / S) / np.sqrt(S)).astype(np.float32)
        stab = (np.sin(2 * np.pi * k / S) / np.sqrt(S)).astype(np.float32)
        nsrc = len(src_list)
        pos_src = (np.array(src_list, dtype=np.int64)[None, :] * 128 + np.arange(128, dtype=np.int64)[:, None])
        cm = np.empty((nown_t, 128, 2, nsrc, 128), dtype=np.float32)
        for t in range(nown_t):
            pos_own = (own_base + t) * 128 + np.arange(128, dtype=np.int64)
            idx = (pos_src[:, :, None] * pos_own[None, None, :]) % S
            cm[t, :, 0] = ctab[idx]
            cm[t, :, 1] = stab[idx]
        out.append(cm.reshape(nown_t, 128, 2 * nsrc * 128))
    _CONST_CACHE[h] = out
    return out


def _chan_table():
    d = np.arange(256, dtype=np.int64)
    idx = (d[:, None] * d[None, :]) % 256
    k = np.arange(256, dtype=np.float64)
    c = (np.cos(2 * np.pi * k / 256) / 16.0).astype(np.float32)[idx]
    s = (-np.sin(2 * np.pi * k / 256) / 16.0).astype(np.float32)[idx]
    t = np.stack([c, s], axis=1)
    t = t.reshape(2, 128, 2, 256).transpose(1, 0, 2, 3)
    return np.ascontiguousarray(t.reshape(128, 2 * 2 * 256))


def _cols(v, n):
    return np.ascontiguousarray(np.asarray(v, dtype=np.float32).reshape(n, 128).T)


def _prep(inputs):
    xp = np.asarray(inputs['x_prompt'], dtype=np.float32)
    xs = np.asarray(inputs['x_sample'], dtype=np.float32)
    pp_ = np.asarray(inputs['p_prompt'], dtype=np.float32)[0]
    ps_ = np.asarray(inputs['p_sample'], dtype=np.float32)[0]
    g = lambda n: np.asarray(inputs[n], dtype=np.float32)
    wdw = g('w_dw')[0]
    wdw_cols = np.ascontiguousarray(wdw.reshape(31, 8, 128).transpose(2, 1, 0).reshape(128, 248))
    pp = np.concatenate([
        _cols(g('emb_ln_g'), 16), _cols(g('emb_ln_b'), 16), _cols(g('conv_ln_g')[0], 8), _cols(g('conv_ln_b')[0], 8),
        _cols(g('b_dw')[0], 8), wdw_cols, _cols(g('ln1_g')[0], 16), _cols(g('ln1_b')[0], 16), _cols(g('b_ff1')[0], 64),
        _cols(g('b_ff2')[0], 16), _cols(g('ln2_g')[0], 16), _cols(g('ln2_b')[0], 16), _cols(g('b_gate')[0], 16),
        _cols(g('ln3_g')[0], 16), _cols(g('ln3_b')[0], 16)], axis=1)
    assert pp.shape == (128, NP_IN)
    shared = dict(w_in=np.ascontiguousarray(g('w_in')[0]), w_out=np.ascontiguousarray(g('w_out')[0]),
                  w_ff1=np.ascontiguousarray(g('w_ff1')[0]), w_ff2=np.ascontiguousarray(g('w_ff2')[0]),
                  w_gate=np.ascontiguousarray(g('w_gate')[0]), w_ple=np.ascontiguousarray(g('w_ple')[0]),
                  pp=np.ascontiguousarray(pp), cdt=_chan_table(), ident=np.eye(128, dtype=np.float32))
    in_maps = []
    for c in range(8):
        b, h = c // 2, c % 2
        order = _src_tiles(h)
        seqs = {'p': xp[b], 's': xs[b]}
        xsrc = np.stack([seqs[q][t * 128:(t + 1) * 128] for (q, t) in order], axis=0)
        xhalo = np.zeros((NG, 32, 2048), dtype=np.float32)
        hmask = np.zeros((NG, 32), dtype=np.float32)
        pown = np.empty((NG, 128, 4, 256), dtype=np.float32)
        for gi in range(NG):
            if gi < 4:
                seq, S, g0, pseq = xp[b], 4096, 2048 * h + 512 * gi, pp_[b]
            else:
                seq, S, g0, pseq = xs[b], 2048, 1024 * h + 512 * (gi - 4), ps_[b]
            for r in range(30):
                pos = g0 - 15 + r if r < 15 else g0 + 512 + (r - 15)
                if 0 <= pos < S:
                    xhalo[gi, r] = seq[pos]
                    hmask[gi, r] = 1.0
            pown[gi] = pseq[g0:g0 + 512].reshape(4, 128, 256).transpose(1, 0, 2)
        cmp_, cms_ = _dft_tables(h)
        m = dict(shared)
        m.update(xsrc=np.ascontiguousarray(xsrc), xhalo=xhalo,
                 hmask=np.ascontiguousarray(np.broadcast_to(hmask.reshape(1, NG * 32), (128, NG * 32))),
                 pown=pown, cmat_p=cmp_, cmat_s=cms_)
        in_maps.append(m)
    return in_maps


_NC_CACHE = {}


def kernel(**inputs):
    in_maps = _prep(inputs)
    if 'nc' not in _NC_CACHE:
        _NC_CACHE['nc'] = build_program()
    nc = _NC_CACHE['nc']
    res = run_bass_kernel_spmd(nc, in_maps, core_ids=list(range(8)))
    y_prompt = np.empty((4, 4096, 2048), dtype=np.float32)
    y_sample = np.empty((4, 2048, 2048), dtype=np.float32)
    for c in range(8):
        b, h = c // 2, c % 2
        y = np.asarray(res.results[c]["yown"], dtype=np.float32).reshape(NOWN * 128, 2048)
        y_prompt[b, 2048 * h:2048 * h + 2048] = y[0:2048]
        y_sample[b, 1024 * h:1024 * h + 1024] = y[2048:3072]
    return (y_prompt, y_sample)
```
